# Optimizing a Trainium2 kernel written in Bass

```python
import jax, jax.numpy as jnp
from jax import lax
import numpy as np

D_MODEL = 2048
BATCH = 32
SEQ = 256
DEPTH = 4
DEC_BATCH = 4
DEC_SEQ = 2048
PAST_LEN = 256

GRID_W = 64
N_EVEN = (DEPTH + 1) // 2
N_ODD = DEPTH // 2
H_A = 8
DH_A = 128
NA = H_A * DH_A
WIN_R = 8
WIN_C = 16
NB = 1024
CONV_W = 3
IN_AB = 3 * NA + 3 * NB
H_C = 16
Q_LORA = 512
KV_LORA = 256
NOPE = 128
ROPE = 64
V_DIM = 128
QK_DIM = NOPE + ROPE
ROPE_BASE = 10000.0
D_FF = -(-8 * D_MODEL // (3 * 256)) * 256
Q_BLOCK = 128
EPS = 1e-6

kernel_name = "hybrid_nat_conv_mla_diffusion_step"


def rms_norm(x, g):
    xf = x.astype(jnp.float32)
    y = xf * lax.rsqrt(jnp.mean(xf * xf, axis=-1, keepdims=True) + EPS)
    return (y * g.astype(jnp.float32)).astype(x.dtype)


def adaln(cond, w, b):
    m = jax.nn.silu(cond) @ w + b
    return jnp.split(m[..., None, :], 6, axis=-1)


def modulate(x, g, shift, scale):
    return rms_norm(x, g) * (1.0 + scale) + shift


def _heads(t, n_heads):
    b, l, _ = t.shape
    return t.reshape(b, l, n_heads, -1).transpose(0, 2, 1, 3)


def _merge(t):
    b, n, l, d = t.shape
    return t.transpose(0, 2, 1, 3).reshape(b, l, n * d)


def attend(q, k, v, scale):
    b, h, lq, dq = q.shape
    nb = lq // Q_BLOCK
    qb = q.reshape(b, h, nb, Q_BLOCK, dq).transpose(2, 0, 1, 3, 4)

    def one(qi):
        s = jnp.einsum('bhqd,bhkd->bhqk', qi, k, preferred_element_type=jnp.float32) * scale
        p = jax.nn.softmax(s, axis=-1).astype(v.dtype)
        return jnp.einsum('bhqk,bhkd->bhqd', p, v)

    o = lax.map(one, qb)
    return o.transpose(1, 2, 0, 3, 4).reshape(b, h, lq, -1)


def axial_rope(x, n_tok):
    t = jnp.arange(n_tok)
    half = x.shape[-1] // 2
    freqs = ROPE_BASE ** (-jnp.arange(half // 2, dtype=jnp.float32) / (half // 2))

    def rot(xa, pos):
        ang = pos.astype(jnp.float32)[:, None] * freqs
        cos, sin = jnp.cos(ang), jnp.sin(ang)
        x1, x2 = jnp.split(xa.astype(jnp.float32), 2, axis=-1)
        return jnp.concatenate([x1 * cos - x2 * sin, x1 * sin + x2 * cos], axis=-1)

    xr, xc = jnp.split(x, 2, axis=-1)
    return jnp.concatenate([rot(xr, t // GRID_W), rot(xc, t % GRID_W)], axis=-1).astype(x.dtype)


def short_conv(u, w):
    return lax.conv_general_dilated(u, w[:, None, :], window_strides=(1,), padding=((1, 1),),
                                    dimension_numbers=('NWC', 'WIO', 'NWC'),
                                    feature_group_count=u.shape[-1])


def ab_split(h, w_in, g_qn, g_kn, conv_w):
    proj = h @ w_in
    q, k, v, gb, gc, u = jnp.split(proj, [NA, 2 * NA, 3 * NA, 3 * NA + NB, 3 * NA + 2 * NB], axis=-1)
    q = rms_norm(_heads(q, H_A), g_qn)
    k = rms_norm(_heads(k, H_A), g_kn)
    v = _heads(v, H_A)
    y_b = gb * short_conv(gc * u, conv_w)
    return q, k, v, y_b


def neighbourhood_attention(q, k, v, ctx_k, ctx_v, rpb):
    b, h, n, dh = q.shape
    rows = n // GRID_W
    kr = min(WIN_R, rows)
    r = jnp.arange(rows)
    rs = jnp.clip(r - kr // 2, 0, rows - kr)
    key_rows = rs[:, None] + jnp.arange(kr)
    j = jnp.arange(GRID_W)
    cs = jnp.clip(j - WIN_C // 2, 0, GRID_W - WIN_C)
    col_in = (j[None, :] >= cs[:, None]) & (j[None, :] < cs[:, None] + WIN_C)
    qg = q.reshape(b, h, rows, GRID_W, dh)
    kg = k.reshape(b, h, rows, GRID_W, dh)[:, :, key_rows]
    vg = v.reshape(b, h, rows, GRID_W, dh)[:, :, key_rows]
    scale = dh ** -0.5
    s_loc = jnp.einsum('bhrqd,bhrikd->bhrqik', qg, kg, preferred_element_type=jnp.float32) * scale
    ro = key_rows - r[:, None] + (WIN_R - 1)
    co = jnp.clip(j[None, :] - j[:, None] + (WIN_C - 1), 0, 2 * WIN_C - 2)
    bias = rpb[:, ro[:, None, :, None], co[None, :, None, :]].astype(jnp.float32)
    s_loc = jnp.where(col_in[None, None, None, :, None, :], s_loc + bias[None], -jnp.inf)
    s_ctx = jnp.einsum('bhrqd,bhcd->bhrqc', qg, ctx_k, preferred_element_type=jnp.float32) * scale
    lw = kr * GRID_W
    s = jnp.concatenate([s_loc.reshape(b, h, rows, GRID_W, lw), s_ctx], axis=-1)
    p = jax.nn.softmax(s, axis=-1).astype(v.dtype)
    o = (jnp.einsum('bhrqik,bhrikd->bhrqd', p[..., :lw].reshape(b, h, rows, GRID_W, kr, GRID_W), vg)
         + jnp.einsum('bhrqc,bhcd->bhrqd', p[..., lw:], ctx_v))
    return o.reshape(b, h, n, dh)


def mla_latents(h, w_down, g_cq, g_ckv):
    cq, ckv, kr = jnp.split(h @ w_down, [Q_LORA, Q_LORA + KV_LORA], axis=-1)
    return rms_norm(cq, g_cq), rms_norm(ckv, g_ckv), kr


def mla_q(cq, w_uq, g_qn):
    return rms_norm(_heads(cq @ w_uq, H_C), g_qn)


def mla_kv(ckv, kr, w_ukv, g_kn):
    kv = _heads(ckv @ w_ukv, H_C)
    k_nope, v = jnp.split(kv, [NOPE], axis=-1)
    k_r = jnp.broadcast_to(kr[:, None], k_nope.shape[:-1] + (ROPE,))
    k = rms_norm(jnp.concatenate([k_nope, k_r], axis=-1), g_kn)
    return k, v


def rope_part(t):
    return jnp.concatenate([t[..., :NOPE], axial_rope(t[..., NOPE:], t.shape[-2])], axis=-1)


def swiglu(h, w_in, w_out):
    gate, up = jnp.split(h @ w_in, 2, axis=-1)
    return (jax.nn.silu(gate) * up) @ w_out


def setup_inputs(seed: int = 0) -> dict:
    key = jax.random.key(seed)
    ks = iter(jax.random.split(key, 32))

    def nrm(shape, scale):
        return jax.random.normal(next(ks), shape, jnp.float32) * scale

    def gain(shape):
        return 1.0 + nrm(shape, 0.02)

    D = D_MODEL
    return {
        'x_prompt': nrm((BATCH, SEQ, D), 1.0),
        'x_sample': nrm((DEC_BATCH, DEC_SEQ, D), 1.0),
        'cache_nat_k': nrm((DEC_BATCH, N_EVEN, H_A, PAST_LEN, DH_A), 1.0),
        'cache_nat_v': nrm((DEC_BATCH, N_EVEN, H_A, PAST_LEN, DH_A), 1.0),
        'cache_mla_ckv': nrm((DEC_BATCH, N_ODD, PAST_LEN, KV_LORA), 1.0),
        'cache_mla_krope': nrm((DEC_BATCH, N_ODD, PAST_LEN, ROPE), 1.0),
        'c': nrm((DEC_BATCH, D), 1.0),
        'c_ctx': nrm((D,), 1.0),
        'norm1_g': gain((DEPTH, D)),
        'norm2_g': gain((DEPTH, D)),
        'w_ada': nrm((DEPTH, D, 6 * D), 0.5 * D ** -0.5),
        'b_ada': nrm((DEPTH, 6 * D), 0.01),
        'w_in_ab': nrm((N_EVEN, D, IN_AB), D ** -0.5),
        'g_qn_a': gain((N_EVEN, DH_A)),
        'g_kn_a': gain((N_EVEN, DH_A)),
        'rpb_a': nrm((N_EVEN, H_A, 2 * WIN_R - 1, 2 * WIN_C - 1), 0.1),
        'conv_b_w': nrm((N_EVEN, CONV_W, NB), CONV_W ** -0.5),
        'w_out_ab': nrm((N_EVEN, NA + NB, D), (NA + NB) ** -0.5),
        'w_down_c': nrm((N_ODD, D, Q_LORA + KV_LORA + ROPE), D ** -0.5),
        'g_cq': gain((N_ODD, Q_LORA)),
        'g_ckv': gain((N_ODD, KV_LORA)),
        'w_uq_c': nrm((N_ODD, Q_LORA, H_C * QK_DIM), Q_LORA ** -0.5),
        'w_ukv_c': nrm((N_ODD, KV_LORA, H_C * (NOPE + V_DIM)), KV_LORA ** -0.5),
        'g_qn_c': gain((N_ODD, QK_DIM)),
        'g_kn_c': gain((N_ODD, QK_DIM)),
        'w_o_c': nrm((N_ODD, H_C * V_DIM, D), (H_C * V_DIM) ** -0.5),
        'w_ffn_in': nrm((DEPTH, D, 2 * D_FF), D ** -0.5),
        'w_ffn_out': nrm((DEPTH, D_FF, D), D_FF ** -0.5),
    }


def reference(x_prompt, x_sample, cache_nat_k, cache_nat_v, cache_mla_ckv, cache_mla_krope, c, c_ctx,
              norm1_g, norm2_g, w_ada, b_ada, w_in_ab, g_qn_a, g_kn_a, rpb_a, conv_b_w, w_out_ab,
              w_down_c, g_cq, g_ckv, w_uq_c, w_ukv_c, g_qn_c, g_kn_c, w_o_c, w_ffn_in, w_ffn_out):
    xc = x_prompt
    xs = x_sample
    nat_k, nat_v, mla_ckv, mla_kr = [], [], [], []
    for l in range(DEPTH):
        i = l // 2
        mc = adaln(c_ctx, w_ada[l], b_ada[l])
        ms = adaln(c, w_ada[l], b_ada[l])
        hc = modulate(xc, norm1_g[l], mc[0], mc[1])
        hs = modulate(xs, norm1_g[l], ms[0], ms[1])
        if l % 2 == 0:
            q, k, v, yb = ab_split(hc, w_in_ab[i], g_qn_a[i], g_kn_a[i], conv_b_w[i])
            ya = attend(q, k, v, DH_A ** -0.5)
            yc = jnp.concatenate([_merge(ya), yb], axis=-1) @ w_out_ab[i]
            nat_k.append(k)
            nat_v.append(v)
            q, k, v, yb = ab_split(hs, w_in_ab[i], g_qn_a[i], g_kn_a[i], conv_b_w[i])
            ya = neighbourhood_attention(q, k, v, cache_nat_k[:, i], cache_nat_v[:, i], rpb_a[i])
            ys = jnp.concatenate([_merge(ya), yb], axis=-1) @ w_out_ab[i]
        else:
            scale = QK_DIM ** -0.5
            cq, ckv, kr = mla_latents(hc, w_down_c[i], g_cq[i], g_ckv[i])
            q = mla_q(cq, w_uq_c[i], g_qn_c[i])
            k, v = mla_kv(ckv, kr, w_ukv_c[i], g_kn_c[i])
            yc = _merge(attend(q, k, v, scale)) @ w_o_c[i]
            mla_ckv.append(ckv)
            mla_kr.append(kr)
            cq, ckv, kr = mla_latents(hs, w_down_c[i], g_cq[i], g_ckv[i])
            q = rope_part(mla_q(cq, w_uq_c[i], g_qn_c[i]))
            k, v = mla_kv(ckv, kr, w_ukv_c[i], g_kn_c[i])
            k = rope_part(k)
            kx, vx = mla_kv(cache_mla_ckv[:, i], cache_mla_krope[:, i], w_ukv_c[i], g_kn_c[i])
            o = attend(q, jnp.concatenate([kx, k], axis=2), jnp.concatenate([vx, v], axis=2), scale)
            ys = _merge(o) @ w_o_c[i]
        xc = xc + mc[2] * yc
        xs = xs + ms[2] * ys
        xc = xc + mc[5] * swiglu(modulate(xc, norm2_g[l], mc[3], mc[4]), w_ffn_in[l], w_ffn_out[l])
        xs = xs + ms[5] * swiglu(modulate(xs, norm2_g[l], ms[3], ms[4]), w_ffn_in[l], w_ffn_out[l])
    return (xc, xs, jnp.stack(nat_k, axis=1), jnp.stack(nat_v, axis=1),
            jnp.stack(mla_ckv, axis=1), jnp.stack(mla_kr, axis=1))
```

```python
import contextlib
import numpy as np
import concourse.bass as bass
import concourse.mybir as mybir
from concourse.bass_utils import run_bass_kernel_spmd

F32 = mybir.dt.float32
BF16 = mybir.dt.bfloat16
AF = mybir.ActivationFunctionType
ALU = mybir.AluOpType

D = 2048
KC = 16
DEPTH = 4
TP = 1024
TS = 2048
TALL = TP + TS
DFF = 5632
EPS = 1e-6
NDS = 16


class Op:
    __slots__ = ("eng", "fn", "waits", "idx", "dma", "dn")


class Prog:
    ENGS = ("pe", "act", "dve", "pool", "sp")

    def __init__(self):
        self.ops = {e: [] for e in self.ENGS}
        self.lastw = {}
        self.readers = {}
        self.seen = {e: {} for e in self.ENGS}
        self.seen_d = {e: {} for e in self.ENGS}
        self.ndma = {e: 0 for e in self.ENGS}
        self.lastc = {e: -1 for e in self.ENGS}

    def _filter(self, eng, deps):
        waits = []
        for t in deps:
            if t[0] == "e":
                if t[1] == eng and eng == "pe":
                    continue
                if self.seen[eng].get(t[1], -1) >= t[2]:
                    continue
                self.seen[eng][t[1]] = t[2]
                waits.append(t)
            else:
                s = self.seen_d[eng].setdefault(t[1], set())
                if t[2] in s:
                    continue
                s.add(t[2])
                waits.append(t)
        return waits

    def add(self, eng, fn, reads=(), writes=(), dma=False):
        deps = set()
        for k in reads:
            t = self.lastw.get(k)
            if t is not None:
                deps.add(t)
        for k in writes:
            t = self.lastw.get(k)
            if t is not None:
                deps.add(t)
            for t in self.readers.get(k, ()):
                deps.add(t)
        op = Op()
        op.eng = eng
        op.idx = len(self.ops[eng])
        op.fn = fn
        op.dma = dma
        op.dn = -1
        if dma:
            n = self.ndma[eng]
            self.ndma[eng] += 1
            op.dn = n
            if n >= NDS:
                deps.add(("d", eng, n - NDS))
            tok = ("d", eng, n)
        else:
            tok = ("e", eng, op.idx)
            if fn is not None:
                self.lastc[eng] = op.idx
        op.waits = self._filter(eng, deps)
        self.ops[eng].append(op)
        if fn is not None:
            for k in reads:
                self.readers.setdefault(k, []).append(tok)
            for k in writes:
                self.lastw[k] = tok
                self.readers[k] = []
        return tok

    def barrier(self):
        toks = []
        for e in self.ENGS:
            if self.lastc[e] >= 0:
                toks.append(("e", e, self.lastc[e]))
            for n in range(max(0, self.ndma[e] - NDS), self.ndma[e]):
                toks.append(("d", e, n))
        for e in self.ENGS:
            op = Op()
            op.eng = e
            op.idx = len(self.ops[e])
            op.fn = None
            op.dma = False
            op.dn = -1
            op.waits = self._filter(e, toks)
            self.ops[e].append(op)
        self.lastw.clear()
        self.readers.clear()

    def emit(self, nc, block, esem, dsem):
        need = set()
        for e in self.ENGS:
            for op in self.ops[e]:
                for t in op.waits:
                    if t[0] == "e":
                        need.add((t[1], t[2]))
        sigval = {}
        for e in self.ENGS:
            c = 0
            for op in self.ops[e]:
                if (not op.dma) and (e, op.idx) in need:
                    c += 1
                    sigval[(e, op.idx)] = c
        self.sigmax = {e: max([v for (ee, _), v in sigval.items() if ee == e] + [0]) for e in self.ENGS}

        def run(en):
            def body(eng):
                for op in self.ops[en]:
                    for t in op.waits:
                        if t[0] == "e":
                            eng.wait_ge(esem[t[1]], sigval[(t[1], t[2])])
                        else:
                            eng.wait_ge(dsem[t[1]][t[2] % NDS], 16 * (t[2] // NDS + 1))
                    if op.fn is None:
                        continue
                    ins = op.fn(eng)
                    if op.dma:
                        ins.then_inc(dsem[en][op.dn % NDS], 16)
                    elif (en, op.idx) in sigval:
                        ins.then_inc(esem[en], 1)
            return body

        block.tensor(run("pe"))
        block.scalar(run("act"))
        block.vector(run("dve"))
        block.gpsimd(run("pool"))
        block.sync(run("sp"))


GRID_W, WIN_R, WIN_C = 64, 8, 16
ROWS = TS // GRID_W


def nat_blocks():
    out = []
    tiles = {}
    for j in range(ROWS // 2):
        rs0 = min(max(2 * j - WIN_R // 2, 0), ROWS - WIN_R)
        rs1 = min(max(2 * j + 1 - WIN_R // 2, 0), ROWS - WIN_R)
        c0 = rs0 // 2
        c1 = (rs1 + WIN_R - 1) // 2
        lst = []
        for kc in range(c0, c1 + 1):
            generic = (2 <= j <= ROWS // 2 - 3)
            key = ("g", kc - j) if generic else ("e", j, kc)
            if key not in tiles:
                tiles[key] = (len(tiles), j, kc)
            lst.append((kc, tiles[key][0]))
        out.append(lst)
    tl = sorted(tiles.values())
    return out, [(j, kc) for (_, j, kc) in tl]


NAT_BLOCKS, NAT_TILES = nat_blocks()
NBT = len(NAT_TILES)


def nat_bias_index():
    ro = np.zeros((NBT, 128, 128), np.int64)
    co = np.zeros((NBT, 128, 128), np.int64)
    ok = np.zeros((NBT, 128, 128), bool)
    kk = np.arange(128)[:, None]
    qq = np.arange(128)[None, :]
    for t, (j, kc) in enumerate(NAT_TILES):
        krow = 2 * kc + kk // 64
        kcol = kk % 64
        qrow = 2 * j + qq // 64
        qcol = qq % 64
        rs = np.clip(qrow - WIN_R // 2, 0, ROWS - WIN_R)
        cs = np.clip(qcol - WIN_C // 2, 0, GRID_W - WIN_C)
        valid = (krow >= rs) & (krow < rs + WIN_R) & (kcol >= cs) & (kcol < cs + WIN_C)
        ro[t] = np.clip(krow - qrow + (WIN_R - 1), 0, 2 * WIN_R - 2)
        co[t] = np.clip(kcol - qcol + (WIN_C - 1), 0, 2 * WIN_C - 2)
        ok[t] = valid
    return ro, co, ok


def build(depth=DEPTH, stop_after=None):
    nc = bass.Bass("TRN2", target_bir_lowering=False)
    P = Prog()
    NE = (depth + 1) // 2
    NO = max(depth // 2, 1)

    def din(name, shape):
        return nc.dram_tensor(name, list(shape), F32, kind="ExternalInput").ap()

    def dout(name, shape):
        return nc.dram_tensor(name, list(shape), F32, kind="ExternalOutput").ap()

    xin = din("xT", [D, TALL])
    cond_d = din("cond", [128, KC, 2])
    n1g_d = din("n1g", [128, DEPTH, KC])
    n2g_d = din("n2g", [128, DEPTH, KC])
    w_ada = din("w_ada", [DEPTH, D, 6 * D])
    b_ada = din("b_ada", [128, DEPTH, 96])
    w_in_ab = din("w_in_ab", [2, D, 6144])
    gqa_d = din("gqa", [128, 2])
    gka_d = din("gka", [128, 2])
    convw_d = din("convw", [128, 2, 8, 3])
    biasT_d = din("biasT", [2, 8, NBT, 128, 128])
    w_out_ab = din("w_out_ab", [2, D, D])
    w_down = din("w_down", [2, D, 832])
    gcq_d = din("gcq", [128, 2, 4])
    gckv_d = din("gckv", [128, 2, 2])
    w_uq = din("w_uq", [2, 512, 3072])
    w_ukv = din("w_ukv", [2, 256, 4096])
    gqc_d = din("gqc", [128, 2, 2])
    gkc_d = din("gkc", [128, 2, 2])
    w_o = din("w_o", [2, D, D])
    w_ffn_in = din("w_ffn_in", [DEPTH, D, 2 * DFF])
    w_ffn_out = din("w_ffn_out", [DEPTH, DFF, D])
    cnkT_d = din("cnkT", [2, 8, 128, 256])
    cnv_d = din("cnv", [2, 8, 256, 128])
    cckvT_d = din("cckvT", [2, 256, 256])
    ckrT_d = din("ckrT", [2, 64, 256])
    cos_d = din("ropecos", [64, TS])
    sin_d = din("ropesin", [64, TS])
    pt_d = din("ropept", [64, 64])
    ident_d = din("ident", [128, 128])

    yout = dout("yT", [D, TALL])
    natk_o = dout("natk", [2, 8, 128, TP])
    natv_o = dout("natv", [2, TP, 1024])
    ckv_o = dout("ckvo", [2, 256, TP])
    kr_o = dout("kro", [2, 64, TP])
    xd = nc.dram_tensor("xscratch", [D, TALL], F32).ap()

    def fm(ap):
        return ap.rearrange("(c p) t -> p c t", p=128)

    st = contextlib.ExitStack()
    with st:
        def sb(name, shape, dt):
            return st.enter_context(nc.sbuf_tensor(name, list(shape), dt))

        HT = sb("HT", [128, KC, TS], BF16)
        YX = sb("YX", [128, 16384], F32)
        WR = sb("WR", [128, 16384], BF16)
        AW = sb("AW", [128, 10240], F32)
        ones = sb("ones", [128, 128], BF16)
        ident = sb("identb", [128, 128], BF16)
        ptm = sb("ptm", [64, 64], BF16)
        cond_s = sb("cond_s", [128, KC, 2], F32)
        sct = sb("sct", [128, KC, 2], BF16)
        n1g = sb("n1g_s", [128, DEPTH, KC], F32)
        n2g = sb("n2g_s", [128, DEPTH, KC], F32)
        bada = sb("bada", [128, DEPTH, 96], F32)
        modv = sb("modv", [128, 96, 2], F32)
        gs1 = sb("gs1", [128, KC, 2], F32)
        gs2 = sb("gs2", [128, KC, 2], F32)
        gqa = sb("gqa_s", [128, 2], F32)
        gka = sb("gka_s", [128, 2], F32)
        convw = sb("convw_s", [128, 2, 8, 3], F32)
        gcq = sb("gcq_s", [128, 2, 4], F32)
        gckv = sb("gckv_s", [128, 2, 2], F32)
        gqc = sb("gqc_s", [128, 2, 2], F32)
        gkc = sb("gkc_s", [128, 2, 2], F32)
        ps = [st.enter_context(nc.psum_tensor("ps%d" % i, [128, 512], F32)) for i in range(8)]
        esem = {e: st.enter_context(nc.semaphore("es_" + e)) for e in Prog.ENGS}
        dsem = {e: [st.enter_context(nc.semaphore("ds_%s%d" % (e, i))) for i in range(NDS)]
                for e in ("pool", "sp")}

        psc = [0]
        held = set()

        def nps(hold=False):
            while True:
                i = psc[0] % 8
                psc[0] += 1
                if i not in held:
                    break
            if hold:
                held.add(i)
            return ps[i], ("ps", i)

        def release(key):
            held.discard(key[1])

        def carve(region, off_b, shape, dt):
            n = int(np.prod(shape[1:]))
            eb = 4 if dt == F32 else 2
            assert off_b % 4 == 0
            nf = (n * eb + 3) // 4
            v = region[0:128, off_b // 4: off_b // 4 + nf]
            if dt != F32:
                v = v.bitcast(dt)
            v = v[0:shape[0], 0:n]
            if len(shape) == 3:
                v = v.rearrange("p (a b) -> p a b", a=shape[1])
            elif len(shape) == 4:
                v = v.rearrange("p (a b c) -> p a b c", a=shape[1], b=shape[2])
            return v

        yT = YX[:, :].bitcast(BF16).rearrange("p (c t) -> p c t", c=KC)
        xres = YX[:, :].rearrange("p (c t) -> p c t", c=KC)
        wslot = [WR[:, 0:8192], WR[:, 8192:16384]]
        wcnt = [0]

        def nw():
            i = wcnt[0] % 2
            wcnt[0] += 1
            return wslot[i], ("w", i)

        def dma(q, out, in_, reads, writes):
            P.add(q, lambda e, o=out, i=in_: e.dma_start(out=o, in_=i), reads, writes, dma=True)

        def mm(items, reads, writes):
            def fn(pe, items=items):
                ins = None
                for (o, l, r, s, e) in items:
                    ins = pe.matmul(o, l, r, start=s, stop=e)
                return ins
            P.add("pe", fn, reads, writes)

        def act(out, in_, func, reads, writes, bias=None, scale=None):
            kw = {}
            if bias is not None:
                kw["bias"] = bias
            if scale is not None:
                kw["scale"] = scale
            P.add("act", lambda e, o=out, i=in_, f=func, kw=kw: e.activation(out=o, in_=i, func=f, **kw),
                  reads, writes)

        def tt(out, in0, in1, op, reads, writes, eng="dve"):
            P.add(eng, lambda e, o=out, a=in0, b=in1, op=op: e.tensor_tensor(out=o, in0=a, in1=b, op=op),
                  reads, writes)

        def stt(out, in0, scalar, in1, op0, op1, reads, writes):
            P.add("dve", lambda e, o=out, a=in0, s=scalar, b=in1, o0=op0, o1=op1:
                  e.scalar_tensor_tensor(out=o, in0=a, scalar=s, in1=b, op0=o0, op1=o1), reads, writes)

        def ts(out, in0, s1, s2, op0, op1, reads, writes):
            if s2 is None:
                P.add("dve", lambda e, o=out, a=in0, s=s1, o0=op0:
                      e.tensor_scalar(out=o, in0=a, scalar1=s, scalar2=None, op0=o0), reads, writes)
            else:
                P.add("dve", lambda e, o=out, a=in0, s=s1, s2=s2, o0=op0, o1=op1:
                      e.tensor_scalar(out=o, in0=a, scalar1=s, scalar2=s2, op0=o0, op1=o1), reads, writes)

        def recip(out, in_, reads, writes):
            P.add("dve", lambda e, o=out, i=in_: e.reciprocal(out=o, in_=i), reads, writes)

        def copy(eng, out, in_, reads, writes):
            if eng == "act":
                P.add("act", lambda e, o=out, i=in_: e.copy(out=o, in_=i), reads, writes)
            else:
                P.add(eng, lambda e, o=out, i=in_: e.tensor_copy(out=o, in_=i), reads, writes)

        def rstd_from(psn, npart, ncol, n_eps, sq_tmp, sq_key, rs_out, rs_key, pkey):
            act(sq_tmp[0:npart, 0:ncol], psn[0:npart, 0:ncol], AF.Sqrt, [pkey], [sq_key], bias=epsb[0:npart, n_eps:n_eps + 1])
            recip(rs_out[0:npart, 0:ncol], sq_tmp[0:npart, 0:ncol], [sq_key], [rs_key])

        epsb = sb("epsb", [128, 4], F32)
        eps256 = sb("eps256", [128, 1], F32)
        P.add("pool", lambda e: e.memset(ones[:, :], 1.0), [], ["ones"])
        P.add("pool", lambda e: e.memset(epsb[:, 0:1], D * EPS), [], ["epsb"])
        P.add("pool", lambda e: e.memset(epsb[:, 1:2], 128 * EPS), [], ["epsb"])
        P.add("pool", lambda e: e.memset(epsb[:, 2:3], 512 * EPS), [], ["epsb"])
        P.add("pool", lambda e: e.memset(epsb[:, 3:4], 192 * EPS), [], ["epsb"])
        P.add("pool", lambda e: e.memset(eps256[:, :], 256 * EPS), [], ["epsb"])
        dma("pool", ident[:, :], ident_d, [], ["ident"])
        dma("pool", ptm[:, :], pt_d, [], ["ptm"])
        dma("sp", cond_s[:, :, :], cond_d, [], ["cond"])
        dma("sp", n1g[:, :, :], n1g_d, [], ["n1g"])
        dma("sp", n2g[:, :, :], n2g_d, [], ["n2g"])
        dma("sp", bada[:, :, :], b_ada, [], ["bada"])
        dma("sp", gqa[:, :], gqa_d, [], ["gsm"])
        dma("sp", gka[:, :], gka_d, [], ["gsm"])
        dma("sp", convw[:, :, :, :], convw_d, [], ["gsm"])
        dma("sp", gcq[:, :, :], gcq_d, [], ["gsm"])
        dma("sp", gckv[:, :, :], gckv_d, [], ["gsm"])
        dma("sp", gqc[:, :, :], gqc_d, [], ["gsm"])
        dma("sp", gkc[:, :, :], gkc_d, [], ["gsm"])
        for c in range(KC):
            dma("sp", xd[c * 128:(c + 1) * 128, :], xin[c * 128:(c + 1) * 128, :], [], [("xd", c)])
        ts(n1g[:, :, :], n1g[:, :, :], float(np.sqrt(D)), None, ALU.mult, None, ["n1g"], ["n1g"])
        ts(n2g[:, :, :], n2g[:, :, :], float(np.sqrt(D)), None, ALU.mult, None, ["n2g"], ["n2g"])
        ts(gka[:, :], gka[:, :], float(np.sqrt(128.0)), None, ALU.mult, None, ["gsm"], ["gsm"])
        ts(gcq[:, :, :], gcq[:, :, :], float(np.sqrt(512.0)), None, ALU.mult, None, ["gsm"], ["gsm"])
        ts(gckv[:, :, :], gckv[:, :, :], float(np.sqrt(256.0)), None, ALU.mult, None, ["gsm"], ["gsm"])
        ts(gkc[:, :, :], gkc[:, :, :], float(np.sqrt(192.0)), None, ALU.mult, None, ["gsm"], ["gsm"])
        act(sct[:, :, :], cond_s[:, :, :], AF.Silu, ["cond"], ["sct"])
        P.barrier()

        xdv = fm(xd)

        def adaln(l):
            pa, pk = nps(hold=True)
            wv = w_ada[l].rearrange("(kc p) n -> p kc n", p=128)
            for nt in range(24):
                wsl, wk = nw()
                wt = wsl.rearrange("p (k n) -> p k n", k=KC)
                dma("pool", wt, wv[:, :, nt * 512:(nt + 1) * 512], [], [wk])
                items = []
                for j in range(4):
                    ch = nt * 4 + j
                    for kc in range(KC):
                        items.append((pa[:, ch * 2:ch * 2 + 2], wt[:, kc, j * 128:(j + 1) * 128],
                                      sct[:, kc, :], kc == 0, kc == KC - 1))
                mm(items, [wk, "sct"], [pk])
            release(pk)
            tt(modv[:, :, :], pa[:, 0:192].rearrange("p (a b) -> p a b", b=2),
               bada[:, l, :].unsqueeze(2).to_broadcast([128, 96, 2]), ALU.add, [pk, "bada"], ["modv"])
            stt(gs1[:, :, :], modv[:, 16:32, :], 1.0, n1g[:, l, :].unsqueeze(2).to_broadcast([128, KC, 2]),
                ALU.add, ALU.mult, ["modv", "n1g"], ["gs1"])
            stt(gs2[:, :, :], modv[:, 64:80, :], 1.0, n2g[:, l, :].unsqueeze(2).to_broadcast([128, KC, 2]),
                ALU.add, ALU.mult, ["modv", "n2g"], ["gs2"])
            P.barrier()

        def norm_block(xb, xk, sq, tmp, rs, rsq, nb, gsv, shift_lo, cd, hcol0, tag):
            act(sq[:, :, 0:nb], xb, AF.Square, [xk], [tag + "sq"])
            pn, pk = nps()
            mm([(pn[:, 0:nb], ones[:, :], sq[:, c, 0:nb], c == 0, c == KC - 1) for c in range(KC)],
               [tag + "sq", "ones"], [pk])
            act(rsq[:, 0:nb], pn[:, 0:nb], AF.Sqrt, [pk, "epsb"], [tag + "rsq"], bias=epsb[:, 0:1])
            recip(rs[:, 0:nb], rsq[:, 0:nb], [tag + "rsq"], [tag + "rs"])
            tt(tmp[:, :, 0:nb], xb, rs[:, 0:nb].unsqueeze(1).to_broadcast([128, KC, nb]), ALU.mult,
               [xk, tag + "rs"], [tag + "tmp"])
            for c in range(KC):
                act(HT[:, c, hcol0:hcol0 + nb], tmp[:, c, 0:nb], AF.Identity, [tag + "tmp", "gs", "modv"],
                    [("ht", c, hcol0 // 512)], bias=modv[:, shift_lo + c, cd:cd + 1], scale=gsv[:, c, cd:cd + 1])

        def norm_stream(t0, T, gsv, shift_lo, cd):
            NB = 256
            xb = [carve(YX, 0, [128, KC, NB], F32), carve(YX, 16384, [128, KC, NB], F32)]
            tmp = carve(YX, 32768, [128, KC, NB], F32)
            sq = carve(YX, 49152, [128, KC, NB], BF16)
            rs = carve(YX, 57344, [128, NB], F32)
            rsq = carve(YX, 58368, [128, NB], F32)
            for b in range(T // NB):
                xk = ("xb", b % 2)
                dma("sp", xb[b % 2], xdv[:, :, t0 + b * NB:t0 + (b + 1) * NB], [("xd", c) for c in range(KC)], [xk])
                norm_block(xb[b % 2], xk, sq, tmp, rs, rsq, NB, gsv, shift_lo, cd, b * NB, "ns")
            P.barrier()

        def out_proj(wsrc, t0, T, cd, gate_lo):
            wv = wsrc.rearrange("(kc p) n -> p kc n", p=128)
            xt = [carve(AW, 2048 * i, [128, 512], F32) for i in range(4)]
            cnt = 0
            for fg in range(4):
                wsl, wk = nw()
                wt = wsl.rearrange("p (k n) -> p k n", k=KC)
                dma("pool", wt, wv[:, :, fg * 512:(fg + 1) * 512], [], [wk])
                for tb in range(T // 512):
                    for j in range(4):
                        fc = fg * 4 + j
                        xk = ("xt", cnt % 4)
                        xtile = xt[cnt % 4]
                        cnt += 1
                        dma("sp", xtile, xdv[:, fc, t0 + tb * 512:t0 + (tb + 1) * 512], [("xd", fc)], [xk])
                        pp, pk = nps()
                        mm([(pp[:, :], wt[:, kc, j * 128:(j + 1) * 128], yT[:, kc, tb * 512:(tb + 1) * 512],
                             kc == 0, kc == KC - 1) for kc in range(KC)], [wk, ("yt", tb)], [pk])
                        stt(xtile, pp[:, :], modv[:, gate_lo + fc, cd:cd + 1], xtile, ALU.mult, ALU.add,
                            [pk, xk, "modv"], [xk])
                        dma("sp", xdv[:, fc, t0 + tb * 512:t0 + (tb + 1) * 512], xtile, [xk], [("xd", fc)])
            P.barrier()

        def attn_block(kparts, qparts, vfn, chunks, Nq, bias, out_ap, okey, rkeys, ex, exk, rec, reck):
            nch = len(chunks)
            per = 512 // Nq
            gi = 0
            while gi < nch:
                g = chunks[gi:gi + per]
                pS, pSk = nps()
                items = []
                for ci, ch in enumerate(g):
                    o = pS[:, ci * Nq:(ci + 1) * Nq]
                    np_ = len(kparts)
                    for pi in range(np_):
                        last = (pi == np_ - 1) and bias is None
                        items.append((o, kparts[pi](ch), qparts[pi], pi == 0, last))
                    if bias is not None:
                        items.append((o, ident[:, :], bias(gi + ci), False, True))
                mm(items, rkeys, [pSk])
                act(ex[:, gi * Nq:(gi + len(g)) * Nq], pS[:, 0:len(g) * Nq], AF.Exp, [pSk], [(exk, gi // per)])
                gi += per
            pO, pOk = nps()
            if Nq == 512:
                pD, pDk = nps()
                od, dd = pO[:, 0:512], pD[:, 0:512]
                wk_ = [pOk, pDk]
            else:
                od, dd = pO[:, 0:Nq], pO[:, Nq:2 * Nq]
                wk_ = [pOk]
            items = []
            for ci, ch in enumerate(chunks):
                items.append((od, vfn(ch), ex[:, ci * Nq:(ci + 1) * Nq], ci == 0, ci == nch - 1))
            for ci, ch in enumerate(chunks):
                items.append((dd, ones[:, :], ex[:, ci * Nq:(ci + 1) * Nq], ci == 0, ci == nch - 1))
            mm(items, rkeys + [(exk, i) for i in range((nch + per - 1) // per)] + ["ones"], wk_)
            recip(rec[:, 0:Nq], dd, wk_, [reck])
            tt(out_ap, od, rec[:, 0:Nq], ALU.mult, wk_ + [reck], [okey])

        def mixer_even(l, grp):
            i = l // 2
            if grp == "P":
                t0, T, cd, nseq, L = 0, TP, 0, 4, 256
            else:
                t0, T, cd, nseq, L = TP, TS, 1, 1, TS
            norm_stream(t0, T, gs1, 0, cd)
            wv = w_in_ab[i].rearrange("(kc p) n -> p kc n", p=128)
            NT = T // 512
            zt = carve(AW, 0, [128, nseq, L + 2], F32)
            gcs = [carve(AW, 8704 + 2048 * k, [128, 512], F32) for k in range(2)]
            acc = [carve(AW, 12800 + 2048 * k, [128, 512], F32) for k in range(2)]
            P.add("pool", lambda e: e.memset(zt, 0.0), [], ["zt"])

            def zview(tb, sh):
                if grp == "P":
                    return zt[:, 2 * tb:2 * tb + 2, sh:sh + 256]
                return zt[:, 0, tb * 512 + sh:tb * 512 + sh + 512]

            def pv(ap):
                if grp == "P":
                    return ap.rearrange("p (a b) -> p a b", a=2)
                return ap

            for ch in range(8):
                wsl, wk = nw()
                wt = wsl[:, 0:3 * KC * 128].rearrange("p (s k n) -> p s k n", s=3, k=KC)
                for s_ in range(3):
                    c0 = 3072 + s_ * 1024 + ch * 128
                    dma("pool", wt[:, s_, :, :], wv[:, :, c0:c0 + 128], [], [wk])
                for tb in range(NT):
                    hk = [("ht", c, tb) for c in range(KC)]
                    pa, pak = nps()
                    mm([(pa[:, :], wt[:, 1, kc, :], HT[:, kc, tb * 512:(tb + 1) * 512], kc == 0, kc == KC - 1)
                        for kc in range(KC)], [wk] + hk, [pak])
                    pb, pbk = nps()
                    mm([(pb[:, :], wt[:, 2, kc, :], HT[:, kc, tb * 512:(tb + 1) * 512], kc == 0, kc == KC - 1)
                        for kc in range(KC)], [wk] + hk, [pbk])
                    g = gcs[tb % 2]
                    copy("act", g, pa[:, :], [pak], [("gcs", tb % 2)])
                    tt(zview(tb, 1), pv(pb[:, :]), pv(g), ALU.mult, [pbk, ("gcs", tb % 2)], ["zt"])
                for tb in range(NT):
                    hk = [("ht", c, tb) for c in range(KC)]
                    pc, pck = nps()
                    mm([(pc[:, :], wt[:, 0, kc, :], HT[:, kc, tb * 512:(tb + 1) * 512], kc == 0, kc == KC - 1)
                        for kc in range(KC)], [wk] + hk, [pck])
                    a = acc[tb % 2]
                    ak = ("acc", tb % 2)
                    ts(pv(a), zview(tb, 0), convw[:, i, ch, 0:1], None, ALU.mult, None, ["zt", "gsm"], [ak])
                    stt(pv(a), zview(tb, 1), convw[:, i, ch, 1:2], pv(a), ALU.mult, ALU.add, ["zt", ak], [ak])
                    stt(pv(a), zview(tb, 2), convw[:, i, ch, 2:3], pv(a), ALU.mult, ALU.add, ["zt", ak], [ak])
                    tt(yT[:, 8 + ch, tb * 512:(tb + 1) * 512], a, pc[:, :], ALU.mult, [ak, pck], [("yt", tb)])
            P.barrier()
            NK = T + (256 if grp == "S" else 0)
            qT = carve(AW, 0, [128, T], BF16)
            kT = carve(AW, 4096, [128, NK], BF16)
            Vh = carve(AW, 8704, [128, NK // 128, 128], BF16)
            sqb = [carve(AW, 13312 + 1024 * k, [128, 512], BF16) for k in range(2)]
            rsq = carve(AW, 15360, [128, 512], F32)
            rsb = [carve(AW, 17408 + 2048 * k, [128, 512], F32) for k in range(2)]
            exb = [carve(AW, 21504 + 2048 * k, [128, 1024], BF16) for k in range(2)]
            recb = [carve(AW, 25600 + 1024 * k, [128, 256], F32) for k in range(2)]
            bsl = [carve(AW, 27648 + 5632 * k, [128, NBT, 128], BF16) for k in range(1)]
            stg = [carve(AW, 33280 + 2048 * k, [128, 512], F32) for k in range(2)]
            for h in range(8):
                wsl, wk = nw()
                wt = wsl[:, 0:3 * KC * 128].rearrange("p (s k n) -> p s k n", s=3, k=KC)
                for s_ in range(3):
                    c0 = s_ * 1024 + h * 128
                    dma("pool", wt[:, s_, :, :], wv[:, :, c0:c0 + 128], [], [wk])
                if grp == "S":
                    dma("pool", bsl[0], biasT_d[i, h].rearrange("t k q -> k t q"), [], ["bias"])
                    dma("pool", kT[:, T:T + 256], cnkT_d[i, h], [], [("kT", NT)])
                    dma("pool", Vh[:, T // 128:T // 128 + 2, :], cnv_d[i, h].rearrange("(c p) d -> p c d", p=128),
                        [], [("V", NT)])
                for tb in range(NT):
                    hk = [("ht", c, tb) for c in range(KC)]
                    cols = slice(tb * 512, (tb + 1) * 512)
                    for which, dst, gv, dk in ((0, qT, gqa, ("qT", tb)), (1, kT, gka, ("kT", tb))):
                        pq, pqk = nps()
                        mm([(pq[:, :], wt[:, which, kc, :], HT[:, kc, cols], kc == 0, kc == KC - 1)
                            for kc in range(KC)], [wk] + hk, [pqk])
                        sq = sqb[which]
                        act(sq, pq[:, :], AF.Square, [pqk], [("sqb", which)])
                        pn, pnk = nps()
                        mm([(pn[:, :], ones[:, :], sq, True, True)], [("sqb", which), "ones"], [pnk])
                        act(rsq, pn[:, :], AF.Sqrt, [pnk, "epsb"], ["rsq"], bias=epsb[:, 1:2])
                        rs = rsb[which]
                        recip(rs, rsq, ["rsq"], [("rsb", which)])
                        if grp == "P" and which == 1:
                            sg = stg[tb % 2]
                            stt(sg, pq[:, :], gv[:, i:i + 1], rs, ALU.mult, ALU.mult, [pqk, ("rsb", which), "gsm"],
                                [("stg", tb % 2)])
                            dma("sp", natk_o[i, h, :, cols], sg, [("stg", tb % 2)], [])
                            copy("act", dst[:, cols], sg, [("stg", tb % 2)], [dk])
                        else:
                            stt(dst[:, cols], pq[:, :], gv[:, i:i + 1], rs, ALU.mult, ALU.mult,
                                [pqk, ("rsb", which), "gsm"], [dk])
                    pvv, pvk = nps()
                    mm([(pvv[:, c4 * 128:(c4 + 1) * 128], HT[:, kc, tb * 512 + c4 * 128:tb * 512 + (c4 + 1) * 128],
                         wt[:, 2, kc, :], kc == 0, kc == KC - 1) for c4 in range(4) for kc in range(KC)],
                       [wk] + hk, [pvk])
                    copy("act", Vh[:, tb * 4:tb * 4 + 4, :], pvv[:, :].rearrange("p (a b) -> p a b", a=4), [pvk],
                         [("V", tb)])
                    if grp == "P":
                        sg = stg[tb % 2]
                        copy("dve", sg, pvv[:, :], [pvk], [("stg", tb % 2)])
                        dma("sp", natv_o[i, tb * 512:(tb + 1) * 512, h * 128:(h + 1) * 128]
                            .rearrange("(c p) d -> p c d", p=128),
                            sg.rearrange("p (a b) -> p a b", a=4), [("stg", tb % 2)], [])
                allk = [("qT", t) for t in range(NT)] + [("kT", t) for t in range(NT + 1)] + \
                       [("V", t) for t in range(NT + 1)] + ["bias", "ident"]
                if grp == "P":
                    for s_ in range(4):
                        attn_block([lambda ch: kT[:, ch * 128:(ch + 1) * 128]], [qT[:, s_ * 256:(s_ + 1) * 256]],
                                   lambda ch: Vh[:, ch, :], [2 * s_, 2 * s_ + 1], 256, None,
                                   yT[:, h, s_ * 256:(s_ + 1) * 256], ("yt", s_ // 2), allk,
                                   exb[s_ % 2], ("ex", s_ % 2), recb[s_ % 2], ("rec", s_ % 2))
                else:
                    for j in range(ROWS // 2):
                        lst = NAT_BLOCKS[j]
                        chunks = [kc for kc, _ in lst] + [16, 17]
                        tids = [t for _, t in lst]

                        attn_nat(kT, qT, Vh, chunks, tids, bsl[0], j, h, allk, exb[j % 2], ("ex", j % 2),
                                 recb[j % 2], ("rec", j % 2))
            P.barrier()
            out_proj(w_out_ab[i], t0, T, cd, 32)

        def attn_nat(kT, qT, Vh, chunks, tids, bs, j, h, rkeys, ex, exk, rec, reck):
            Nq = 128
            nch = len(chunks)
            qv = qT[:, j * 128:(j + 1) * 128]
            gi = 0
            while gi < nch:
                g = chunks[gi:gi + 4]
                pS, pSk = nps()
                items = []
                for ci, ch in enumerate(g):
                    o = pS[:, ci * Nq:(ci + 1) * Nq]
                    hasb = (gi + ci) < len(tids)
                    items.append((o, kT[:, ch * 128:(ch + 1) * 128], qv, True, not hasb))
                    if hasb:
                        items.append((o, ident[:, :], bs[:, tids[gi + ci], :], False, True))
                mm(items, rkeys, [pSk])
                act(ex[:, gi * Nq:(gi + len(g)) * Nq], pS[:, 0:len(g) * Nq], AF.Exp, [pSk], [(exk, gi // 4)])
                gi += 4
            pO, pOk = nps()
            od, dd = pO[:, 0:Nq], pO[:, Nq:2 * Nq]
            items = []
            for ci, ch in enumerate(chunks):
                items.append((od, Vh[:, ch, :], ex[:, ci * Nq:(ci + 1) * Nq], ci == 0, ci == nch - 1))
            for ci, ch in enumerate(chunks):
                items.append((dd, ones[:, :], ex[:, ci * Nq:(ci + 1) * Nq], ci == 0, ci == nch - 1))
            mm(items, rkeys + [(exk, 0), (exk, 1), "ones"], [pOk])
            recip(rec[:, 0:Nq], dd, [pOk], [reck])
            tt(yT[:, h, j * 128:(j + 1) * 128], od, rec[:, 0:Nq], ALU.mult, [pOk, reck], [("yt", j // 4)])

        def mixer_odd(l, grp):
            i = l // 2
            if grp == "P":
                t0, T, cd = 0, TP, 0
            else:
                t0, T, cd = TP, TS, 1
            NT = T // 512
            NK = T + (256 if grp == "S" else 0)
            norm_stream(t0, T, gs1, 0, cd)
            wv = w_down[i].rearrange("(kc p) n -> p kc n", p=128)
            cqT = carve(AW, 0, [128, 4, T], BF16)
            ckvT = carve(AW, 16384, [128, 2, NK], BF16)
            krsq = carve(AW, 25600, [64, NK], BF16)
            kgr = carve(AW, 30208, [64, NK], BF16)
            sqb = [carve(YX, 1024 * k, [128, 512], BF16) for k in range(4)]
            rsq = carve(YX, 4096, [128, 512], F32)
            rs = carve(YX, 6144, [128, 512], F32)
            stg = [carve(YX, 8192 + 2048 * k, [128, 512], F32) for k in range(3)]
            cosb = carve(YX, 14336, [64, 512], F32)
            sinb = carve(YX, 16384, [64, 512], F32)
            kg = carve(YX, 18432, [64, 512], BF16)
            t1 = carve(YX, 20480, [64, 512], F32)
            t2 = carve(YX, 22528, [64, 512], F32)
            krc = carve(YX, 24576, [64, 256], F32)
            wslA, wkA = nw()
            wA = wslA.rearrange("p (k n) -> p k n", k=KC)
            dma("pool", wA, wv[:, :, 0:512], [], [wkA])
            wslB, wkB = nw()
            wB = wslB[:, 0:KC * 320].rearrange("p (k n) -> p k n", k=KC)
            dma("pool", wB, wv[:, :, 512:832], [], [wkB])
            if grp == "S":
                dma("pool", ckvT[:, :, T:T + 256], cckvT_d[i].rearrange("(c p) t -> p c t", p=128), [], [("ckvT", NT)])
                dma("sp", krc, ckrT_d[i], [], ["krc"])
                act(krsq[:, T:T + 256], krc, AF.Square, ["krc"], [("krsq", NT)])
                ts(kgr[:, T:T + 256], krc, gkc[0:64, i, 1:2], None, ALU.mult, None, ["krc", "gsm"], [("kgr", NT)])
            for tb in range(NT):
                hk = [("ht", c, tb) for c in range(KC)]
                cols = slice(tb * 512, (tb + 1) * 512)
                pq = []
                for j in range(4):
                    p_, k_ = nps()
                    mm([(p_[:, :], wA[:, kc, j * 128:(j + 1) * 128], HT[:, kc, cols], kc == 0, kc == KC - 1)
                        for kc in range(KC)], [wkA] + hk, [k_])
                    act(sqb[j], p_[:, :], AF.Square, [k_], [("sqb", j)])
                    pq.append((p_, k_))
                pn, pnk = nps()
                mm([(pn[:, :], ones[:, :], sqb[j], j == 0, j == 3) for j in range(4)],
                   [("sqb", j) for j in range(4)] + ["ones"], [pnk])
                act(rsq, pn[:, :], AF.Sqrt, [pnk, "epsb"], ["rsq"], bias=epsb[:, 2:3])
                recip(rs, rsq, ["rsq"], ["rs"])
                for j in range(4):
                    stt(cqT[:, j, cols], pq[j][0][:, :], gcq[:, i, j:j + 1], rs, ALU.mult, ALU.mult,
                        [pq[j][1], "rs", "gsm"], [("cqT", tb)])
                pc = []
                for j in range(2):
                    p_, k_ = nps()
                    mm([(p_[:, :], wB[:, kc, j * 128:(j + 1) * 128], HT[:, kc, cols], kc == 0, kc == KC - 1)
                        for kc in range(KC)], [wkB] + hk, [k_])
                    act(sqb[j], p_[:, :], AF.Square, [k_], [("sqb", j)])
                    pc.append((p_, k_))
                pn, pnk = nps()
                mm([(pn[:, :], ones[:, :], sqb[j], j == 0, j == 1) for j in range(2)],
                   [("sqb", 0), ("sqb", 1), "ones"], [pnk])
                act(rsq, pn[:, :], AF.Sqrt, [pnk, "epsb"], ["rsq"], bias=eps256[:, 0:1])
                recip(rs, rsq, ["rsq"], ["rs"])
                for j in range(2):
                    if grp == "P":
                        sg = stg[j]
                        stt(sg, pc[j][0][:, :], gckv[:, i, j:j + 1], rs, ALU.mult, ALU.mult,
                            [pc[j][1], "rs", "gsm"], [("stg", j)])
                        dma("sp", ckv_o[i, j * 128:(j + 1) * 128, cols], sg, [("stg", j)], [])
                        copy("act", ckvT[:, j, cols], sg, [("stg", j)], [("ckvT", tb)])
                    else:
                        stt(ckvT[:, j, cols], pc[j][0][:, :], gckv[:, i, j:j + 1], rs, ALU.mult, ALU.mult,
                            [pc[j][1], "rs", "gsm"], [("ckvT", tb)])
                pr, prk = nps()
                mm([(pr[0:64, :], wB[:, kc, 256:320], HT[:, kc, cols], kc == 0, kc == KC - 1) for kc in range(KC)],
                   [wkB] + hk, [prk])
                act(krsq[:, cols], pr[0:64, :], AF.Square, [prk], [("krsq", tb)])
                if grp == "P":
                    sg = stg[2]
                    copy("dve", sg[0:64, :], pr[0:64, :], [prk], [("stg", 2)])
                    dma("sp", kr_o[i, :, cols], sg[0:64, :], [("stg", 2)], [])
                    act(kgr[:, cols], pr[0:64, :], AF.Identity, [prk, "gsm"], [("kgr", tb)], scale=gkc[0:64, i, 1:2])
                else:
                    act(kg, pr[0:64, :], AF.Identity, [prk, "gsm"], ["kg"], scale=gkc[0:64, i, 1:2])
                    dma("sp", cosb, cos_d[:, cols], [], ["cosb"])
                    dma("sp", sinb, sin_d[:, cols], [], ["sinb"])
                    pp, ppk = nps()
                    mm([(pp[0:64, :], ptm[:, :], kg, True, True)], ["kg", "ptm"], [ppk])
                    tt(t1, kg, cosb, ALU.mult, ["kg", "cosb"], ["t1"])
                    tt(t2, pp[0:64, :], sinb, ALU.mult, [ppk, "sinb"], ["t2"])
                    tt(kgr[:, cols], t1, t2, ALU.add, ["t1", "t2"], [("kgr", tb)])
            P.barrier()
            HTf = HT[:, :, :].rearrange("p c t -> p (c t)").bitcast(F32)
            kTn = carve(HTf, 0, [128, NK], BF16)
            kTr = carve(HTf, 4608, [64, NK], BF16)
            Vh = carve(HTf, 9216, [128, NK // 128, 128], BF16)
            qTn = [carve(HTf, 13824 + 1024 * k, [128, 512], BF16) for k in range(2)]
            qTr = [carve(HTf, 15872 + 1024 * k, [64, 512], BF16) for k in range(2)]
            sqn = carve(HTf, 17920, [128, 512], BF16)
            sqr = carve(HTf, 18944, [64, 512], BF16)
            rsq2 = carve(HTf, 19968, [128, 512], F32)
            rs2 = carve(HTf, 22016, [128, 512], F32)
            qg = carve(HTf, 24064, [64, 512], BF16)
            u1 = carve(HTf, 25088, [64, 512], F32)
            u2 = carve(HTf, 27136, [64, 512], F32)
            exs = [carve(HTf, 29184 + 1024 * k, [128, 512], BF16) for k in range(4)]
            rec = carve(HTf, 33280, [128, 512], F32)
            cosq = carve(HTf, 35328, [64, 512], F32)
            sinq = carve(HTf, 37376, [64, 512], F32)
            wslK, wkK = nw()
            wK = wslK.rearrange("p (k n) -> p k n", k=2)
            dma("pool", wK, w_ukv[i].rearrange("(kc p) n -> p kc n", p=128), [], [wkK])
            wslQ, wkQ = None, None
            ktiles = [(a, min(512, NK - a)) for a in range(0, NK, 512)]
            latk = [("ckvT", t) for t in range(NT + 1)] + [("krsq", t) for t in range(NT + 1)] + \
                   [("kgr", t) for t in range(NT + 1)]
            for h in range(16):
                if h % 8 == 0:
                    wslQ, wkQ = nw()
                    wcnt[0] += 1
                    wQ = wslQ[:, 0:4 * 1536].rearrange("p (k n) -> p k n", k=4)
                    dma("pool", wQ, w_uq[i].rearrange("(kc p) n -> p kc n", p=128)[:, :, (h // 8) * 1536:(h // 8 + 1) * 1536],
                        [], [wkQ])
                hh = h % 8
                for (a, n) in ktiles:
                    pk_, pkk = nps()
                    mm([(pk_[:, 0:n], wK[:, j, h * 256:h * 256 + 128], ckvT[:, j, a:a + n], j == 0, j == 1)
                        for j in range(2)], [wkK] + latk, [pkk])
                    act(sqn[:, 0:n], pk_[:, 0:n], AF.Square, [pkk], ["sqn"])
                    pn, pnk = nps()
                    mm([(pn[:, 0:n], ones[:, :], sqn[:, 0:n], True, False),
                        (pn[:, 0:n], ones[0:64, :], krsq[:, a:a + n], False, True)], ["sqn", "ones"] + latk, [pnk])
                    act(rsq2[:, 0:n], pn[:, 0:n], AF.Sqrt, [pnk, "epsb"], ["rsq2"], bias=epsb[:, 3:4])
                    recip(rs2[:, 0:n], rsq2[:, 0:n], ["rsq2"], ["rs2"])
                    stt(kTn[:, a:a + n], pk_[:, 0:n], gkc[:, i, 0:1], rs2[:, 0:n], ALU.mult, ALU.mult,
                        [pkk, "rs2", "gsm"], [("kTn", a // 512)])
                    tt(kTr[:, a:a + n], kgr[:, a:a + n], rs2[0:64, 0:n], ALU.mult, ["rs2"] + latk, [("kTr", a // 512)])
                for c4 in range(0, NK // 128, 4):
                    nn = min(4, NK // 128 - c4)
                    pv_, pvk = nps()
                    mm([(pv_[:, q * 128:(q + 1) * 128], ckvT[:, j, (c4 + q) * 128:(c4 + q + 1) * 128],
                         wK[:, j, h * 256 + 128:h * 256 + 256], j == 0, j == 1) for q in range(nn) for j in range(2)],
                       [wkK] + latk, [pvk])
                    copy("act", Vh[:, c4:c4 + nn, :], pv_[:, 0:nn * 128].rearrange("p (a b) -> p a b", a=nn), [pvk],
                         [("Vh", c4 // 4)])
                kvk = [("kTn", t) for t in range(len(ktiles))] + [("kTr", t) for t in range(len(ktiles))] + \
                      [("Vh", t) for t in range((NK // 128 + 3) // 4)]
                for tb in range(NT):
                    cols = slice(tb * 512, (tb + 1) * 512)
                    qs = tb % 2
                    pq_, pqk = nps()
                    mm([(pq_[:, :], wQ[:, j, hh * 192:hh * 192 + 128], cqT[:, j, cols], j == 0, j == 3) for j in range(4)],
                       [wkQ, ("cqT", tb)], [pqk])
                    pr_, prk = nps()
                    mm([(pr_[0:64, :], wQ[:, j, hh * 192 + 128:hh * 192 + 192], cqT[:, j, cols], j == 0, j == 3)
                        for j in range(4)], [wkQ, ("cqT", tb)], [prk])
                    act(sqn, pq_[:, :], AF.Square, [pqk], ["sqn"])
                    act(sqr, pr_[0:64, :], AF.Square, [prk], ["sqr"])
                    pn, pnk = nps()
                    mm([(pn[:, :], ones[:, :], sqn, True, False), (pn[:, :], ones[0:64, :], sqr, False, True)],
                       ["sqn", "sqr", "ones"], [pnk])
                    act(rsq2, pn[:, :], AF.Sqrt, [pnk, "epsb"], ["rsq2"], bias=epsb[:, 3:4])
                    recip(rs2, rsq2, ["rsq2"], ["rs2"])
                    stt(qTn[qs], pq_[:, :], gqc[:, i, 0:1], rs2, ALU.mult, ALU.mult, [pqk, "rs2", "gsm"], [("qTn", qs)])
                    if grp == "P":
                        stt(qTr[qs], pr_[0:64, :], gqc[0:64, i, 1:2], rs2[0:64, :], ALU.mult, ALU.mult,
                            [prk, "rs2", "gsm"], [("qTr", qs)])
                    else:
                        act(qg, pr_[0:64, :], AF.Identity, [prk, "gsm"], ["qg"], scale=gqc[0:64, i, 1:2])
                        if h == 0 or True:
                            dma("sp", cosq, cos_d[:, cols], [], ["cosq"])
                            dma("sp", sinq, sin_d[:, cols], [], ["sinq"])
                        pp, ppk = nps()
                        mm([(pp[0:64, :], ptm[:, :], qg, True, True)], ["qg", "ptm"], [ppk])
                        tt(u1, qg, cosq, ALU.mult, ["qg", "cosq"], ["u1"])
                        tt(u2, pp[0:64, :], sinq, ALU.mult, [ppk, "sinq"], ["u2"])
                        tt(u1, u1, u2, ALU.add, ["u1", "u2"], ["u1"])
                        tt(qTr[qs], u1, rs2[0:64, :], ALU.mult, ["u1", "rs2"], [("qTr", qs)])
                    qk_ = [("qTn", qs), ("qTr", qs)]
                    if grp == "P":
                        for s2 in range(2):
                            s_ = tb * 2 + s2
                            qc = slice(s2 * 256, (s2 + 1) * 256)
                            attn_block([lambda ch: kTn[:, ch * 128:(ch + 1) * 128],
                                        lambda ch: kTr[:, ch * 128:(ch + 1) * 128]],
                                       [qTn[qs][:, qc], qTr[qs][:, qc]], lambda ch: Vh[:, ch, :],
                                       [2 * s_, 2 * s_ + 1], 256, None, yT[:, h, s_ * 256:(s_ + 1) * 256],
                                       ("yt", tb), kvk + qk_, exs[s_ % 2], ("ex", s_ % 2), rec, "rec")
                    else:
                        nch = NK // 128
                        pO, pOk = nps(hold=True)
                        pD, pDk = nps(hold=True)
                        def scores(c):
                            pS, pSk = nps()
                            mm([(pS[:, :], kTn[:, c * 128:(c + 1) * 128], qTn[qs], True, False),
                                (pS[:, :], kTr[:, c * 128:(c + 1) * 128], qTr[qs], False, True)], kvk + qk_, [pSk])
                            act(exs[c % 4], pS[:, :], AF.Exp, [pSk], [("ex", c % 4)])
                        scores(0)
                        for c in range(nch):
                            if c + 1 < nch:
                                scores(c + 1)
                            mm([(pO[:, :], Vh[:, c, :], exs[c % 4], c == 0, c == nch - 1),
                                (pD[:, :], ones[:, :], exs[c % 4], c == 0, c == nch - 1)],
                               kvk + [("ex", c % 4), "ones"], [pOk, pDk])
                        release(pOk)
                        release(pDk)
                        recip(rec, pD[:, :], [pDk], ["rec"])
                        tt(yT[:, h, cols], pO[:, :], rec, ALU.mult, [pOk, "rec"], [("yt", tb)])
            P.barrier()
            out_proj(w_o[i], t0, T, cd, 32)

        def ffn(l, t0, cd, final):
            T = 1024
            for c4 in range(0, KC, 4):
                dma("sp", xres[:, c4:c4 + 4, :], xdv[:, c4:c4 + 4, t0:t0 + T], [("xd", c) for c in range(c4, c4 + 4)],
                    [("xr", c, t) for c in range(c4, c4 + 4) for t in range(2)])
            sq = carve(AW, 0, [128, KC, 256], BF16)
            tmp = carve(AW, 8192, [128, KC, 256], F32)
            rs = carve(AW, 24576, [128, 256], F32)
            rsq = carve(AW, 25600, [128, 256], F32)
            aT = [carve(AW, 26624 + 4096 * k, [128, 2, 1024], BF16) for k in range(2)]
            sg = [carve(AW, 34816 + 2048 * k, [128, 512], F32) for k in range(2)]
            for b in range(T // 256):
                xk = [("xr", c, b // 2) for c in range(KC)]
                xb = xres[:, :, b * 256:(b + 1) * 256]
                act(sq, xb, AF.Square, xk, ["fsq"])
                pn, pnk = nps()
                mm([(pn[:, 0:256], ones[:, :], sq[:, c, :], c == 0, c == KC - 1) for c in range(KC)], ["fsq", "ones"], [pnk])
                act(rsq, pn[:, 0:256], AF.Sqrt, [pnk, "epsb"], ["frsq"], bias=epsb[:, 0:1])
                recip(rs, rsq, ["frsq"], ["frs"])
                tt(tmp, xb, rs.unsqueeze(1).to_broadcast([128, KC, 256]), ALU.mult, xk + ["frs"], ["ftmp"])
                for c in range(KC):
                    act(HT[:, c, b * 256:(b + 1) * 256], tmp[:, c, :], AF.Identity, ["ftmp", "gs2", "modv"],
                        [("ht", c, b // 2)], bias=modv[:, 48 + c, cd:cd + 1], scale=gs2[:, c, cd:cd + 1])
            wi = w_ffn_in[l].rearrange("(kc p) n -> p kc n", p=128)
            wo = w_ffn_out[l].rearrange("(r p) n -> p r n", p=128)
            NG = DFF // 256
            inw = {}
            outw = {}

            def load_in(g):
                wsl, wk = nw()
                wt = wsl.rearrange("p (s k n) -> p s k n", s=2, k=KC)
                dma("pool", wt[:, 0, :, :], wi[:, :, g * 256:(g + 1) * 256], [], [wk])
                dma("pool", wt[:, 1, :, :], wi[:, :, DFF + g * 256:DFF + (g + 1) * 256], [], [wk])
                inw[g] = (wt, wk)

            def load_out(g):
                s_ = g % 2
                k_ = ("wo", s_)
                for j in range(2):
                    for half in range(2):
                        dma("pool", HT[:, 4 * s_ + 2 * j + half, 1024:2048],
                            wo[:, 2 * g + j, half * 1024:(half + 1) * 1024], [], [k_])
                outw[g] = (s_, k_)

            def stage_in(g):
                wt, wk = inw[g]
                s_ = g % 2
                for tb in range(2):
                    hk = [("ht", c, tb) for c in range(KC)]
                    cols = slice(tb * 512, (tb + 1) * 512)
                    for j in range(2):
                        pg, pgk = nps()
                        mm([(pg[:, :], wt[:, 0, kc, j * 128:(j + 1) * 128], HT[:, kc, cols], kc == 0, kc == KC - 1)
                            for kc in range(KC)], [wk] + hk, [pgk])
                        pu, puk = nps()
                        mm([(pu[:, :], wt[:, 1, kc, j * 128:(j + 1) * 128], HT[:, kc, cols], kc == 0, kc == KC - 1)
                            for kc in range(KC)], [wk] + hk, [puk])
                        act(sg[j], pg[:, :], AF.Silu, [pgk], [("sg", j)])
                        tt(aT[s_][:, j, cols], sg[j], pu[:, :], ALU.mult, [("sg", j), puk], [("aT", s_, tb)])

            def stage_out(g):
                s_, k_ = outw[g]
                for tb in range(2):
                    cols = slice(tb * 512, (tb + 1) * 512)
                    for fc in range(KC):
                        py, pyk = nps()
                        mm([(py[:, :], HT[:, 4 * s_ + 2 * j + fc // 8, 1024 + (fc % 8) * 128:1024 + (fc % 8 + 1) * 128],
                             aT[s_][:, j, cols], j == 0, j == 1) for j in range(2)], [k_, ("aT", s_, tb)], [pyk])
                        stt(xres[:, fc, cols], py[:, :], modv[:, 80 + fc, cd:cd + 1], xres[:, fc, cols],
                            ALU.mult, ALU.add, [pyk, ("xr", fc, tb), "modv"], [("xr", fc, tb)])

            load_in(0)
            load_out(0)
            load_in(1)
            stage_in(0)
            for g in range(NG):
                if g + 1 < NG:
                    load_out(g + 1)
                    if g + 2 < NG:
                        load_in(g + 2)
                    stage_in(g + 1)
                stage_out(g)
            dst = fm(yout) if final else xdv
            for c4 in range(0, KC, 4):
                dma("sp", dst[:, c4:c4 + 4, t0:t0 + T], xres[:, c4:c4 + 4, :],
                    [("xr", c, t) for c in range(c4, c4 + 4) for t in range(2)], [("xd", c) for c in range(c4, c4 + 4)])
            P.barrier()

        done = False
        for l in range(depth):
            adaln(l)
            last = (l == depth - 1)
            for grp in ("P", "S"):
                if l % 2 == 0:
                    mixer_even(l, grp)
                else:
                    mixer_odd(l, grp)
                if stop_after == ("mix", l, grp):
                    done = True
                    break
                if grp == "P":
                    ffn(l, 0, 0, last)
                else:
                    ffn(l, TP, 1, last)
                    ffn(l, TP + 1024, 1, last)
            if done:
                break
        if done or stop_after is not None:
            P.barrier()
            for c in range(KC):
                dma("sp", yout[c * 128:(c + 1) * 128, :], xd[c * 128:(c + 1) * 128, :], [], [])
        P.barrier()

        with nc.Block() as block:
            P.emit(nc, block, esem, dsem)
    return nc, P


def _fmvec(v):
    v = np.asarray(v, np.float32)
    lead = v.shape[:-1]
    n = v.shape[-1] // 128
    v = v.reshape(lead + (n, 128))
    return np.ascontiguousarray(np.moveaxis(v, -1, 0))


def _rope_tables():
    half = 32
    freqs = (10000.0 ** (-np.arange(half // 2, dtype=np.float32) / (half // 2))).astype(np.float32)
    t = np.arange(TS)
    cos = np.zeros((64, TS), np.float32)
    sin = np.zeros((64, TS), np.float32)
    for g, pos in enumerate((t // GRID_W, t % GRID_W)):
        ang = pos.astype(np.float32)[None, :] * freqs[:, None]
        c, s = np.cos(ang).astype(np.float32), np.sin(ang).astype(np.float32)
        cos[g * 32:g * 32 + 16] = c
        cos[g * 32 + 16:g * 32 + 32] = c
        sin[g * 32:g * 32 + 16] = s
        sin[g * 32 + 16:g * 32 + 32] = s
    Pm = np.zeros((64, 64), np.float32)
    for g in range(2):
        for d in range(16):
            Pm[g * 32 + d, g * 32 + d + 16] = -1.0
            Pm[g * 32 + d + 16, g * 32 + d] = 1.0
    return cos, sin, np.ascontiguousarray(Pm.T)


_CACHE = {}


def prepare_inputs(inp, depth=DEPTH):
    f = lambda k: np.asarray(inp[k], np.float32)
    shared = {}
    shared["n1g"] = _fmvec(f("norm1_g"))
    shared["n2g"] = _fmvec(f("norm2_g"))
    shared["w_ada"] = f("w_ada")
    shared["b_ada"] = _fmvec(f("b_ada"))
    shared["w_in_ab"] = f("w_in_ab")
    shared["gqa"] = np.ascontiguousarray(f("g_qn_a").T)
    shared["gka"] = np.ascontiguousarray(f("g_kn_a").T)
    cw = f("conv_b_w")
    shared["convw"] = np.ascontiguousarray(cw.reshape(2, 3, 8, 128).transpose(3, 0, 2, 1))
    ro, co, ok = nat_bias_index()
    rpb = f("rpb_a")
    bt = rpb[:, :, ro, co]
    bt = np.where(ok[None, None], bt, np.float32(-30000.0)).astype(np.float32)
    shared["biasT"] = np.ascontiguousarray(bt)
    shared["w_out_ab"] = f("w_out_ab")
    shared["w_down"] = f("w_down_c")
    shared["gcq"] = _fmvec(f("g_cq"))
    shared["gckv"] = _fmvec(f("g_ckv"))
    for nm, key in (("gqc", "g_qn_c"), ("gkc", "g_kn_c")):
        g = f(key)
        a = np.zeros((128, 2, 2), np.float32)
        a[:, :, 0] = g[:, 0:128].T
        a[0:64, :, 1] = g[:, 128:192].T
        shared[nm] = a
    shared["w_uq"] = f("w_uq_c")
    shared["w_ukv"] = f("w_ukv_c")
    shared["w_o"] = f("w_o_c")
    shared["w_ffn_in"] = f("w_ffn_in")
    shared["w_ffn_out"] = f("w_ffn_out")
    cos, sin, pt = _rope_tables()
    shared["ropecos"], shared["ropesin"], shared["ropept"] = cos, sin, pt
    shared["ident"] = np.eye(128, dtype=np.float32)
    xp, xs = f("x_prompt"), f("x_sample")
    cnk, cnv = f("cache_nat_k"), f("cache_nat_v")
    cckv, ckr = f("cache_mla_ckv"), f("cache_mla_krope")
    c, cctx = f("c"), f("c_ctx")
    maps = []
    for core in range(8):
        b = core // 2
        m = dict(shared)
        xT = np.empty((D, TALL), np.float32)
        xT[:, 0:TP] = xp[4 * core:4 * core + 4].reshape(TP, D).T
        xT[:, TP:] = xs[b].T
        m["xT"] = xT
        cd = np.stack([cctx, c[b]], axis=-1)
        m["cond"] = np.ascontiguousarray(cd.reshape(KC, 128, 2).transpose(1, 0, 2))
        m["cnkT"] = np.ascontiguousarray(cnk[b].transpose(0, 1, 3, 2))
        m["cnv"] = np.ascontiguousarray(cnv[b])
        m["cckvT"] = np.ascontiguousarray(cckv[b].transpose(0, 2, 1))
        m["ckrT"] = np.ascontiguousarray(ckr[b].transpose(0, 2, 1))
        maps.append(m)
    return maps


def assemble(results):
    B, SEQ = 32, 256
    y_p = np.empty((B, SEQ, D), np.float32)
    y_s = np.empty((4, TS, D), np.float32)
    nat_k = np.empty((B, 2, 8, SEQ, 128), np.float32)
    nat_v = np.empty((B, 2, 8, SEQ, 128), np.float32)
    ckv = np.empty((B, 2, SEQ, 256), np.float32)
    kr = np.empty((B, 2, SEQ, 64), np.float32)
    for core in range(8):
        r = results[core]
        yT = r["yT"]
        y_p[4 * core:4 * core + 4] = yT[:, 0:TP].T.reshape(4, SEQ, D)
        if core % 2 == 0:
            y_s[core // 2] = yT[:, TP:].T
        nk = r["natk"].reshape(2, 8, 128, 4, SEQ)
        nat_k[4 * core:4 * core + 4] = nk.transpose(3, 0, 1, 4, 2)
        nv = r["natv"].reshape(2, 4, SEQ, 8, 128)
        nat_v[4 * core:4 * core + 4] = nv.transpose(1, 0, 3, 2, 4)
        co = r["ckvo"].reshape(2, 256, 4, SEQ)
        ckv[4 * core:4 * core + 4] = co.transpose(2, 0, 3, 1)
        ko = r["kro"].reshape(2, 64, 4, SEQ)
        kr[4 * core:4 * core + 4] = ko.transpose(2, 0, 3, 1)
    return y_p, y_s, nat_k, nat_v, ckv, kr


def kernel(**inputs):
    if "nc" not in _CACHE:
        _CACHE["nc"] = build()[0]
    nc = _CACHE["nc"]
    maps = prepare_inputs(inputs)
    res = run_bass_kernel_spmd(nc, maps, core_ids=list(range(8)))
    return assemble(res.results)
```

```python
import contextlib
import numpy as np
import concourse.bass as bass
import concourse.mybir as mybir
from concourse.bass_utils import run_bass_kernel_spmd

F32 = mybir.dt.float32
BF16 = mybir.dt.bfloat16
AF = mybir.ActivationFunctionType
ALU = mybir.AluOpType

D = 2048
KC = 16
DEPTH = 4
TP = 1024
TS = 2048
TALL = TP + TS
DFF = 5632
EPS = 1e-6
NDS = 16
import os
OPT_LN = os.environ.get('OPT_LN', '0') == '1'
OPT_LOOK = int(os.environ.get('OPT_LOOK', '2'))
OPT_PIPE = os.environ.get('OPT_PIPE', '1') == '1'
OPT_LNP = os.environ.get('OPT_LNP', '0') == '1'


class Op:
    __slots__ = ("eng", "fn", "waits", "idx", "dma", "dn")


class Prog:
    ENGS = ("pe", "act", "dve", "pool", "sp")

    def __init__(self):
        self.ops = {e: [] for e in self.ENGS}
        self.lastw = {}
        self.readers = {}
        self.seen = {e: {} for e in self.ENGS}
        self.seen_d = {e: {} for e in self.ENGS}
        self.ndma = {e: 0 for e in self.ENGS}
        self.lastc = {e: -1 for e in self.ENGS}

    def _filter(self, eng, deps):
        waits = []
        for t in deps:
            if t[0] == "e":
                if t[1] == eng and eng == "pe":
                    continue
                if self.seen[eng].get(t[1], -1) >= t[2]:
                    continue
                self.seen[eng][t[1]] = t[2]
                waits.append(t)
            else:
                s = self.seen_d[eng].setdefault(t[1], set())
                if t[2] in s:
                    continue
                s.add(t[2])
                waits.append(t)
        return waits

    def add(self, eng, fn, reads=(), writes=(), dma=False):
        deps = set()
        for k in reads:
            t = self.lastw.get(k)
            if t is not None:
                deps.add(t)
        for k in writes:
            t = self.lastw.get(k)
            if t is not None:
                deps.add(t)
            for t in self.readers.get(k, ()):
                deps.add(t)
        op = Op()
        op.eng = eng
        op.idx = len(self.ops[eng])
        op.fn = fn
        op.dma = dma
        op.dn = -1
        if dma:
            n = self.ndma[eng]
            self.ndma[eng] += 1
            op.dn = n
            if n >= NDS:
                deps.add(("d", eng, n - NDS))
            tok = ("d", eng, n)
        else:
            tok = ("e", eng, op.idx)
            if fn is not None:
                self.lastc[eng] = op.idx
        op.waits = self._filter(eng, deps)
        self.ops[eng].append(op)
        if fn is not None:
            for k in reads:
                self.readers.setdefault(k, []).append(tok)
            for k in writes:
                self.lastw[k] = tok
                self.readers[k] = []
        return tok

    def barrier(self):
        toks = []
        for e in self.ENGS:
            if self.lastc[e] >= 0:
                toks.append(("e", e, self.lastc[e]))
            for n in range(max(0, self.ndma[e] - NDS), self.ndma[e]):
                toks.append(("d", e, n))
        for e in self.ENGS:
            op = Op()
            op.eng = e
            op.idx = len(self.ops[e])
            op.fn = None
            op.dma = False
            op.dn = -1
            op.waits = self._filter(e, toks)
            self.ops[e].append(op)
        self.lastw.clear()
        self.readers.clear()

    def emit(self, nc, block, esem, dsem):
        need = set()
        for e in self.ENGS:
            for op in self.ops[e]:
                for t in op.waits:
                    if t[0] == "e":
                        need.add((t[1], t[2]))
        sigval = {}
        for e in self.ENGS:
            c = 0
            for op in self.ops[e]:
                if (not op.dma) and (e, op.idx) in need:
                    c += 1
                    sigval[(e, op.idx)] = c
        self.sigmax = {e: max([v for (ee, _), v in sigval.items() if ee == e] + [0]) for e in self.ENGS}

        def run(en):
            def body(eng):
                for op in self.ops[en]:
                    for t in op.waits:
                        if t[0] == "e":
                            eng.wait_ge(esem[t[1]], sigval[(t[1], t[2])])
                        else:
                            eng.wait_ge(dsem[t[1]][t[2] % NDS], 16 * (t[2] // NDS + 1))
                    if op.fn is None:
                        continue
                    ins = op.fn(eng)
                    if op.dma:
                        ins.then_inc(dsem[en][op.dn % NDS], 16)
                    elif (en, op.idx) in sigval:
                        ins.then_inc(esem[en], 1)
            return body

        block.tensor(run("pe"))
        block.scalar(run("act"))
        block.vector(run("dve"))
        block.gpsimd(run("pool"))
        block.sync(run("sp"))


GRID_W, WIN_R, WIN_C = 64, 8, 16
ROWS = TS // GRID_W


def nat_blocks():
    out = []
    tiles = {}
    for j in range(ROWS // 2):
        rs0 = min(max(2 * j - WIN_R // 2, 0), ROWS - WIN_R)
        rs1 = min(max(2 * j + 1 - WIN_R // 2, 0), ROWS - WIN_R)
        c0 = rs0 // 2
        c1 = (rs1 + WIN_R - 1) // 2
        lst = []
        for kc in range(c0, c1 + 1):
            generic = (2 <= j <= ROWS // 2 - 3)
            key = ("g", kc - j) if generic else ("e", j, kc)
            if key not in tiles:
                tiles[key] = (len(tiles), j, kc)
            lst.append((kc, tiles[key][0]))
        out.append(lst)
    tl = sorted(tiles.values())
    return out, [(j, kc) for (_, j, kc) in tl]


NAT_BLOCKS, NAT_TILES = nat_blocks()
NBT = len(NAT_TILES)


def nat_bias_index():
    ro = np.zeros((NBT, 128, 128), np.int64)
    co = np.zeros((NBT, 128, 128), np.int64)
    ok = np.zeros((NBT, 128, 128), bool)
    kk = np.arange(128)[:, None]
    qq = np.arange(128)[None, :]
    for t, (j, kc) in enumerate(NAT_TILES):
        krow = 2 * kc + kk // 64
        kcol = kk % 64
        qrow = 2 * j + qq // 64
        qcol = qq % 64
        rs = np.clip(qrow - WIN_R // 2, 0, ROWS - WIN_R)
        cs = np.clip(qcol - WIN_C // 2, 0, GRID_W - WIN_C)
        valid = (krow >= rs) & (krow < rs + WIN_R) & (kcol >= cs) & (kcol < cs + WIN_C)
        ro[t] = np.clip(krow - qrow + (WIN_R - 1), 0, 2 * WIN_R - 2)
        co[t] = np.clip(kcol - qcol + (WIN_C - 1), 0, 2 * WIN_C - 2)
        ok[t] = valid
    return ro, co, ok


def build(depth=DEPTH, stop_after=None, plan=None):
    nc = bass.Bass("TRN2", target_bir_lowering=False)
    P = Prog()
    NE = (depth + 1) // 2
    NO = max(depth // 2, 1)

    def din(name, shape):
        return nc.dram_tensor(name, list(shape), F32, kind="ExternalInput").ap()

    def dout(name, shape):
        return nc.dram_tensor(name, list(shape), F32, kind="ExternalOutput").ap()

    xin = din("xT", [D, TALL])
    cond_d = din("cond", [128, KC, 2])
    n1g_d = din("n1g", [128, DEPTH, KC])
    n2g_d = din("n2g", [128, DEPTH, KC])
    w_ada = din("w_ada", [DEPTH, D, 6 * D])
    b_ada = din("b_ada", [128, DEPTH, 96])
    w_in_ab = din("w_in_ab", [2, D, 6144])
    gqa_d = din("gqa", [128, 2])
    gka_d = din("gka", [128, 2])
    convw_d = din("convw", [128, 2, 8, 3])
    biasT_d = din("biasT", [2, 8, NBT, 128, 128])
    w_out_ab = din("w_out_ab", [2, D, D])
    w_down = din("w_down", [2, D, 832])
    gcq_d = din("gcq", [128, 2, 4])
    gckv_d = din("gckv", [128, 2, 2])
    w_uq = din("w_uq", [2, 512, 3072])
    w_ukv = din("w_ukv", [2, 256, 4096])
    gqc_d = din("gqc", [128, 2, 2])
    gkc_d = din("gkc", [128, 2, 2])
    w_o = din("w_o", [2, D, D])
    w_ffn_in = din("w_ffn_in", [DEPTH, D, 2 * DFF])
    w_ffn_out = din("w_ffn_out", [DEPTH, DFF, D])
    cnkT_d = din("cnkT", [2, 8, 128, 256])
    cnv_d = din("cnv", [2, 8, 256, 128])
    cckvT_d = din("cckvT", [2, 256, 256])
    ckrT_d = din("ckrT", [2, 64, 256])
    cos_d = din("ropecos", [64, TS])
    sin_d = din("ropesin", [64, TS])
    pt_d = din("ropept", [64, 64])
    ident_d = din("ident", [128, 128])

    yout = dout("yT", [D, TALL])
    natk_o = dout("natk", [2, 8, 128, TP])
    natv_o = dout("natv", [2, TP, 1024])
    ckv_o = dout("ckvo", [2, 256, TP])
    kr_o = dout("kro", [2, 64, TP])
    xd = nc.dram_tensor("xscratch", [D, TALL], F32).ap()

    def fm(ap):
        return ap.rearrange("(c p) t -> p c t", p=128)

    st = contextlib.ExitStack()
    with st:
        def sb(name, shape, dt):
            return st.enter_context(nc.sbuf_tensor(name, list(shape), dt))

        HT = sb("HT", [128, KC, TS], BF16)
        YX = sb("YX", [128, 16384], F32)
        WR = sb("WR", [128, 16384], BF16)
        AW = sb("AW", [128, 10240], F32)
        ones = sb("ones", [128, 128], BF16)
        ident = sb("identb", [128, 128], BF16)
        ptm = sb("ptm", [64, 64], BF16)
        cond_s = sb("cond_s", [128, KC, 2], F32)
        sct = sb("sct", [128, KC, 2], BF16)
        n1g = sb("n1g_s", [128, DEPTH, KC], F32)
        n2g = sb("n2g_s", [128, DEPTH, KC], F32)
        bada = sb("bada", [128, DEPTH, 96], F32)
        modv = sb("modv", [128, 96, 2], F32)
        gs1 = sb("gs1", [128, KC, 2], F32)
        gs2 = sb("gs2", [128, KC, 2], F32)
        gqa = sb("gqa_s", [128, 2], F32)
        gka = sb("gka_s", [128, 2], F32)
        convw = sb("convw_s", [128, 2, 8, 3], F32)
        gcq = sb("gcq_s", [128, 2, 4], F32)
        gckv = sb("gckv_s", [128, 2, 2], F32)
        gqc = sb("gqc_s", [128, 2, 2], F32)
        gkc = sb("gkc_s", [128, 2, 2], F32)
        ps = [st.enter_context(nc.psum_tensor("ps%d" % i, [128, 512], F32)) for i in range(8)]
        esem = {e: st.enter_context(nc.semaphore("es_" + e)) for e in Prog.ENGS}
        dsem = {e: [st.enter_context(nc.semaphore("ds_%s%d" % (e, i))) for i in range(NDS)]
                for e in ("pool", "sp")}

        psc = [0]
        held = set()

        def nps(hold=False):
            while True:
                i = psc[0] % 8
                psc[0] += 1
                if i not in held:
                    break
            if hold:
                held.add(i)
            return ps[i], ("ps", i)

        def release(key):
            held.discard(key[1])

        def carve(region, off_b, shape, dt):
            n = int(np.prod(shape[1:]))
            eb = 4 if dt == F32 else 2
            assert off_b % 4 == 0
            nf = (n * eb + 3) // 4
            v = region[0:128, off_b // 4: off_b // 4 + nf]
            if dt != F32:
                v = v.bitcast(dt)
            v = v[0:shape[0], 0:n]
            if len(shape) == 3:
                v = v.rearrange("p (a b) -> p a b", a=shape[1])
            elif len(shape) == 4:
                v = v.rearrange("p (a b c) -> p a b c", a=shape[1], b=shape[2])
            return v

        yT = YX[:, :].bitcast(BF16).rearrange("p (c t) -> p c t", c=KC)
        xres = YX[:, :].rearrange("p (c t) -> p c t", c=KC)
        wslot = [WR[:, 0:8192], WR[:, 8192:16384]]
        wcnt = [0]

        def nw():
            i = wcnt[0] % 2
            wcnt[0] += 1
            return wslot[i], ("w", i)

        def dma(q, out, in_, reads, writes):
            P.add(q, lambda e, o=out, i=in_: e.dma_start(out=o, in_=i), reads, writes, dma=True)

        def mm(items, reads, writes):
            def fn(pe, items=items):
                ins = None
                for (o, l, r, s, e) in items:
                    ins = pe.matmul(o, l, r, start=s, stop=e)
                return ins
            P.add("pe", fn, reads, writes)

        def act(out, in_, func, reads, writes, bias=None, scale=None):
            kw = {}
            if bias is not None:
                kw["bias"] = bias
            if scale is not None:
                kw["scale"] = scale
            P.add("act", lambda e, o=out, i=in_, f=func, kw=kw: e.activation(out=o, in_=i, func=f, **kw),
                  reads, writes)

        def tt(out, in0, in1, op, reads, writes, eng="dve"):
            P.add(eng, lambda e, o=out, a=in0, b=in1, op=op: e.tensor_tensor(out=o, in0=a, in1=b, op=op),
                  reads, writes)

        def stt(out, in0, scalar, in1, op0, op1, reads, writes):
            P.add("dve", lambda e, o=out, a=in0, s=scalar, b=in1, o0=op0, o1=op1:
                  e.scalar_tensor_tensor(out=o, in0=a, scalar=s, in1=b, op0=o0, op1=o1), reads, writes)

        def ts(out, in0, s1, s2, op0, op1, reads, writes):
            if s2 is None:
                P.add("dve", lambda e, o=out, a=in0, s=s1, o0=op0:
                      e.tensor_scalar(out=o, in0=a, scalar1=s, scalar2=None, op0=o0), reads, writes)
            else:
                P.add("dve", lambda e, o=out, a=in0, s=s1, s2=s2, o0=op0, o1=op1:
                      e.tensor_scalar(out=o, in0=a, scalar1=s, scalar2=s2, op0=o0, op1=o1), reads, writes)

        LNF = [OPT_LN]

        def rfn():
            return AF.Ln if LNF[0] else AF.Sqrt

        def recip(out, in_, reads, writes):
            if LNF[0]:
                act(out, in_, AF.Exp, reads, writes, scale=-0.5)
            else:
                P.add("dve", lambda e, o=out, i=in_: e.reciprocal(out=o, in_=i), reads, writes)

        def recip_den(out, in_, reads, key):
            if LNF[0]:
                act(out, in_, AF.Ln, reads, [key])
                act(out, out, AF.Exp, [key], [key], scale=-1.0)
            else:
                P.add("dve", lambda e, o=out, i=in_: e.reciprocal(out=o, in_=i), reads, [key])

        def copy(eng, out, in_, reads, writes):
            if eng == "act":
                P.add("act", lambda e, o=out, i=in_: e.copy(out=o, in_=i), reads, writes)
            else:
                P.add(eng, lambda e, o=out, i=in_: e.tensor_copy(out=o, in_=i), reads, writes)

        def rstd_from(psn, npart, ncol, n_eps, sq_tmp, sq_key, rs_out, rs_key, pkey):
            act(sq_tmp[0:npart, 0:ncol], psn[0:npart, 0:ncol], rfn(), [pkey], [sq_key], bias=epsb[0:npart, n_eps:n_eps + 1])
            recip(rs_out[0:npart, 0:ncol], sq_tmp[0:npart, 0:ncol], [sq_key], [rs_key])

        epsb = sb("epsb", [128, 4], F32)
        eps256 = sb("eps256", [128, 1], F32)
        P.add("pool", lambda e: e.memset(ones[:, :], 1.0), [], ["ones"])
        P.add("pool", lambda e: e.memset(epsb[:, 0:1], D * EPS), [], ["epsb"])
        P.add("pool", lambda e: e.memset(epsb[:, 1:2], 128 * EPS), [], ["epsb"])
        P.add("pool", lambda e: e.memset(epsb[:, 2:3], 512 * EPS), [], ["epsb"])
        P.add("pool", lambda e: e.memset(epsb[:, 3:4], 192 * EPS), [], ["epsb"])
        P.add("pool", lambda e: e.memset(eps256[:, :], 256 * EPS), [], ["epsb"])
        dma("pool", ident[:, :], ident_d, [], ["ident"])
        dma("pool", ptm[:, :], pt_d, [], ["ptm"])
        dma("sp", cond_s[:, :, :], cond_d, [], ["cond"])
        dma("sp", n1g[:, :, :], n1g_d, [], ["n1g"])
        dma("sp", n2g[:, :, :], n2g_d, [], ["n2g"])
        dma("sp", bada[:, :, :], b_ada, [], ["bada"])
        dma("sp", gqa[:, :], gqa_d, [], ["gsm"])
        dma("sp", gka[:, :], gka_d, [], ["gsm"])
        dma("sp", convw[:, :, :, :], convw_d, [], ["gsm"])
        dma("sp", gcq[:, :, :], gcq_d, [], ["gsm"])
        dma("sp", gckv[:, :, :], gckv_d, [], ["gsm"])
        dma("sp", gqc[:, :, :], gqc_d, [], ["gsm"])
        dma("sp", gkc[:, :, :], gkc_d, [], ["gsm"])
        for c in range(KC):
            dma("sp", xd[c * 128:(c + 1) * 128, :], xin[c * 128:(c + 1) * 128, :], [], [("xd", c)])
        ts(n1g[:, :, :], n1g[:, :, :], float(np.sqrt(D)), None, ALU.mult, None, ["n1g"], ["n1g"])
        ts(n2g[:, :, :], n2g[:, :, :], float(np.sqrt(D)), None, ALU.mult, None, ["n2g"], ["n2g"])
        ts(gka[:, :], gka[:, :], float(np.sqrt(128.0)), None, ALU.mult, None, ["gsm"], ["gsm"])
        ts(gcq[:, :, :], gcq[:, :, :], float(np.sqrt(512.0)), None, ALU.mult, None, ["gsm"], ["gsm"])
        ts(gckv[:, :, :], gckv[:, :, :], float(np.sqrt(256.0)), None, ALU.mult, None, ["gsm"], ["gsm"])
        ts(gkc[:, :, :], gkc[:, :, :], float(np.sqrt(192.0)), None, ALU.mult, None, ["gsm"], ["gsm"])
        act(sct[:, :, :], cond_s[:, :, :], AF.Silu, ["cond"], ["sct"])
        P.barrier()

        xdv = fm(xd)

        def adaln(l):
            pa, pk = nps(hold=True)
            wv = w_ada[l].rearrange("(kc p) n -> p kc n", p=128)
            for nt in range(24):
                wsl, wk = nw()
                wt = wsl.rearrange("p (k n) -> p k n", k=KC)
                dma("pool", wt, wv[:, :, nt * 512:(nt + 1) * 512], [], [wk])
                items = []
                for j in range(4):
                    ch = nt * 4 + j
                    for kc in range(KC):
                        items.append((pa[:, ch * 2:ch * 2 + 2], wt[:, kc, j * 128:(j + 1) * 128],
                                      sct[:, kc, :], kc == 0, kc == KC - 1))
                mm(items, [wk, "sct"], [pk])
            release(pk)
            tt(modv[:, :, :], pa[:, 0:192].rearrange("p (a b) -> p a b", b=2),
               bada[:, l, :].unsqueeze(2).to_broadcast([128, 96, 2]), ALU.add, [pk, "bada"], ["modv"])
            stt(gs1[:, :, :], modv[:, 16:32, :], 1.0, n1g[:, l, :].unsqueeze(2).to_broadcast([128, KC, 2]),
                ALU.add, ALU.mult, ["modv", "n1g"], ["gs1"])
            stt(gs2[:, :, :], modv[:, 64:80, :], 1.0, n2g[:, l, :].unsqueeze(2).to_broadcast([128, KC, 2]),
                ALU.add, ALU.mult, ["modv", "n2g"], ["gs2"])
            P.barrier()

        def norm_block(xb, xk, sq, tmp, rs, rsq, nb, gsv, shift_lo, cd, hcol0, tag):
            act(sq[:, :, 0:nb], xb, AF.Square, [xk], [tag + "sq"])
            pn, pk = nps()
            mm([(pn[:, 0:nb], ones[:, :], sq[:, c, 0:nb], c == 0, c == KC - 1) for c in range(KC)],
               [tag + "sq", "ones"], [pk])
            act(rsq[:, 0:nb], pn[:, 0:nb], rfn(), [pk, "epsb"], [tag + "rsq"], bias=epsb[:, 0:1])
            recip(rs[:, 0:nb], rsq[:, 0:nb], [tag + "rsq"], [tag + "rs"])
            tt(tmp[:, :, 0:nb], xb, rs[:, 0:nb].unsqueeze(1).to_broadcast([128, KC, nb]), ALU.mult,
               [xk, tag + "rs"], [tag + "tmp"])
            for c in range(KC):
                act(HT[:, c, hcol0:hcol0 + nb], tmp[:, c, 0:nb], AF.Identity, [tag + "tmp", "gs", "modv"],
                    [("ht", c, hcol0 // 512)], bias=modv[:, shift_lo + c, cd:cd + 1], scale=gsv[:, c, cd:cd + 1])

        def norm_stream(t0, T, gsv, shift_lo, cd):
            NB = 256
            xb = [carve(YX, 0, [128, KC, NB], F32), carve(YX, 16384, [128, KC, NB], F32)]
            tmp = carve(YX, 32768, [128, KC, NB], F32)
            sq = carve(YX, 49152, [128, KC, NB], BF16)
            rs = carve(YX, 57344, [128, NB], F32)
            rsq = carve(YX, 58368, [128, NB], F32)
            for b in range(T // NB):
                xk = ("xb", b % 2)
                dma("sp", xb[b % 2], xdv[:, :, t0 + b * NB:t0 + (b + 1) * NB], [("xd", c) for c in range(KC)], [xk])
                norm_block(xb[b % 2], xk, sq, tmp, rs, rsq, NB, gsv, shift_lo, cd, b * NB, "ns")
            P.barrier()

        def out_proj(wsrc, t0, T, cd, gate_lo):
            wv = wsrc.rearrange("(kc p) n -> p kc n", p=128)
            xt = [carve(AW, 2048 * i, [128, 512], F32) for i in range(4)]
            cnt = 0
            for fg in range(4):
                wsl, wk = nw()
                wt = wsl.rearrange("p (k n) -> p k n", k=KC)
                dma("pool", wt, wv[:, :, fg * 512:(fg + 1) * 512], [], [wk])
                for tb in range(T // 512):
                    for j in range(4):
                        fc = fg * 4 + j
                        xk = ("xt", cnt % 4)
                        xtile = xt[cnt % 4]
                        cnt += 1
                        dma("sp", xtile, xdv[:, fc, t0 + tb * 512:t0 + (tb + 1) * 512], [("xd", fc)], [xk])
                        pp, pk = nps()
                        mm([(pp[:, :], wt[:, kc, j * 128:(j + 1) * 128], yT[:, kc, tb * 512:(tb + 1) * 512],
                             kc == 0, kc == KC - 1) for kc in range(KC)], [wk, ("yt", tb)], [pk])
                        stt(xtile, pp[:, :], modv[:, gate_lo + fc, cd:cd + 1], xtile, ALU.mult, ALU.add,
                            [pk, xk, "modv"], [xk])
                        dma("sp", xdv[:, fc, t0 + tb * 512:t0 + (tb + 1) * 512], xtile, [xk], [("xd", fc)])
            P.barrier()

        def attn_block(kparts, qparts, vfn, chunks, Nq, bias, out_ap, okey, rkeys, ex, exk, rec, reck):
            nch = len(chunks)
            per = 512 // Nq
            gi = 0
            while gi < nch:
                g = chunks[gi:gi + per]
                pS, pSk = nps()
                items = []
                for ci, ch in enumerate(g):
                    o = pS[:, ci * Nq:(ci + 1) * Nq]
                    np_ = len(kparts)
                    for pi in range(np_):
                        last = (pi == np_ - 1) and bias is None
                        items.append((o, kparts[pi](ch), qparts[pi], pi == 0, last))
                    if bias is not None:
                        items.append((o, ident[:, :], bias(gi + ci), False, True))
                mm(items, rkeys, [pSk])
                act(ex[:, gi * Nq:(gi + len(g)) * Nq], pS[:, 0:len(g) * Nq], AF.Exp, [pSk], [(exk, gi // per)])
                gi += per
            def finish():
                pO, pOk = nps()
                if Nq == 512:
                    pD, pDk = nps()
                    od, dd = pO[:, 0:512], pD[:, 0:512]
                    wk_ = [pOk, pDk]
                else:
                    od, dd = pO[:, 0:Nq], pO[:, Nq:2 * Nq]
                    wk_ = [pOk]
                items = []
                for ci, ch in enumerate(chunks):
                    items.append((od, vfn(ch), ex[:, ci * Nq:(ci + 1) * Nq], ci == 0, ci == nch - 1))
                for ci, ch in enumerate(chunks):
                    items.append((dd, ones[:, :], ex[:, ci * Nq:(ci + 1) * Nq], ci == 0, ci == nch - 1))
                mm(items, rkeys + [(exk, i) for i in range((nch + per - 1) // per)] + ["ones"], wk_)
                recip_den(rec[:, 0:Nq], dd, wk_, reck)
                tt(out_ap, od, rec[:, 0:Nq], ALU.mult, wk_ + [reck], [okey])
            return finish

        def mixer_even(l, grp):
            i = l // 2
            if grp == "P":
                t0, T, cd, nseq, L = 0, TP, 0, 4, 256
            else:
                t0, T, cd, nseq, L = TP, TS, 1, 1, TS
            norm_stream(t0, T, gs1, 0, cd)
            wv = w_in_ab[i].rearrange("(kc p) n -> p kc n", p=128)
            NT = T // 512
            zt = carve(AW, 0, [128, nseq, L + 2], F32)
            gcs = [carve(AW, 8704 + 2048 * k, [128, 512], F32) for k in range(2)]
            acc = [carve(AW, 12800 + 2048 * k, [128, 512], F32) for k in range(2)]
            P.add("pool", lambda e: e.memset(zt, 0.0), [], ["zt"])

            def zview(tb, sh):
                if grp == "P":
                    return zt[:, 2 * tb:2 * tb + 2, sh:sh + 256]
                return zt[:, 0, tb * 512 + sh:tb * 512 + sh + 512]

            def pv(ap):
                if grp == "P":
                    return ap.rearrange("p (a b) -> p a b", a=2)
                return ap

            for ch in range(8):
                wsl, wk = nw()
                wt = wsl[:, 0:3 * KC * 128].rearrange("p (s k n) -> p s k n", s=3, k=KC)
                for s_ in range(3):
                    c0 = 3072 + s_ * 1024 + ch * 128
                    dma("pool", wt[:, s_, :, :], wv[:, :, c0:c0 + 128], [], [wk])
                for tb in range(NT):
                    hk = [("ht", c, tb) for c in range(KC)]
                    pa, pak = nps()
                    mm([(pa[:, :], wt[:, 1, kc, :], HT[:, kc, tb * 512:(tb + 1) * 512], kc == 0, kc == KC - 1)
                        for kc in range(KC)], [wk] + hk, [pak])
                    pb, pbk = nps()
                    mm([(pb[:, :], wt[:, 2, kc, :], HT[:, kc, tb * 512:(tb + 1) * 512], kc == 0, kc == KC - 1)
                        for kc in range(KC)], [wk] + hk, [pbk])
                    g = gcs[tb % 2]
                    copy("act", g, pa[:, :], [pak], [("gcs", tb % 2)])
                    tt(zview(tb, 1), pv(pb[:, :]), pv(g), ALU.mult, [pbk, ("gcs", tb % 2)], ["zt"])
                for tb in range(NT):
                    hk = [("ht", c, tb) for c in range(KC)]
                    pc, pck = nps()
                    mm([(pc[:, :], wt[:, 0, kc, :], HT[:, kc, tb * 512:(tb + 1) * 512], kc == 0, kc == KC - 1)
                        for kc in range(KC)], [wk] + hk, [pck])
                    a = acc[tb % 2]
                    ak = ("acc", tb % 2)
                    ts(pv(a), zview(tb, 0), convw[:, i, ch, 0:1], None, ALU.mult, None, ["zt", "gsm"], [ak])
                    stt(pv(a), zview(tb, 1), convw[:, i, ch, 1:2], pv(a), ALU.mult, ALU.add, ["zt", ak], [ak])
                    stt(pv(a), zview(tb, 2), convw[:, i, ch, 2:3], pv(a), ALU.mult, ALU.add, ["zt", ak], [ak])
                    tt(yT[:, 8 + ch, tb * 512:(tb + 1) * 512], a, pc[:, :], ALU.mult, [ak, pck], [("yt", tb)])
            P.barrier()
            NK = T + (256 if grp == "S" else 0)
            qT = carve(AW, 0, [128, T], BF16)
            kT = carve(AW, 4096, [128, NK], BF16)
            Vh = carve(AW, 8704, [128, NK // 128, 128], BF16)
            sqb = [carve(AW, 13312 + 1024 * k, [128, 512], BF16) for k in range(2)]
            rsq = carve(AW, 15360, [128, 512], F32)
            rsb = [carve(AW, 17408 + 2048 * k, [128, 512], F32) for k in range(2)]
            exb = [carve(AW, 21504 + 2048 * k, [128, 1024], BF16) for k in range(2)]
            recb = [carve(AW, 25600 + 1024 * k, [128, 256], F32) for k in range(2)]
            bsl = [carve(AW, 27648 + 5632 * k, [128, NBT, 128], BF16) for k in range(1)]
            stg = [carve(AW, 33280 + 2048 * k, [128, 512], F32) for k in range(2)]
            for h in range(8):
                wsl, wk = nw()
                wt = wsl[:, 0:3 * KC * 128].rearrange("p (s k n) -> p s k n", s=3, k=KC)
                for s_ in range(3):
                    c0 = s_ * 1024 + h * 128
                    dma("pool", wt[:, s_, :, :], wv[:, :, c0:c0 + 128], [], [wk])
                if grp == "S":
                    dma("pool", bsl[0], biasT_d[i, h].rearrange("t k q -> k t q"), [], ["bias"])
                    dma("pool", kT[:, T:T + 256], cnkT_d[i, h], [], [("kT", NT)])
                    dma("pool", Vh[:, T // 128:T // 128 + 2, :], cnv_d[i, h].rearrange("(c p) d -> p c d", p=128),
                        [], [("V", NT)])
                for tb in range(NT):
                    hk = [("ht", c, tb) for c in range(KC)]
                    cols = slice(tb * 512, (tb + 1) * 512)
                    for which, dst, gv, dk in ((0, qT, gqa, ("qT", tb)), (1, kT, gka, ("kT", tb))):
                        pq, pqk = nps()
                        mm([(pq[:, :], wt[:, which, kc, :], HT[:, kc, cols], kc == 0, kc == KC - 1)
                            for kc in range(KC)], [wk] + hk, [pqk])
                        sq = sqb[which]
                        act(sq, pq[:, :], AF.Square, [pqk], [("sqb", which)])
                        pn, pnk = nps()
                        mm([(pn[:, :], ones[:, :], sq, True, True)], [("sqb", which), "ones"], [pnk])
                        act(rsq, pn[:, :], rfn(), [pnk, "epsb"], ["rsq"], bias=epsb[:, 1:2])
                        rs = rsb[which]
                        recip(rs, rsq, ["rsq"], [("rsb", which)])
                        if grp == "P" and which == 1:
                            sg = stg[tb % 2]
                            stt(sg, pq[:, :], gv[:, i:i + 1], rs, ALU.mult, ALU.mult, [pqk, ("rsb", which), "gsm"],
                                [("stg", tb % 2)])
                            dma("sp", natk_o[i, h, :, cols], sg, [("stg", tb % 2)], [])
                            copy("act", dst[:, cols], sg, [("stg", tb % 2)], [dk])
                        else:
                            stt(dst[:, cols], pq[:, :], gv[:, i:i + 1], rs, ALU.mult, ALU.mult,
                                [pqk, ("rsb", which), "gsm"], [dk])
                    pvv, pvk = nps()
                    mm([(pvv[:, c4 * 128:(c4 + 1) * 128], HT[:, kc, tb * 512 + c4 * 128:tb * 512 + (c4 + 1) * 128],
                         wt[:, 2, kc, :], kc == 0, kc == KC - 1) for c4 in range(4) for kc in range(KC)],
                       [wk] + hk, [pvk])
                    copy("act", Vh[:, tb * 4:tb * 4 + 4, :], pvv[:, :].rearrange("p (a b) -> p a b", a=4), [pvk],
                         [("V", tb)])
                    if grp == "P":
                        sg = stg[tb % 2]
                        copy("dve", sg, pvv[:, :], [pvk], [("stg", tb % 2)])
                        dma("sp", natv_o[i, tb * 512:(tb + 1) * 512, h * 128:(h + 1) * 128]
                            .rearrange("(c p) d -> p c d", p=128),
                            sg.rearrange("p (a b) -> p a b", a=4), [("stg", tb % 2)], [])
                allk = [("qT", t) for t in range(NT)] + [("kT", t) for t in range(NT + 1)] + \
                       [("V", t) for t in range(NT + 1)] + ["bias", "ident"]
                pend = None
                if grp == "P":
                    for s_ in range(4):
                        fin = attn_block([lambda ch: kT[:, ch * 128:(ch + 1) * 128]], [qT[:, s_ * 256:(s_ + 1) * 256]],
                                         lambda ch: Vh[:, ch, :], [2 * s_, 2 * s_ + 1], 256, None,
                                         yT[:, h, s_ * 256:(s_ + 1) * 256], ("yt", s_ // 2), allk,
                                         exb[s_ % 2], ("ex", s_ % 2), recb[s_ % 2], ("rec", s_ % 2))
                        if not OPT_PIPE:
                            fin()
                            fin = None
                        if pend:
                            pend()
                        pend = fin
                else:
                    for j in range(ROWS // 2):
                        lst = NAT_BLOCKS[j]
                        chunks = [kc for kc, _ in lst] + [16, 17]
                        tids = [t for _, t in lst]

                        fin = attn_nat(kT, qT, Vh, chunks, tids, bsl[0], j, h, allk, exb[j % 2], ("ex", j % 2),
                                       recb[j % 2], ("rec", j % 2))
                        if not OPT_PIPE:
                            fin()
                            fin = None
                        if pend:
                            pend()
                        pend = fin
                if pend:
                    pend()
            P.barrier()
            out_proj(w_out_ab[i], t0, T, cd, 32)

        def attn_nat(kT, qT, Vh, chunks, tids, bs, j, h, rkeys, ex, exk, rec, reck):
            Nq = 128
            nch = len(chunks)
            qv = qT[:, j * 128:(j + 1) * 128]
            gi = 0
            while gi < nch:
                g = chunks[gi:gi + 4]
                pS, pSk = nps()
                items = []
                for ci, ch in enumerate(g):
                    o = pS[:, ci * Nq:(ci + 1) * Nq]
                    hasb = (gi + ci) < len(tids)
                    items.append((o, kT[:, ch * 128:(ch + 1) * 128], qv, True, not hasb))
                    if hasb:
                        items.append((o, ident[:, :], bs[:, tids[gi + ci], :], False, True))
                mm(items, rkeys, [pSk])
                act(ex[:, gi * Nq:(gi + len(g)) * Nq], pS[:, 0:len(g) * Nq], AF.Exp, [pSk], [(exk, gi // 4)])
                gi += 4
            def finish():
                pO, pOk = nps()
                od, dd = pO[:, 0:Nq], pO[:, Nq:2 * Nq]
                items = []
                for ci, ch in enumerate(chunks):
                    items.append((od, Vh[:, ch, :], ex[:, ci * Nq:(ci + 1) * Nq], ci == 0, ci == nch - 1))
                for ci, ch in enumerate(chunks):
                    items.append((dd, ones[:, :], ex[:, ci * Nq:(ci + 1) * Nq], ci == 0, ci == nch - 1))
                mm(items, rkeys + [(exk, 0), (exk, 1), "ones"], [pOk])
                recip_den(rec[:, 0:Nq], dd, [pOk], reck)
                tt(yT[:, h, j * 128:(j + 1) * 128], od, rec[:, 0:Nq], ALU.mult, [pOk, reck], [("yt", j // 4)])
            return finish

        def mixer_odd(l, grp):
            i = l // 2
            if grp == "P":
                t0, T, cd = 0, TP, 0
            else:
                t0, T, cd = TP, TS, 1
            NT = T // 512
            NK = T + (256 if grp == "S" else 0)
            norm_stream(t0, T, gs1, 0, cd)
            LNF[0] = OPT_LN and (grp == "S" or OPT_LNP)
            wv = w_down[i].rearrange("(kc p) n -> p kc n", p=128)
            cqT = carve(AW, 0, [128, 4, T], BF16)
            ckvT = carve(AW, 16384, [128, 2, NK], BF16)
            krsq = carve(AW, 25600, [64, NK], BF16)
            kgr = carve(AW, 30208, [64, NK], BF16)
            sqb = [carve(YX, 1024 * k, [128, 512], BF16) for k in range(4)]
            rsq = carve(YX, 4096, [128, 512], F32)
            rs = carve(YX, 6144, [128, 512], F32)
            stg = [carve(YX, 8192 + 2048 * k, [128, 512], F32) for k in range(3)]
            cosb = carve(YX, 14336, [64, 512], F32)
            sinb = carve(YX, 16384, [64, 512], F32)
            kg = carve(YX, 18432, [64, 512], BF16)
            t1 = carve(YX, 20480, [64, 512], F32)
            t2 = carve(YX, 22528, [64, 512], F32)
            krc = carve(YX, 24576, [64, 256], F32)
            wslA, wkA = nw()
            wA = wslA.rearrange("p (k n) -> p k n", k=KC)
            dma("pool", wA, wv[:, :, 0:512], [], [wkA])
            wslB, wkB = nw()
            wB = wslB[:, 0:KC * 320].rearrange("p (k n) -> p k n", k=KC)
            dma("pool", wB, wv[:, :, 512:832], [], [wkB])
            if grp == "S":
                dma("pool", ckvT[:, :, T:T + 256], cckvT_d[i].rearrange("(c p) t -> p c t", p=128), [], [("ckvT", NT)])
                dma("sp", krc, ckrT_d[i], [], ["krc"])
                act(krsq[:, T:T + 256], krc, AF.Square, ["krc"], [("krsq", NT)])
                ts(kgr[:, T:T + 256], krc, gkc[0:64, i, 1:2], None, ALU.mult, None, ["krc", "gsm"], [("kgr", NT)])
            for tb in range(NT):
                hk = [("ht", c, tb) for c in range(KC)]
                cols = slice(tb * 512, (tb + 1) * 512)
                pq = []
                for j in range(4):
                    p_, k_ = nps()
                    mm([(p_[:, :], wA[:, kc, j * 128:(j + 1) * 128], HT[:, kc, cols], kc == 0, kc == KC - 1)
                        for kc in range(KC)], [wkA] + hk, [k_])
                    act(sqb[j], p_[:, :], AF.Square, [k_], [("sqb", j)])
                    pq.append((p_, k_))
                pn, pnk = nps()
                mm([(pn[:, :], ones[:, :], sqb[j], j == 0, j == 3) for j in range(4)],
                   [("sqb", j) for j in range(4)] + ["ones"], [pnk])
                act(rsq, pn[:, :], rfn(), [pnk, "epsb"], ["rsq"], bias=epsb[:, 2:3])
                recip(rs, rsq, ["rsq"], ["rs"])
                for j in range(4):
                    stt(cqT[:, j, cols], pq[j][0][:, :], gcq[:, i, j:j + 1], rs, ALU.mult, ALU.mult,
                        [pq[j][1], "rs", "gsm"], [("cqT", tb)])
                pc = []
                for j in range(2):
                    p_, k_ = nps()
                    mm([(p_[:, :], wB[:, kc, j * 128:(j + 1) * 128], HT[:, kc, cols], kc == 0, kc == KC - 1)
                        for kc in range(KC)], [wkB] + hk, [k_])
                    act(sqb[j], p_[:, :], AF.Square, [k_], [("sqb", j)])
                    pc.append((p_, k_))
                pn, pnk = nps()
                mm([(pn[:, :], ones[:, :], sqb[j], j == 0, j == 1) for j in range(2)],
                   [("sqb", 0), ("sqb", 1), "ones"], [pnk])
                act(rsq, pn[:, :], rfn(), [pnk, "epsb"], ["rsq"], bias=eps256[:, 0:1])
                recip(rs, rsq, ["rsq"], ["rs"])
                for j in range(2):
                    if grp == "P":
                        sg = stg[j]
                        stt(sg, pc[j][0][:, :], gckv[:, i, j:j + 1], rs, ALU.mult, ALU.mult,
                            [pc[j][1], "rs", "gsm"], [("stg", j)])
                        dma("sp", ckv_o[i, j * 128:(j + 1) * 128, cols], sg, [("stg", j)], [])
                        copy("act", ckvT[:, j, cols], sg, [("stg", j)], [("ckvT", tb)])
                    else:
                        stt(ckvT[:, j, cols], pc[j][0][:, :], gckv[:, i, j:j + 1], rs, ALU.mult, ALU.mult,
                            [pc[j][1], "rs", "gsm"], [("ckvT", tb)])
                pr, prk = nps()
                mm([(pr[0:64, :], wB[:, kc, 256:320], HT[:, kc, cols], kc == 0, kc == KC - 1) for kc in range(KC)],
                   [wkB] + hk, [prk])
                act(krsq[:, cols], pr[0:64, :], AF.Square, [prk], [("krsq", tb)])
                if grp == "P":
                    sg = stg[2]
                    copy("dve", sg[0:64, :], pr[0:64, :], [prk], [("stg", 2)])
                    dma("sp", kr_o[i, :, cols], sg[0:64, :], [("stg", 2)], [])
                    act(kgr[:, cols], pr[0:64, :], AF.Identity, [prk, "gsm"], [("kgr", tb)], scale=gkc[0:64, i, 1:2])
                else:
                    act(kg, pr[0:64, :], AF.Identity, [prk, "gsm"], ["kg"], scale=gkc[0:64, i, 1:2])
                    dma("sp", cosb, cos_d[:, cols], [], ["cosb"])
                    dma("sp", sinb, sin_d[:, cols], [], ["sinb"])
                    pp, ppk = nps()
                    mm([(pp[0:64, :], ptm[:, :], kg, True, True)], ["kg", "ptm"], [ppk])
                    tt(t1, kg, cosb, ALU.mult, ["kg", "cosb"], ["t1"])
                    tt(t2, pp[0:64, :], sinb, ALU.mult, [ppk, "sinb"], ["t2"])
                    tt(kgr[:, cols], t1, t2, ALU.add, ["t1", "t2"], [("kgr", tb)])
            P.barrier()
            HTf = HT[:, :, :].rearrange("p c t -> p (c t)").bitcast(F32)
            kTn = carve(HTf, 0, [128, NK], BF16)
            kTr = carve(HTf, 4608, [64, NK], BF16)
            Vh = carve(HTf, 9216, [128, NK // 128, 128], BF16)
            qTn = [carve(HTf, 13824 + 1024 * k, [128, 512], BF16) for k in range(2)]
            qTr = [carve(HTf, 15872 + 1024 * k, [64, 512], BF16) for k in range(2)]
            sqn = carve(HTf, 17920, [128, 512], BF16)
            sqr = carve(HTf, 18944, [64, 512], BF16)
            rsq2 = carve(HTf, 19968, [128, 512], F32)
            rs2 = carve(HTf, 22016, [128, 512], F32)
            qg = carve(HTf, 24064, [64, 512], BF16)
            u1 = carve(HTf, 25088, [64, 512], F32)
            u2 = carve(HTf, 27136, [64, 512], F32)
            exs = [carve(HTf, 29184 + 1024 * k, [128, 512], BF16) for k in range(4)]
            rec = carve(HTf, 33280, [128, 512], F32)
            cosq = carve(HTf, 35328, [64, 512], F32)
            sinq = carve(HTf, 37376, [64, 512], F32)
            wslK, wkK = nw()
            wK = wslK.rearrange("p (k n) -> p k n", k=2)
            dma("pool", wK, w_ukv[i].rearrange("(kc p) n -> p kc n", p=128), [], [wkK])
            wslQ, wkQ = None, None
            ktiles = [(a, min(512, NK - a)) for a in range(0, NK, 512)]
            latk = [("ckvT", t) for t in range(NT + 1)] + [("krsq", t) for t in range(NT + 1)] + \
                   [("kgr", t) for t in range(NT + 1)]
            for h in range(16):
                if h % 8 == 0:
                    wslQ, wkQ = nw()
                    wcnt[0] += 1
                    wQ = wslQ[:, 0:4 * 1536].rearrange("p (k n) -> p k n", k=4)
                    dma("pool", wQ, w_uq[i].rearrange("(kc p) n -> p kc n", p=128)[:, :, (h // 8) * 1536:(h // 8 + 1) * 1536],
                        [], [wkQ])
                hh = h % 8
                for (a, n) in ktiles:
                    pk_, pkk = nps()
                    mm([(pk_[:, 0:n], wK[:, j, h * 256:h * 256 + 128], ckvT[:, j, a:a + n], j == 0, j == 1)
                        for j in range(2)], [wkK] + latk, [pkk])
                    act(sqn[:, 0:n], pk_[:, 0:n], AF.Square, [pkk], ["sqn"])
                    pn, pnk = nps()
                    mm([(pn[:, 0:n], ones[:, :], sqn[:, 0:n], True, False),
                        (pn[:, 0:n], ones[0:64, :], krsq[:, a:a + n], False, True)], ["sqn", "ones"] + latk, [pnk])
                    act(rsq2[:, 0:n], pn[:, 0:n], rfn(), [pnk, "epsb"], ["rsq2"], bias=epsb[:, 3:4])
                    recip(rs2[:, 0:n], rsq2[:, 0:n], ["rsq2"], ["rs2"])
                    stt(kTn[:, a:a + n], pk_[:, 0:n], gkc[:, i, 0:1], rs2[:, 0:n], ALU.mult, ALU.mult,
                        [pkk, "rs2", "gsm"], [("kTn", a // 512)])
                    tt(kTr[:, a:a + n], kgr[:, a:a + n], rs2[0:64, 0:n], ALU.mult, ["rs2"] + latk, [("kTr", a // 512)])
                for c4 in range(0, NK // 128, 4):
                    nn = min(4, NK // 128 - c4)
                    pv_, pvk = nps()
                    mm([(pv_[:, q * 128:(q + 1) * 128], ckvT[:, j, (c4 + q) * 128:(c4 + q + 1) * 128],
                         wK[:, j, h * 256 + 128:h * 256 + 256], j == 0, j == 1) for q in range(nn) for j in range(2)],
                       [wkK] + latk, [pvk])
                    copy("act", Vh[:, c4:c4 + nn, :], pv_[:, 0:nn * 128].rearrange("p (a b) -> p a b", a=nn), [pvk],
                         [("Vh", c4 // 4)])
                kvk = [("kTn", t) for t in range(len(ktiles))] + [("kTr", t) for t in range(len(ktiles))] + \
                      [("Vh", t) for t in range((NK // 128 + 3) // 4)]
                pendm = [None]
                for tb in range(NT):
                    cols = slice(tb * 512, (tb + 1) * 512)
                    qs = tb % 2
                    pq_, pqk = nps()
                    mm([(pq_[:, :], wQ[:, j, hh * 192:hh * 192 + 128], cqT[:, j, cols], j == 0, j == 3) for j in range(4)],
                       [wkQ, ("cqT", tb)], [pqk])
                    pr_, prk = nps()
                    mm([(pr_[0:64, :], wQ[:, j, hh * 192 + 128:hh * 192 + 192], cqT[:, j, cols], j == 0, j == 3)
                        for j in range(4)], [wkQ, ("cqT", tb)], [prk])
                    act(sqn, pq_[:, :], AF.Square, [pqk], ["sqn"])
                    act(sqr, pr_[0:64, :], AF.Square, [prk], ["sqr"])
                    pn, pnk = nps()
                    mm([(pn[:, :], ones[:, :], sqn, True, False), (pn[:, :], ones[0:64, :], sqr, False, True)],
                       ["sqn", "sqr", "ones"], [pnk])
                    act(rsq2, pn[:, :], rfn(), [pnk, "epsb"], ["rsq2"], bias=epsb[:, 3:4])
                    recip(rs2, rsq2, ["rsq2"], ["rs2"])
                    stt(qTn[qs], pq_[:, :], gqc[:, i, 0:1], rs2, ALU.mult, ALU.mult, [pqk, "rs2", "gsm"], [("qTn", qs)])
                    if grp == "P":
                        stt(qTr[qs], pr_[0:64, :], gqc[0:64, i, 1:2], rs2[0:64, :], ALU.mult, ALU.mult,
                            [prk, "rs2", "gsm"], [("qTr", qs)])
                    else:
                        act(qg, pr_[0:64, :], AF.Identity, [prk, "gsm"], ["qg"], scale=gqc[0:64, i, 1:2])
                        if h == 0 or True:
                            dma("sp", cosq, cos_d[:, cols], [], ["cosq"])
                            dma("sp", sinq, sin_d[:, cols], [], ["sinq"])
                        pp, ppk = nps()
                        mm([(pp[0:64, :], ptm[:, :], qg, True, True)], ["qg", "ptm"], [ppk])
                        tt(u1, qg, cosq, ALU.mult, ["qg", "cosq"], ["u1"])
                        tt(u2, pp[0:64, :], sinq, ALU.mult, [ppk, "sinq"], ["u2"])
                        tt(u1, u1, u2, ALU.add, ["u1", "u2"], ["u1"])
                        tt(qTr[qs], u1, rs2[0:64, :], ALU.mult, ["u1", "rs2"], [("qTr", qs)])
                    qk_ = [("qTn", qs), ("qTr", qs)]
                    if grp == "P":
                        for s2 in range(2):
                            s_ = tb * 2 + s2
                            qc = slice(s2 * 256, (s2 + 1) * 256)
                            fin = attn_block([lambda ch: kTn[:, ch * 128:(ch + 1) * 128],
                                              lambda ch: kTr[:, ch * 128:(ch + 1) * 128]],
                                             [qTn[qs][:, qc], qTr[qs][:, qc]], lambda ch: Vh[:, ch, :],
                                             [2 * s_, 2 * s_ + 1], 256, None, yT[:, h, s_ * 256:(s_ + 1) * 256],
                                             ("yt", tb), kvk + qk_, exs[s_ % 2], ("ex", s_ % 2), rec, "rec")
                            if not OPT_PIPE:
                                fin()
                                fin = None
                            if pendm[0]:
                                pendm[0]()
                            pendm[0] = fin
                    else:
                        nch = NK // 128
                        pO, pOk = nps(hold=True)
                        pD, pDk = nps(hold=True)
                        def scores(c):
                            pS, pSk = nps()
                            mm([(pS[:, :], kTn[:, c * 128:(c + 1) * 128], qTn[qs], True, False),
                                (pS[:, :], kTr[:, c * 128:(c + 1) * 128], qTr[qs], False, True)], kvk + qk_, [pSk])
                            act(exs[c % 4], pS[:, :], AF.Exp, [pSk], [("ex", c % 4)])
                        for c0 in range(OPT_LOOK):
                            scores(c0)
                        for c in range(nch):
                            if c + OPT_LOOK < nch:
                                scores(c + OPT_LOOK)
                            mm([(pO[:, :], Vh[:, c, :], exs[c % 4], c == 0, c == nch - 1),
                                (pD[:, :], ones[:, :], exs[c % 4], c == 0, c == nch - 1)],
                               kvk + [("ex", c % 4), "ones"], [pOk, pDk])
                        release(pOk)
                        release(pDk)
                        recip_den(rec, pD[:, :], [pDk], "rec")
                        tt(yT[:, h, cols], pO[:, :], rec, ALU.mult, [pOk, "rec"], [("yt", tb)])
                if pendm[0]:
                    pendm[0]()
            LNF[0] = OPT_LN
            P.barrier()
            out_proj(w_o[i], t0, T, cd, 32)

        def ffn(l, t0, cd, final):
            T = 1024
            for c4 in range(0, KC, 4):
                dma("sp", xres[:, c4:c4 + 4, :], xdv[:, c4:c4 + 4, t0:t0 + T], [("xd", c) for c in range(c4, c4 + 4)],
                    [("xr", c, t) for c in range(c4, c4 + 4) for t in range(2)])
            sq = carve(AW, 0, [128, KC, 256], BF16)
            tmp = carve(AW, 8192, [128, KC, 256], F32)
            rs = carve(AW, 24576, [128, 256], F32)
            rsq = carve(AW, 25600, [128, 256], F32)
            aT = [carve(AW, 26624 + 4096 * k, [128, 2, 1024], BF16) for k in range(2)]
            sg = [carve(AW, 34816 + 2048 * k, [128, 512], F32) for k in range(2)]
            for b in range(T // 256):
                xk = [("xr", c, b // 2) for c in range(KC)]
                xb = xres[:, :, b * 256:(b + 1) * 256]
                act(sq, xb, AF.Square, xk, ["fsq"])
                pn, pnk = nps()
                mm([(pn[:, 0:256], ones[:, :], sq[:, c, :], c == 0, c == KC - 1) for c in range(KC)], ["fsq", "ones"], [pnk])
                act(rsq, pn[:, 0:256], rfn(), [pnk, "epsb"], ["frsq"], bias=epsb[:, 0:1])
                recip(rs, rsq, ["frsq"], ["frs"])
                tt(tmp, xb, rs.unsqueeze(1).to_broadcast([128, KC, 256]), ALU.mult, xk + ["frs"], ["ftmp"])
                for c in range(KC):
                    act(HT[:, c, b * 256:(b + 1) * 256], tmp[:, c, :], AF.Identity, ["ftmp", "gs2", "modv"],
                        [("ht", c, b // 2)], bias=modv[:, 48 + c, cd:cd + 1], scale=gs2[:, c, cd:cd + 1])
            wi = w_ffn_in[l].rearrange("(kc p) n -> p kc n", p=128)
            wo = w_ffn_out[l].rearrange("(r p) n -> p r n", p=128)
            NG = DFF // 256
            inw = {}
            outw = {}

            def load_in(g):
                wsl, wk = nw()
                wt = wsl.rearrange("p (s k n) -> p s k n", s=2, k=KC)
                dma("pool", wt[:, 0, :, :], wi[:, :, g * 256:(g + 1) * 256], [], [wk])
                dma("pool", wt[:, 1, :, :], wi[:, :, DFF + g * 256:DFF + (g + 1) * 256], [], [wk])
                inw[g] = (wt, wk)

            def load_out(g):
                s_ = g % 2
                k_ = ("wo", s_)
                for j in range(2):
                    for half in range(2):
                        dma("pool", HT[:, 4 * s_ + 2 * j + half, 1024:2048],
                            wo[:, 2 * g + j, half * 1024:(half + 1) * 1024], [], [k_])
                outw[g] = (s_, k_)

            def stage_in(g):
                wt, wk = inw[g]
                s_ = g % 2
                for tb in range(2):
                    hk = [("ht", c, tb) for c in range(KC)]
                    cols = slice(tb * 512, (tb + 1) * 512)
                    for j in range(2):
                        pg, pgk = nps()
                        mm([(pg[:, :], wt[:, 0, kc, j * 128:(j + 1) * 128], HT[:, kc, cols], kc == 0, kc == KC - 1)
                            for kc in range(KC)], [wk] + hk, [pgk])
                        pu, puk = nps()
                        mm([(pu[:, :], wt[:, 1, kc, j * 128:(j + 1) * 128], HT[:, kc, cols], kc == 0, kc == KC - 1)
                            for kc in range(KC)], [wk] + hk, [puk])
                        act(sg[j], pg[:, :], AF.Silu, [pgk], [("sg", j)])
                        tt(aT[s_][:, j, cols], sg[j], pu[:, :], ALU.mult, [("sg", j), puk], [("aT", s_, tb)])

            def stage_out(g):
                s_, k_ = outw[g]
                for tb in range(2):
                    cols = slice(tb * 512, (tb + 1) * 512)
                    for fc in range(KC):
                        py, pyk = nps()
                        mm([(py[:, :], HT[:, 4 * s_ + 2 * j + fc // 8, 1024 + (fc % 8) * 128:1024 + (fc % 8 + 1) * 128],
                             aT[s_][:, j, cols], j == 0, j == 1) for j in range(2)], [k_, ("aT", s_, tb)], [pyk])
                        stt(xres[:, fc, cols], py[:, :], modv[:, 80 + fc, cd:cd + 1], xres[:, fc, cols],
                            ALU.mult, ALU.add, [pyk, ("xr", fc, tb), "modv"], [("xr", fc, tb)])

            load_in(0)
            load_out(0)
            load_in(1)
            stage_in(0)
            for g in range(NG):
                if g + 1 < NG:
                    load_out(g + 1)
                    if g + 2 < NG:
                        load_in(g + 2)
                    stage_in(g + 1)
                stage_out(g)
            dst = fm(yout) if final else xdv
            for c4 in range(0, KC, 4):
                dma("sp", dst[:, c4:c4 + 4, t0:t0 + T], xres[:, c4:c4 + 4, :],
                    [("xr", c, t) for c in range(c4, c4 + 4) for t in range(2)], [("xd", c) for c in range(c4, c4 + 4)])
            P.barrier()

        steps = plan
        if steps is None:
            steps = []
            for l in range(depth):
                steps.append(("ada", l))
                for grp in ("P", "S"):
                    steps.append(("mix", l, grp))
                    if stop_after == ("mix", l, grp):
                        break
                    steps.append(("ffn", l, grp))
                else:
                    continue
                break
        debug = (plan is not None) or (stop_after is not None)
        for s_ in steps:
            l = s_[1]
            last = (l == depth - 1) and not debug
            if s_[0] == "ada":
                adaln(l)
            elif s_[0] == "mix":
                if l % 2 == 0:
                    mixer_even(l, s_[2])
                else:
                    mixer_odd(l, s_[2])
            elif s_[2] == "P":
                ffn(l, 0, 0, last)
            else:
                ffn(l, TP, 1, last)
                ffn(l, TP + 1024, 1, last)
        if debug:
            P.barrier()
            for c in range(KC):
                dma("sp", yout[c * 128:(c + 1) * 128, :], xd[c * 128:(c + 1) * 128, :], [], [])
        P.barrier()

        with nc.Block() as block:
            P.emit(nc, block, esem, dsem)
    return nc, P


def _fmvec(v):
    v = np.asarray(v, np.float32)
    lead = v.shape[:-1]
    n = v.shape[-1] // 128
    v = v.reshape(lead + (n, 128))
    return np.ascontiguousarray(np.moveaxis(v, -1, 0))


def _rope_tables():
    half = 32
    freqs = (10000.0 ** (-np.arange(half // 2, dtype=np.float32) / (half // 2))).astype(np.float32)
    t = np.arange(TS)
    cos = np.zeros((64, TS), np.float32)
    sin = np.zeros((64, TS), np.float32)
    for g, pos in enumerate((t // GRID_W, t % GRID_W)):
        ang = pos.astype(np.float32)[None, :] * freqs[:, None]
        c, s = np.cos(ang).astype(np.float32), np.sin(ang).astype(np.float32)
        cos[g * 32:g * 32 + 16] = c
        cos[g * 32 + 16:g * 32 + 32] = c
        sin[g * 32:g * 32 + 16] = s
        sin[g * 32 + 16:g * 32 + 32] = s
    Pm = np.zeros((64, 64), np.float32)
    for g in range(2):
        for d in range(16):
            Pm[g * 32 + d, g * 32 + d + 16] = -1.0
            Pm[g * 32 + d + 16, g * 32 + d] = 1.0
    return cos, sin, np.ascontiguousarray(Pm.T)


_CACHE = {}


def prepare_inputs(inp, depth=DEPTH):
    f = lambda k: np.asarray(inp[k], np.float32)
    shared = {}
    shared["n1g"] = _fmvec(f("norm1_g"))
    shared["n2g"] = _fmvec(f("norm2_g"))
    shared["w_ada"] = f("w_ada")
    shared["b_ada"] = _fmvec(f("b_ada"))
    shared["w_in_ab"] = f("w_in_ab")
    shared["gqa"] = np.ascontiguousarray(f("g_qn_a").T)
    shared["gka"] = np.ascontiguousarray(f("g_kn_a").T)
    cw = f("conv_b_w")
    shared["convw"] = np.ascontiguousarray(cw.reshape(2, 3, 8, 128).transpose(3, 0, 2, 1))
    ro, co, ok = nat_bias_index()
    rpb = f("rpb_a")
    bt = rpb[:, :, ro, co]
    bt = np.where(ok[None, None], bt, np.float32(-30000.0)).astype(np.float32)
    shared["biasT"] = np.ascontiguousarray(bt)
    shared["w_out_ab"] = f("w_out_ab")
    shared["w_down"] = f("w_down_c")
    shared["gcq"] = _fmvec(f("g_cq"))
    shared["gckv"] = _fmvec(f("g_ckv"))
    for nm, key in (("gqc", "g_qn_c"), ("gkc", "g_kn_c")):
        g = f(key)
        a = np.zeros((128, 2, 2), np.float32)
        a[:, :, 0] = g[:, 0:128].T
        a[0:64, :, 1] = g[:, 128:192].T
        shared[nm] = a
    shared["w_uq"] = f("w_uq_c")
    shared["w_ukv"] = f("w_ukv_c")
    shared["w_o"] = f("w_o_c")
    shared["w_ffn_in"] = f("w_ffn_in")
    shared["w_ffn_out"] = f("w_ffn_out")
    cos, sin, pt = _rope_tables()
    shared["ropecos"], shared["ropesin"], shared["ropept"] = cos, sin, pt
    shared["ident"] = np.eye(128, dtype=np.float32)
    xp, xs = f("x_prompt"), f("x_sample")
    cnk, cnv = f("cache_nat_k"), f("cache_nat_v")
    cckv, ckr = f("cache_mla_ckv"), f("cache_mla_krope")
    c, cctx = f("c"), f("c_ctx")
    maps = []
    for core in range(8):
        b = core // 2
        m = dict(shared)
        xT = np.empty((D, TALL), np.float32)
        xT[:, 0:TP] = xp[4 * core:4 * core + 4].reshape(TP, D).T
        xT[:, TP:] = xs[b].T
        m["xT"] = xT
        cd = np.stack([cctx, c[b]], axis=-1)
        m["cond"] = np.ascontiguousarray(cd.reshape(KC, 128, 2).transpose(1, 0, 2))
        m["cnkT"] = np.ascontiguousarray(cnk[b].transpose(0, 1, 3, 2))
        m["cnv"] = np.ascontiguousarray(cnv[b])
        m["cckvT"] = np.ascontiguousarray(cckv[b].transpose(0, 2, 1))
        m["ckrT"] = np.ascontiguousarray(ckr[b].transpose(0, 2, 1))
        maps.append(m)
    return maps


def assemble(results):
    B, SEQ = 32, 256
    y_p = np.empty((B, SEQ, D), np.float32)
    y_s = np.empty((4, TS, D), np.float32)
    nat_k = np.empty((B, 2, 8, SEQ, 128), np.float32)
    nat_v = np.empty((B, 2, 8, SEQ, 128), np.float32)
    ckv = np.empty((B, 2, SEQ, 256), np.float32)
    kr = np.empty((B, 2, SEQ, 64), np.float32)
    for core in range(8):
        r = results[core]
        yT = r["yT"]
        y_p[4 * core:4 * core + 4] = yT[:, 0:TP].T.reshape(4, SEQ, D)
        if core % 2 == 0:
            y_s[core // 2] = yT[:, TP:].T
        nk = r["natk"].reshape(2, 8, 128, 4, SEQ)
        nat_k[4 * core:4 * core + 4] = nk.transpose(3, 0, 1, 4, 2)
        nv = r["natv"].reshape(2, 4, SEQ, 8, 128)
        nat_v[4 * core:4 * core + 4] = nv.transpose(1, 0, 3, 2, 4)
        co = r["ckvo"].reshape(2, 256, 4, SEQ)
        ckv[4 * core:4 * core + 4] = co.transpose(2, 0, 3, 1)
        ko = r["kro"].reshape(2, 64, 4, SEQ)
        kr[4 * core:4 * core + 4] = ko.transpose(2, 0, 3, 1)
    return y_p, y_s, nat_k, nat_v, ckv, kr


def kernel(**inputs):
    if "nc" not in _CACHE:
        _CACHE["nc"] = build()[0]
    nc = _CACHE["nc"]
    maps = prepare_inputs(inputs)
    res = run_bass_kernel_spmd(nc, maps, core_ids=list(range(8)))
    return assemble(res.results)
```

```python
import contextlib
import numpy as np
import concourse.bass as bass
import concourse.mybir as mybir
from concourse.bass_utils import run_bass_kernel_spmd

F32 = mybir.dt.float32
BF16 = mybir.dt.bfloat16
AF = mybir.ActivationFunctionType
ALU = mybir.AluOpType

D = 2048
KC = 16
DEPTH = 4
TP = 1024
TS = 2048
TALL = TP + TS
DFF = 5632
EPS = 1e-6
NDS = 16
import os
OPT_LN = os.environ.get('OPT_LN', '0') == '1'
OPT_LOOK = int(os.environ.get('OPT_LOOK', '2'))
OPT_PIPE = os.environ.get('OPT_PIPE', '1') == '1'
OPT_LNP = os.environ.get('OPT_LNP', '0') == '1'
OPT_KHIDE = os.environ.get('OPT_KHIDE', '1') == '1'


class Op:
    __slots__ = ("eng", "fn", "waits", "idx", "dma", "dn")


class Prog:
    ENGS = ("pe", "act", "dve", "pool", "sp")

    def __init__(self):
        self.ops = {e: [] for e in self.ENGS}
        self.lastw = {}
        self.readers = {}
        self.seen = {e: {} for e in self.ENGS}
        self.seen_d = {e: {} for e in self.ENGS}
        self.ndma = {e: 0 for e in self.ENGS}
        self.lastc = {e: -1 for e in self.ENGS}

    def _filter(self, eng, deps):
        waits = []
        for t in deps:
            if t[0] == "e":
                if t[1] == eng and eng == "pe":
                    continue
                if self.seen[eng].get(t[1], -1) >= t[2]:
                    continue
                self.seen[eng][t[1]] = t[2]
                waits.append(t)
            else:
                s = self.seen_d[eng].setdefault(t[1], set())
                if t[2] in s:
                    continue
                s.add(t[2])
                waits.append(t)
        return waits

    def add(self, eng, fn, reads=(), writes=(), dma=False):
        deps = set()
        for k in reads:
            t = self.lastw.get(k)
            if t is not None:
                deps.add(t)
        for k in writes:
            t = self.lastw.get(k)
            if t is not None:
                deps.add(t)
            for t in self.readers.get(k, ()):
                deps.add(t)
        op = Op()
        op.eng = eng
        op.idx = len(self.ops[eng])
        op.fn = fn
        op.dma = dma
        op.dn = -1
        if dma:
            n = self.ndma[eng]
            self.ndma[eng] += 1
            op.dn = n
            if n >= NDS:
                deps.add(("d", eng, n - NDS))
            tok = ("d", eng, n)
        else:
            tok = ("e", eng, op.idx)
            if fn is not None:
                self.lastc[eng] = op.idx
        op.waits = self._filter(eng, deps)
        self.ops[eng].append(op)
        if fn is not None:
            for k in reads:
                self.readers.setdefault(k, []).append(tok)
            for k in writes:
                self.lastw[k] = tok
                self.readers[k] = []
        return tok

    def barrier(self):
        toks = []
        for e in self.ENGS:
            if self.lastc[e] >= 0:
                toks.append(("e", e, self.lastc[e]))
            for n in range(max(0, self.ndma[e] - NDS), self.ndma[e]):
                toks.append(("d", e, n))
        for e in self.ENGS:
            op = Op()
            op.eng = e
            op.idx = len(self.ops[e])
            op.fn = None
            op.dma = False
            op.dn = -1
            op.waits = self._filter(e, toks)
            self.ops[e].append(op)
        self.lastw.clear()
        self.readers.clear()

    def emit(self, nc, block, esem, dsem):
        need = set()
        for e in self.ENGS:
            for op in self.ops[e]:
                for t in op.waits:
                    if t[0] == "e":
                        need.add((t[1], t[2]))
        sigval = {}
        for e in self.ENGS:
            c = 0
            for op in self.ops[e]:
                if (not op.dma) and (e, op.idx) in need:
                    c += 1
                    sigval[(e, op.idx)] = c
        self.sigmax = {e: max([v for (ee, _), v in sigval.items() if ee == e] + [0]) for e in self.ENGS}

        def run(en):
            def body(eng):
                for op in self.ops[en]:
                    for t in op.waits:
                        if t[0] == "e":
                            eng.wait_ge(esem[t[1]], sigval[(t[1], t[2])])
                        else:
                            eng.wait_ge(dsem[t[1]][t[2] % NDS], 16 * (t[2] // NDS + 1))
                    if op.fn is None:
                        continue
                    ins = op.fn(eng)
                    if op.dma:
                        ins.then_inc(dsem[en][op.dn % NDS], 16)
                    elif (en, op.idx) in sigval:
                        ins.then_inc(esem[en], 1)
            return body

        block.tensor(run("pe"))
        block.scalar(run("act"))
        block.vector(run("dve"))
        block.gpsimd(run("pool"))
        block.sync(run("sp"))


GRID_W, WIN_R, WIN_C = 64, 8, 16
ROWS = TS // GRID_W


def nat_blocks():
    out = []
    tiles = {}
    for j in range(ROWS // 2):
        rs0 = min(max(2 * j - WIN_R // 2, 0), ROWS - WIN_R)
        rs1 = min(max(2 * j + 1 - WIN_R // 2, 0), ROWS - WIN_R)
        c0 = rs0 // 2
        c1 = (rs1 + WIN_R - 1) // 2
        lst = []
        for kc in range(c0, c1 + 1):
            generic = (2 <= j <= ROWS // 2 - 3)
            key = ("g", kc - j) if generic else ("e", j, kc)
            if key not in tiles:
                tiles[key] = (len(tiles), j, kc)
            lst.append((kc, tiles[key][0]))
        out.append(lst)
    tl = sorted(tiles.values())
    return out, [(j, kc) for (_, j, kc) in tl]


NAT_BLOCKS, NAT_TILES = nat_blocks()
NBT = len(NAT_TILES)


def nat_bias_index():
    ro = np.zeros((NBT, 128, 128), np.int64)
    co = np.zeros((NBT, 128, 128), np.int64)
    ok = np.zeros((NBT, 128, 128), bool)
    kk = np.arange(128)[:, None]
    qq = np.arange(128)[None, :]
    for t, (j, kc) in enumerate(NAT_TILES):
        krow = 2 * kc + kk // 64
        kcol = kk % 64
        qrow = 2 * j + qq // 64
        qcol = qq % 64
        rs = np.clip(qrow - WIN_R // 2, 0, ROWS - WIN_R)
        cs = np.clip(qcol - WIN_C // 2, 0, GRID_W - WIN_C)
        valid = (krow >= rs) & (krow < rs + WIN_R) & (kcol >= cs) & (kcol < cs + WIN_C)
        ro[t] = np.clip(krow - qrow + (WIN_R - 1), 0, 2 * WIN_R - 2)
        co[t] = np.clip(kcol - qcol + (WIN_C - 1), 0, 2 * WIN_C - 2)
        ok[t] = valid
    return ro, co, ok


def build(depth=DEPTH, stop_after=None, plan=None):
    nc = bass.Bass("TRN2", target_bir_lowering=False)
    P = Prog()
    NE = (depth + 1) // 2
    NO = max(depth // 2, 1)

    def din(name, shape):
        return nc.dram_tensor(name, list(shape), F32, kind="ExternalInput").ap()

    def dout(name, shape):
        return nc.dram_tensor(name, list(shape), F32, kind="ExternalOutput").ap()

    xin = din("xT", [D, TALL])
    cond_d = din("cond", [128, KC, 2])
    n1g_d = din("n1g", [128, DEPTH, KC])
    n2g_d = din("n2g", [128, DEPTH, KC])
    w_ada = din("w_ada", [DEPTH, D, 6 * D])
    b_ada = din("b_ada", [128, DEPTH, 96])
    w_in_ab = din("w_in_ab", [2, D, 6144])
    gqa_d = din("gqa", [128, 2])
    gka_d = din("gka", [128, 2])
    convw_d = din("convw", [128, 2, 8, 3])
    biasT_d = din("biasT", [2, 8, NBT, 128, 128])
    w_out_ab = din("w_out_ab", [2, D, D])
    w_down = din("w_down", [2, D, 832])
    gcq_d = din("gcq", [128, 2, 4])
    gckv_d = din("gckv", [128, 2, 2])
    w_uq = din("w_uq", [2, 512, 3072])
    w_ukv = din("w_ukv", [2, 256, 4096])
    gqc_d = din("gqc", [128, 2, 2])
    gkc_d = din("gkc", [128, 2, 2])
    w_o = din("w_o", [2, D, D])
    w_ffn_in = din("w_ffn_in", [DEPTH, D, 2 * DFF])
    w_ffn_out = din("w_ffn_out", [DEPTH, DFF, D])
    cnkT_d = din("cnkT", [2, 8, 128, 256])
    cnv_d = din("cnv", [2, 8, 256, 128])
    cckvT_d = din("cckvT", [2, 256, 256])
    ckrT_d = din("ckrT", [2, 64, 256])
    cos_d = din("ropecos", [64, TS])
    sin_d = din("ropesin", [64, TS])
    pt_d = din("ropept", [64, 64])
    ident_d = din("ident", [128, 128])

    yout = dout("yT", [D, TALL])
    natk_o = dout("natk", [2, 8, 128, TP])
    natv_o = dout("natv", [2, TP, 1024])
    ckv_o = dout("ckvo", [2, 256, TP])
    kr_o = dout("kro", [2, 64, TP])
    xd = nc.dram_tensor("xscratch", [D, TALL], F32).ap()

    def fm(ap):
        return ap.rearrange("(c p) t -> p c t", p=128)

    st = contextlib.ExitStack()
    with st:
        def sb(name, shape, dt):
            return st.enter_context(nc.sbuf_tensor(name, list(shape), dt))

        HT = sb("HT", [128, KC, TS], BF16)
        YX = sb("YX", [128, 16384], F32)
        WR = sb("WR", [128, 16384], BF16)
        AW = sb("AW", [128, 10240], F32)
        ones = sb("ones", [128, 128], BF16)
        ident = sb("identb", [128, 128], BF16)
        ptm = sb("ptm", [64, 64], BF16)
        cond_s = sb("cond_s", [128, KC, 2], F32)
        sct = sb("sct", [128, KC, 2], BF16)
        n1g = sb("n1g_s", [128, DEPTH, KC], F32)
        n2g = sb("n2g_s", [128, DEPTH, KC], F32)
        bada = sb("bada", [128, DEPTH, 96], F32)
        modv = sb("modv", [128, 96, 2], F32)
        gs1 = sb("gs1", [128, KC, 2], F32)
        gs2 = sb("gs2", [128, KC, 2], F32)
        gqa = sb("gqa_s", [128, 2], F32)
        gka = sb("gka_s", [128, 2], F32)
        convw = sb("convw_s", [128, 2, 8, 3], F32)
        gcq = sb("gcq_s", [128, 2, 4], F32)
        gckv = sb("gckv_s", [128, 2, 2], F32)
        gqc = sb("gqc_s", [128, 2, 2], F32)
        gkc = sb("gkc_s", [128, 2, 2], F32)
        ps = [st.enter_context(nc.psum_tensor("ps%d" % i, [128, 512], F32)) for i in range(8)]
        esem = {e: st.enter_context(nc.semaphore("es_" + e)) for e in Prog.ENGS}
        dsem = {e: [st.enter_context(nc.semaphore("ds_%s%d" % (e, i))) for i in range(NDS)]
                for e in ("pool", "sp")}

        psc = [0]
        held = set()

        def nps(hold=False):
            while True:
                i = psc[0] % 8
                psc[0] += 1
                if i not in held:
                    break
            if hold:
                held.add(i)
            return ps[i], ("ps", i)

        def release(key):
            held.discard(key[1])

        def carve(region, off_b, shape, dt):
            n = int(np.prod(shape[1:]))
            eb = 4 if dt == F32 else 2
            assert off_b % 4 == 0
            nf = (n * eb + 3) // 4
            v = region[0:128, off_b // 4: off_b // 4 + nf]
            if dt != F32:
                v = v.bitcast(dt)
            v = v[0:shape[0], 0:n]
            if len(shape) == 3:
                v = v.rearrange("p (a b) -> p a b", a=shape[1])
            elif len(shape) == 4:
                v = v.rearrange("p (a b c) -> p a b c", a=shape[1], b=shape[2])
            return v

        yT = YX[:, :].bitcast(BF16).rearrange("p (c t) -> p c t", c=KC)
        xres = YX[:, :].rearrange("p (c t) -> p c t", c=KC)
        wslot = [WR[:, 0:8192], WR[:, 8192:16384]]
        wcnt = [0]

        def nw():
            i = wcnt[0] % 2
            wcnt[0] += 1
            return wslot[i], ("w", i)

        def dma(q, out, in_, reads, writes):
            P.add(q, lambda e, o=out, i=in_: e.dma_start(out=o, in_=i), reads, writes, dma=True)

        def mm(items, reads, writes):
            def fn(pe, items=items):
                ins = None
                for (o, l, r, s, e) in items:
                    ins = pe.matmul(o, l, r, start=s, stop=e)
                return ins
            P.add("pe", fn, reads, writes)

        def act(out, in_, func, reads, writes, bias=None, scale=None):
            kw = {}
            if bias is not None:
                kw["bias"] = bias
            if scale is not None:
                kw["scale"] = scale
            P.add("act", lambda e, o=out, i=in_, f=func, kw=kw: e.activation(out=o, in_=i, func=f, **kw),
                  reads, writes)

        def tt(out, in0, in1, op, reads, writes, eng="dve"):
            P.add(eng, lambda e, o=out, a=in0, b=in1, op=op: e.tensor_tensor(out=o, in0=a, in1=b, op=op),
                  reads, writes)

        def stt(out, in0, scalar, in1, op0, op1, reads, writes):
            P.add("dve", lambda e, o=out, a=in0, s=scalar, b=in1, o0=op0, o1=op1:
                  e.scalar_tensor_tensor(out=o, in0=a, scalar=s, in1=b, op0=o0, op1=o1), reads, writes)

        def ts(out, in0, s1, s2, op0, op1, reads, writes):
            if s2 is None:
                P.add("dve", lambda e, o=out, a=in0, s=s1, o0=op0:
                      e.tensor_scalar(out=o, in0=a, scalar1=s, scalar2=None, op0=o0), reads, writes)
            else:
                P.add("dve", lambda e, o=out, a=in0, s=s1, s2=s2, o0=op0, o1=op1:
                      e.tensor_scalar(out=o, in0=a, scalar1=s, scalar2=s2, op0=o0, op1=o1), reads, writes)

        LNF = [OPT_LN]

        def rfn():
            return AF.Ln if LNF[0] else AF.Sqrt

        def recip(out, in_, reads, writes):
            if LNF[0]:
                act(out, in_, AF.Exp, reads, writes, scale=-0.5)
            else:
                P.add("dve", lambda e, o=out, i=in_: e.reciprocal(out=o, in_=i), reads, writes)

        def recip_den(out, in_, reads, key):
            if LNF[0]:
                act(out, in_, AF.Ln, reads, [key])
                act(out, out, AF.Exp, [key], [key], scale=-1.0)
            else:
                P.add("dve", lambda e, o=out, i=in_: e.reciprocal(out=o, in_=i), reads, [key])

        def copy(eng, out, in_, reads, writes):
            if eng == "act":
                P.add("act", lambda e, o=out, i=in_: e.copy(out=o, in_=i), reads, writes)
            else:
                P.add(eng, lambda e, o=out, i=in_: e.tensor_copy(out=o, in_=i), reads, writes)

        def rstd_from(psn, npart, ncol, n_eps, sq_tmp, sq_key, rs_out, rs_key, pkey):
            act(sq_tmp[0:npart, 0:ncol], psn[0:npart, 0:ncol], rfn(), [pkey], [sq_key], bias=epsb[0:npart, n_eps:n_eps + 1])
            recip(rs_out[0:npart, 0:ncol], sq_tmp[0:npart, 0:ncol], [sq_key], [rs_key])

        epsb = sb("epsb", [128, 4], F32)
        eps256 = sb("eps256", [128, 1], F32)
        P.add("pool", lambda e: e.memset(ones[:, :], 1.0), [], ["ones"])
        P.add("pool", lambda e: e.memset(epsb[:, 0:1], D * EPS), [], ["epsb"])
        P.add("pool", lambda e: e.memset(epsb[:, 1:2], 128 * EPS), [], ["epsb"])
        P.add("pool", lambda e: e.memset(epsb[:, 2:3], 512 * EPS), [], ["epsb"])
        P.add("pool", lambda e: e.memset(epsb[:, 3:4], 192 * EPS), [], ["epsb"])
        P.add("pool", lambda e: e.memset(eps256[:, :], 256 * EPS), [], ["epsb"])
        dma("pool", ident[:, :], ident_d, [], ["ident"])
        dma("pool", ptm[:, :], pt_d, [], ["ptm"])
        dma("sp", cond_s[:, :, :], cond_d, [], ["cond"])
        dma("sp", n1g[:, :, :], n1g_d, [], ["n1g"])
        dma("sp", n2g[:, :, :], n2g_d, [], ["n2g"])
        dma("sp", bada[:, :, :], b_ada, [], ["bada"])
        dma("sp", gqa[:, :], gqa_d, [], ["gsm"])
        dma("sp", gka[:, :], gka_d, [], ["gsm"])
        dma("sp", convw[:, :, :, :], convw_d, [], ["gsm"])
        dma("sp", gcq[:, :, :], gcq_d, [], ["gsm"])
        dma("sp", gckv[:, :, :], gckv_d, [], ["gsm"])
        dma("sp", gqc[:, :, :], gqc_d, [], ["gsm"])
        dma("sp", gkc[:, :, :], gkc_d, [], ["gsm"])
        for c in range(KC):
            dma("sp", xd[c * 128:(c + 1) * 128, :], xin[c * 128:(c + 1) * 128, :], [], [("xd", c)])
        ts(n1g[:, :, :], n1g[:, :, :], float(np.sqrt(D)), None, ALU.mult, None, ["n1g"], ["n1g"])
        ts(n2g[:, :, :], n2g[:, :, :], float(np.sqrt(D)), None, ALU.mult, None, ["n2g"], ["n2g"])
        ts(gka[:, :], gka[:, :], float(np.sqrt(128.0)), None, ALU.mult, None, ["gsm"], ["gsm"])
        ts(gcq[:, :, :], gcq[:, :, :], float(np.sqrt(512.0)), None, ALU.mult, None, ["gsm"], ["gsm"])
        ts(gckv[:, :, :], gckv[:, :, :], float(np.sqrt(256.0)), None, ALU.mult, None, ["gsm"], ["gsm"])
        ts(gkc[:, :, :], gkc[:, :, :], float(np.sqrt(192.0)), None, ALU.mult, None, ["gsm"], ["gsm"])
        act(sct[:, :, :], cond_s[:, :, :], AF.Silu, ["cond"], ["sct"])
        P.barrier()

        xdv = fm(xd)

        def adaln(l):
            pa, pk = nps(hold=True)
            wv = w_ada[l].rearrange("(kc p) n -> p kc n", p=128)
            for nt in range(24):
                wsl, wk = nw()
                wt = wsl.rearrange("p (k n) -> p k n", k=KC)
                dma("pool", wt, wv[:, :, nt * 512:(nt + 1) * 512], [], [wk])
                items = []
                for j in range(4):
                    ch = nt * 4 + j
                    for kc in range(KC):
                        items.append((pa[:, ch * 2:ch * 2 + 2], wt[:, kc, j * 128:(j + 1) * 128],
                                      sct[:, kc, :], kc == 0, kc == KC - 1))
                mm(items, [wk, "sct"], [pk])
            release(pk)
            tt(modv[:, :, :], pa[:, 0:192].rearrange("p (a b) -> p a b", b=2),
               bada[:, l, :].unsqueeze(2).to_broadcast([128, 96, 2]), ALU.add, [pk, "bada"], ["modv"])
            stt(gs1[:, :, :], modv[:, 16:32, :], 1.0, n1g[:, l, :].unsqueeze(2).to_broadcast([128, KC, 2]),
                ALU.add, ALU.mult, ["modv", "n1g"], ["gs1"])
            stt(gs2[:, :, :], modv[:, 64:80, :], 1.0, n2g[:, l, :].unsqueeze(2).to_broadcast([128, KC, 2]),
                ALU.add, ALU.mult, ["modv", "n2g"], ["gs2"])
            P.barrier()

        def norm_block(xb, xk, sq, tmp, rs, rsq, nb, gsv, shift_lo, cd, hcol0, tag):
            act(sq[:, :, 0:nb], xb, AF.Square, [xk], [tag + "sq"])
            pn, pk = nps()
            mm([(pn[:, 0:nb], ones[:, :], sq[:, c, 0:nb], c == 0, c == KC - 1) for c in range(KC)],
               [tag + "sq", "ones"], [pk])
            act(rsq[:, 0:nb], pn[:, 0:nb], rfn(), [pk, "epsb"], [tag + "rsq"], bias=epsb[:, 0:1])
            recip(rs[:, 0:nb], rsq[:, 0:nb], [tag + "rsq"], [tag + "rs"])
            tt(tmp[:, :, 0:nb], xb, rs[:, 0:nb].unsqueeze(1).to_broadcast([128, KC, nb]), ALU.mult,
               [xk, tag + "rs"], [tag + "tmp"])
            for c in range(KC):
                act(HT[:, c, hcol0:hcol0 + nb], tmp[:, c, 0:nb], AF.Identity, [tag + "tmp", "gs", "modv"],
                    [("ht", c, hcol0 // 512)], bias=modv[:, shift_lo + c, cd:cd + 1], scale=gsv[:, c, cd:cd + 1])

        def norm_stream(t0, T, gsv, shift_lo, cd):
            NB = 256
            xb = [carve(YX, 0, [128, KC, NB], F32), carve(YX, 16384, [128, KC, NB], F32)]
            tmp = carve(YX, 32768, [128, KC, NB], F32)
            sq = carve(YX, 49152, [128, KC, NB], BF16)
            rs = carve(YX, 57344, [128, NB], F32)
            rsq = carve(YX, 58368, [128, NB], F32)
            for b in range(T // NB):
                xk = ("xb", b % 2)
                dma("sp", xb[b % 2], xdv[:, :, t0 + b * NB:t0 + (b + 1) * NB], [("xd", c) for c in range(KC)], [xk])
                norm_block(xb[b % 2], xk, sq, tmp, rs, rsq, NB, gsv, shift_lo, cd, b * NB, "ns")
            P.barrier()

        def out_proj(wsrc, t0, T, cd, gate_lo):
            wv = wsrc.rearrange("(kc p) n -> p kc n", p=128)
            xt = [carve(AW, 2048 * i, [128, 512], F32) for i in range(4)]
            cnt = 0
            for fg in range(4):
                wsl, wk = nw()
                wt = wsl.rearrange("p (k n) -> p k n", k=KC)
                dma("pool", wt, wv[:, :, fg * 512:(fg + 1) * 512], [], [wk])
                for tb in range(T // 512):
                    for j in range(4):
                        fc = fg * 4 + j
                        xk = ("xt", cnt % 4)
                        xtile = xt[cnt % 4]
                        cnt += 1
                        dma("sp", xtile, xdv[:, fc, t0 + tb * 512:t0 + (tb + 1) * 512], [("xd", fc)], [xk])
                        pp, pk = nps()
                        mm([(pp[:, :], wt[:, kc, j * 128:(j + 1) * 128], yT[:, kc, tb * 512:(tb + 1) * 512],
                             kc == 0, kc == KC - 1) for kc in range(KC)], [wk, ("yt", tb)], [pk])
                        stt(xtile, pp[:, :], modv[:, gate_lo + fc, cd:cd + 1], xtile, ALU.mult, ALU.add,
                            [pk, xk, "modv"], [xk])
                        dma("sp", xdv[:, fc, t0 + tb * 512:t0 + (tb + 1) * 512], xtile, [xk], [("xd", fc)])
            P.barrier()

        def attn_block(kparts, qparts, vfn, chunks, Nq, bias, out_ap, okey, rkeys, ex, exk, rec, reck):
            nch = len(chunks)
            per = 512 // Nq
            gi = 0
            while gi < nch:
                g = chunks[gi:gi + per]
                pS, pSk = nps()
                items = []
                for ci, ch in enumerate(g):
                    o = pS[:, ci * Nq:(ci + 1) * Nq]
                    np_ = len(kparts)
                    for pi in range(np_):
                        last = (pi == np_ - 1) and bias is None
                        items.append((o, kparts[pi](ch), qparts[pi], pi == 0, last))
                    if bias is not None:
                        items.append((o, ident[:, :], bias(gi + ci), False, True))
                mm(items, rkeys, [pSk])
                act(ex[:, gi * Nq:(gi + len(g)) * Nq], pS[:, 0:len(g) * Nq], AF.Exp, [pSk], [(exk, gi // per)])
                gi += per
            def finish():
                pO, pOk = nps()
                if Nq == 512:
                    pD, pDk = nps()
                    od, dd = pO[:, 0:512], pD[:, 0:512]
                    wk_ = [pOk, pDk]
                else:
                    od, dd = pO[:, 0:Nq], pO[:, Nq:2 * Nq]
                    wk_ = [pOk]
                items = []
                for ci, ch in enumerate(chunks):
                    items.append((od, vfn(ch), ex[:, ci * Nq:(ci + 1) * Nq], ci == 0, ci == nch - 1))
                for ci, ch in enumerate(chunks):
                    items.append((dd, ones[:, :], ex[:, ci * Nq:(ci + 1) * Nq], ci == 0, ci == nch - 1))
                mm(items, rkeys + [(exk, i) for i in range((nch + per - 1) // per)] + ["ones"], wk_)
                recip_den(rec[:, 0:Nq], dd, wk_, reck)
                tt(out_ap, od, rec[:, 0:Nq], ALU.mult, wk_ + [reck], [okey])
            return finish

        def mixer_even(l, grp):
            i = l // 2
            if grp == "P":
                t0, T, cd, nseq, L = 0, TP, 0, 4, 256
            else:
                t0, T, cd, nseq, L = TP, TS, 1, 1, TS
            norm_stream(t0, T, gs1, 0, cd)
            wv = w_in_ab[i].rearrange("(kc p) n -> p kc n", p=128)
            NT = T // 512
            zt = carve(AW, 0, [128, nseq, L + 2], F32)
            gcs = [carve(AW, 8704 + 2048 * k, [128, 512], F32) for k in range(2)]
            acc = [carve(AW, 12800 + 2048 * k, [128, 512], F32) for k in range(2)]
            P.add("pool", lambda e: e.memset(zt, 0.0), [], ["zt"])

            def zview(tb, sh):
                if grp == "P":
                    return zt[:, 2 * tb:2 * tb + 2, sh:sh + 256]
                return zt[:, 0, tb * 512 + sh:tb * 512 + sh + 512]

            def pv(ap):
                if grp == "P":
                    return ap.rearrange("p (a b) -> p a b", a=2)
                return ap

            for ch in range(8):
                wsl, wk = nw()
                wt = wsl[:, 0:3 * KC * 128].rearrange("p (s k n) -> p s k n", s=3, k=KC)
                for s_ in range(3):
                    c0 = 3072 + s_ * 1024 + ch * 128
                    dma("pool", wt[:, s_, :, :], wv[:, :, c0:c0 + 128], [], [wk])
                for tb in range(NT):
                    hk = [("ht", c, tb) for c in range(KC)]
                    pa, pak = nps()
                    mm([(pa[:, :], wt[:, 1, kc, :], HT[:, kc, tb * 512:(tb + 1) * 512], kc == 0, kc == KC - 1)
                        for kc in range(KC)], [wk] + hk, [pak])
                    pb, pbk = nps()
                    mm([(pb[:, :], wt[:, 2, kc, :], HT[:, kc, tb * 512:(tb + 1) * 512], kc == 0, kc == KC - 1)
                        for kc in range(KC)], [wk] + hk, [pbk])
                    g = gcs[tb % 2]
                    copy("act", g, pa[:, :], [pak], [("gcs", tb % 2)])
                    tt(zview(tb, 1), pv(pb[:, :]), pv(g), ALU.mult, [pbk, ("gcs", tb % 2)], ["zt"])
                for tb in range(NT):
                    hk = [("ht", c, tb) for c in range(KC)]
                    pc, pck = nps()
                    mm([(pc[:, :], wt[:, 0, kc, :], HT[:, kc, tb * 512:(tb + 1) * 512], kc == 0, kc == KC - 1)
                        for kc in range(KC)], [wk] + hk, [pck])
                    a = acc[tb % 2]
                    ak = ("acc", tb % 2)
                    ts(pv(a), zview(tb, 0), convw[:, i, ch, 0:1], None, ALU.mult, None, ["zt", "gsm"], [ak])
                    stt(pv(a), zview(tb, 1), convw[:, i, ch, 1:2], pv(a), ALU.mult, ALU.add, ["zt", ak], [ak])
                    stt(pv(a), zview(tb, 2), convw[:, i, ch, 2:3], pv(a), ALU.mult, ALU.add, ["zt", ak], [ak])
                    tt(yT[:, 8 + ch, tb * 512:(tb + 1) * 512], a, pc[:, :], ALU.mult, [ak, pck], [("yt", tb)])
            P.barrier()
            NK = T + (256 if grp == "S" else 0)
            qT = carve(AW, 0, [128, T], BF16)
            kT = carve(AW, 4096, [128, NK], BF16)
            Vh = carve(AW, 8704, [128, NK // 128, 128], BF16)
            sqb = [carve(AW, 13312 + 1024 * k, [128, 512], BF16) for k in range(2)]
            rsq = carve(AW, 15360, [128, 512], F32)
            rsb = [carve(AW, 17408 + 2048 * k, [128, 512], F32) for k in range(2)]
            exb = [carve(AW, 21504 + 2048 * k, [128, 1024], BF16) for k in range(2)]
            recb = [carve(AW, 25600 + 1024 * k, [128, 256], F32) for k in range(2)]
            bsl = [carve(AW, 27648 + 5632 * k, [128, NBT, 128], BF16) for k in range(1)]
            stg = [carve(AW, 33280 + 2048 * k, [128, 512], F32) for k in range(2)]
            for h in range(8):
                wsl, wk = nw()
                wt = wsl[:, 0:3 * KC * 128].rearrange("p (s k n) -> p s k n", s=3, k=KC)
                for s_ in range(3):
                    c0 = s_ * 1024 + h * 128
                    dma("pool", wt[:, s_, :, :], wv[:, :, c0:c0 + 128], [], [wk])
                if grp == "S":
                    dma("pool", bsl[0], biasT_d[i, h].rearrange("t k q -> k t q"), [], ["bias"])
                    dma("pool", kT[:, T:T + 256], cnkT_d[i, h], [], [("kT", NT)])
                    dma("pool", Vh[:, T // 128:T // 128 + 2, :], cnv_d[i, h].rearrange("(c p) d -> p c d", p=128),
                        [], [("V", NT)])
                for tb in range(NT):
                    hk = [("ht", c, tb) for c in range(KC)]
                    cols = slice(tb * 512, (tb + 1) * 512)
                    for which, dst, gv, dk in ((0, qT, gqa, ("qT", tb)), (1, kT, gka, ("kT", tb))):
                        pq, pqk = nps()
                        mm([(pq[:, :], wt[:, which, kc, :], HT[:, kc, cols], kc == 0, kc == KC - 1)
                            for kc in range(KC)], [wk] + hk, [pqk])
                        sq = sqb[which]
                        act(sq, pq[:, :], AF.Square, [pqk], [("sqb", which)])
                        pn, pnk = nps()
                        mm([(pn[:, :], ones[:, :], sq, True, True)], [("sqb", which), "ones"], [pnk])
                        act(rsq, pn[:, :], rfn(), [pnk, "epsb"], ["rsq"], bias=epsb[:, 1:2])
                        rs = rsb[which]
                        recip(rs, rsq, ["rsq"], [("rsb", which)])
                        if grp == "P" and which == 1:
                            sg = stg[tb % 2]
                            stt(sg, pq[:, :], gv[:, i:i + 1], rs, ALU.mult, ALU.mult, [pqk, ("rsb", which), "gsm"],
                                [("stg", tb % 2)])
                            dma("sp", natk_o[i, h, :, cols], sg, [("stg", tb % 2)], [])
                            copy("act", dst[:, cols], sg, [("stg", tb % 2)], [dk])
                        else:
                            stt(dst[:, cols], pq[:, :], gv[:, i:i + 1], rs, ALU.mult, ALU.mult,
                                [pqk, ("rsb", which), "gsm"], [dk])
                    pvv, pvk = nps()
                    mm([(pvv[:, c4 * 128:(c4 + 1) * 128], HT[:, kc, tb * 512 + c4 * 128:tb * 512 + (c4 + 1) * 128],
                         wt[:, 2, kc, :], kc == 0, kc == KC - 1) for c4 in range(4) for kc in range(KC)],
                       [wk] + hk, [pvk])
                    copy("act", Vh[:, tb * 4:tb * 4 + 4, :], pvv[:, :].rearrange("p (a b) -> p a b", a=4), [pvk],
                         [("V", tb)])
                    if grp == "P":
                        sg = stg[tb % 2]
                        copy("dve", sg, pvv[:, :], [pvk], [("stg", tb % 2)])
                        dma("sp", natv_o[i, tb * 512:(tb + 1) * 512, h * 128:(h + 1) * 128]
                            .rearrange("(c p) d -> p c d", p=128),
                            sg.rearrange("p (a b) -> p a b", a=4), [("stg", tb % 2)], [])
                allk = [("qT", t) for t in range(NT)] + [("kT", t) for t in range(NT + 1)] + \
                       [("V", t) for t in range(NT + 1)] + ["bias", "ident"]
                pend = None
                if grp == "P":
                    for s_ in range(4):
                        fin = attn_block([lambda ch: kT[:, ch * 128:(ch + 1) * 128]], [qT[:, s_ * 256:(s_ + 1) * 256]],
                                         lambda ch: Vh[:, ch, :], [2 * s_, 2 * s_ + 1], 256, None,
                                         yT[:, h, s_ * 256:(s_ + 1) * 256], ("yt", s_ // 2), allk,
                                         exb[s_ % 2], ("ex", s_ % 2), recb[s_ % 2], ("rec", s_ % 2))
                        if not OPT_PIPE:
                            fin()
                            fin = None
                        if pend:
                            pend()
                        pend = fin
                else:
                    for j in range(ROWS // 2):
                        lst = NAT_BLOCKS[j]
                        chunks = [kc for kc, _ in lst] + [16, 17]
                        tids = [t for _, t in lst]

                        fin = attn_nat(kT, qT, Vh, chunks, tids, bsl[0], j, h, allk, exb[j % 2], ("ex", j % 2),
                                       recb[j % 2], ("rec", j % 2))
                        if not OPT_PIPE:
                            fin()
                            fin = None
                        if pend:
                            pend()
                        pend = fin
                if pend:
                    pend()
            P.barrier()
            out_proj(w_out_ab[i], t0, T, cd, 32)

        def attn_nat(kT, qT, Vh, chunks, tids, bs, j, h, rkeys, ex, exk, rec, reck):
            Nq = 128
            nch = len(chunks)
            qv = qT[:, j * 128:(j + 1) * 128]
            gi = 0
            while gi < nch:
                g = chunks[gi:gi + 4]
                pS, pSk = nps()
                items = []
                for ci, ch in enumerate(g):
                    o = pS[:, ci * Nq:(ci + 1) * Nq]
                    hasb = (gi + ci) < len(tids)
                    items.append((o, kT[:, ch * 128:(ch + 1) * 128], qv, True, not hasb))
                    if hasb:
                        items.append((o, ident[:, :], bs[:, tids[gi + ci], :], False, True))
                mm(items, rkeys, [pSk])
                act(ex[:, gi * Nq:(gi + len(g)) * Nq], pS[:, 0:len(g) * Nq], AF.Exp, [pSk], [(exk, gi // 4)])
                gi += 4
            def finish():
                pO, pOk = nps()
                od, dd = pO[:, 0:Nq], pO[:, Nq:2 * Nq]
                items = []
                for ci, ch in enumerate(chunks):
                    items.append((od, Vh[:, ch, :], ex[:, ci * Nq:(ci + 1) * Nq], ci == 0, ci == nch - 1))
                for ci, ch in enumerate(chunks):
                    items.append((dd, ones[:, :], ex[:, ci * Nq:(ci + 1) * Nq], ci == 0, ci == nch - 1))
                mm(items, rkeys + [(exk, 0), (exk, 1), "ones"], [pOk])
                recip_den(rec[:, 0:Nq], dd, [pOk], reck)
                tt(yT[:, h, j * 128:(j + 1) * 128], od, rec[:, 0:Nq], ALU.mult, [pOk, reck], [("yt", j // 4)])
            return finish

        def mixer_odd(l, grp):
            i = l // 2
            if grp == "P":
                t0, T, cd = 0, TP, 0
            else:
                t0, T, cd = TP, TS, 1
            NT = T // 512
            NK = T + (256 if grp == "S" else 0)
            norm_stream(t0, T, gs1, 0, cd)
            LNF[0] = OPT_LN and (grp == "S" or OPT_LNP)
            wv = w_down[i].rearrange("(kc p) n -> p kc n", p=128)
            cqT = carve(AW, 0, [128, 4, T], BF16)
            ckvT = carve(AW, 16384, [128, 2, NK], BF16)
            krsq = carve(AW, 25600, [64, NK], BF16)
            kgr = carve(AW, 30208, [64, NK], BF16)
            sqb = [carve(YX, 1024 * k, [128, 512], BF16) for k in range(4)]
            rsq = carve(YX, 4096, [128, 512], F32)
            rs = carve(YX, 6144, [128, 512], F32)
            stg = [carve(YX, 8192 + 2048 * k, [128, 512], F32) for k in range(3)]
            cosb = carve(YX, 14336, [64, 512], F32)
            sinb = carve(YX, 16384, [64, 512], F32)
            kg = carve(YX, 18432, [64, 512], BF16)
            t1 = carve(YX, 20480, [64, 512], F32)
            t2 = carve(YX, 22528, [64, 512], F32)
            krc = carve(YX, 24576, [64, 256], F32)
            wslA, wkA = nw()
            wA = wslA.rearrange("p (k n) -> p k n", k=KC)
            dma("pool", wA, wv[:, :, 0:512], [], [wkA])
            wslB, wkB = nw()
            wB = wslB[:, 0:KC * 320].rearrange("p (k n) -> p k n", k=KC)
            dma("pool", wB, wv[:, :, 512:832], [], [wkB])
            if grp == "S":
                dma("pool", ckvT[:, :, T:T + 256], cckvT_d[i].rearrange("(c p) t -> p c t", p=128), [], [("ckvT", NT)])
                dma("sp", krc, ckrT_d[i], [], ["krc"])
                act(krsq[:, T:T + 256], krc, AF.Square, ["krc"], [("krsq", NT)])
                ts(kgr[:, T:T + 256], krc, gkc[0:64, i, 1:2], None, ALU.mult, None, ["krc", "gsm"], [("kgr", NT)])
            for tb in range(NT):
                hk = [("ht", c, tb) for c in range(KC)]
                cols = slice(tb * 512, (tb + 1) * 512)
                pq = []
                for j in range(4):
                    p_, k_ = nps()
                    mm([(p_[:, :], wA[:, kc, j * 128:(j + 1) * 128], HT[:, kc, cols], kc == 0, kc == KC - 1)
                        for kc in range(KC)], [wkA] + hk, [k_])
                    act(sqb[j], p_[:, :], AF.Square, [k_], [("sqb", j)])
                    pq.append((p_, k_))
                pn, pnk = nps()
                mm([(pn[:, :], ones[:, :], sqb[j], j == 0, j == 3) for j in range(4)],
                   [("sqb", j) for j in range(4)] + ["ones"], [pnk])
                act(rsq, pn[:, :], rfn(), [pnk, "epsb"], ["rsq"], bias=epsb[:, 2:3])
                recip(rs, rsq, ["rsq"], ["rs"])
                for j in range(4):
                    stt(cqT[:, j, cols], pq[j][0][:, :], gcq[:, i, j:j + 1], rs, ALU.mult, ALU.mult,
                        [pq[j][1], "rs", "gsm"], [("cqT", tb)])
                pc = []
                for j in range(2):
                    p_, k_ = nps()
                    mm([(p_[:, :], wB[:, kc, j * 128:(j + 1) * 128], HT[:, kc, cols], kc == 0, kc == KC - 1)
                        for kc in range(KC)], [wkB] + hk, [k_])
                    act(sqb[j], p_[:, :], AF.Square, [k_], [("sqb", j)])
                    pc.append((p_, k_))
                pn, pnk = nps()
                mm([(pn[:, :], ones[:, :], sqb[j], j == 0, j == 1) for j in range(2)],
                   [("sqb", 0), ("sqb", 1), "ones"], [pnk])
                act(rsq, pn[:, :], rfn(), [pnk, "epsb"], ["rsq"], bias=eps256[:, 0:1])
                recip(rs, rsq, ["rsq"], ["rs"])
                for j in range(2):
                    if grp == "P":
                        sg = stg[j]
                        stt(sg, pc[j][0][:, :], gckv[:, i, j:j + 1], rs, ALU.mult, ALU.mult,
                            [pc[j][1], "rs", "gsm"], [("stg", j)])
                        dma("sp", ckv_o[i, j * 128:(j + 1) * 128, cols], sg, [("stg", j)], [])
                        copy("act", ckvT[:, j, cols], sg, [("stg", j)], [("ckvT", tb)])
                    else:
                        stt(ckvT[:, j, cols], pc[j][0][:, :], gckv[:, i, j:j + 1], rs, ALU.mult, ALU.mult,
                            [pc[j][1], "rs", "gsm"], [("ckvT", tb)])
                pr, prk = nps()
                mm([(pr[0:64, :], wB[:, kc, 256:320], HT[:, kc, cols], kc == 0, kc == KC - 1) for kc in range(KC)],
                   [wkB] + hk, [prk])
                act(krsq[:, cols], pr[0:64, :], AF.Square, [prk], [("krsq", tb)])
                if grp == "P":
                    sg = stg[2]
                    copy("dve", sg[0:64, :], pr[0:64, :], [prk], [("stg", 2)])
                    dma("sp", kr_o[i, :, cols], sg[0:64, :], [("stg", 2)], [])
                    act(kgr[:, cols], pr[0:64, :], AF.Identity, [prk, "gsm"], [("kgr", tb)], scale=gkc[0:64, i, 1:2])
                else:
                    act(kg, pr[0:64, :], AF.Identity, [prk, "gsm"], ["kg"], scale=gkc[0:64, i, 1:2])
                    dma("sp", cosb, cos_d[:, cols], [], ["cosb"])
                    dma("sp", sinb, sin_d[:, cols], [], ["sinb"])
                    pp, ppk = nps()
                    mm([(pp[0:64, :], ptm[:, :], kg, True, True)], ["kg", "ptm"], [ppk])
                    tt(t1, kg, cosb, ALU.mult, ["kg", "cosb"], ["t1"])
                    tt(t2, pp[0:64, :], sinb, ALU.mult, [ppk, "sinb"], ["t2"])
                    tt(kgr[:, cols], t1, t2, ALU.add, ["t1", "t2"], [("kgr", tb)])
            P.barrier()
            HTf = HT[:, :, :].rearrange("p c t -> p (c t)").bitcast(F32)
            kT2n = [carve(HTf, 0, [128, NK], BF16), carve(HTf, 39424, [128, NK], BF16)]
            kT2r = [carve(HTf, 4608, [64, NK], BF16), carve(HTf, 44032, [64, NK], BF16)]
            Vh2 = [carve(HTf, 9216, [128, NK // 128, 128], BF16), carve(HTf, 48640, [128, NK // 128, 128], BF16)]
            sqnK = carve(HTf, 53248, [128, 512], BF16)
            rsq2K = carve(HTf, 54272, [128, 512], F32)
            rs2K = carve(HTf, 56320, [128, 512], F32)
            qTn = [carve(HTf, 13824 + 1024 * k, [128, 512], BF16) for k in range(2)]
            qTr = [carve(HTf, 15872 + 1024 * k, [64, 512], BF16) for k in range(2)]
            sqn = carve(HTf, 17920, [128, 512], BF16)
            sqr = carve(HTf, 18944, [64, 512], BF16)
            rsq2 = carve(HTf, 19968, [128, 512], F32)
            rs2 = carve(HTf, 22016, [128, 512], F32)
            qg = carve(HTf, 24064, [64, 512], BF16)
            u1 = carve(HTf, 25088, [64, 512], F32)
            u2 = carve(HTf, 27136, [64, 512], F32)
            exs = [carve(HTf, 29184 + 1024 * k, [128, 512], BF16) for k in range(4)]
            rec = carve(HTf, 33280, [128, 512], F32)
            cosq = carve(HTf, 35328, [64, 512], F32)
            sinq = carve(HTf, 37376, [64, 512], F32)
            wslK, wkK = nw()
            wK = wslK.rearrange("p (k n) -> p k n", k=2)
            dma("pool", wK, w_ukv[i].rearrange("(kc p) n -> p kc n", p=128), [], [wkK])
            wslQ, wkQ = None, None
            ktiles = [(a, min(512, NK - a)) for a in range(0, NK, 512)]
            latk = [("ckvT", t) for t in range(NT + 1)] + [("krsq", t) for t in range(NT + 1)] + \
                   [("kgr", t) for t in range(NT + 1)]
            def kvkeys(slot):
                return [("kTn", slot, t) for t in range(len(ktiles))] + [("kTr", slot, t) for t in range(len(ktiles))] + \
                       [("Vh", slot, t) for t in range((NK // 128 + 3) // 4)]

            def kside_gen(h, slot):
                kTn_, kTr_, Vh_ = kT2n[slot], kT2r[slot], Vh2[slot]
                for (a, n) in ktiles:
                    pk_, pkk = nps(hold=True)
                    mm([(pk_[:, 0:n], wK[:, j, h * 256:h * 256 + 128], ckvT[:, j, a:a + n], j == 0, j == 1)
                        for j in range(2)], [wkK] + latk, [pkk])
                    act(sqnK[:, 0:n], pk_[:, 0:n], AF.Square, [pkk], ["sqnK"])
                    yield
                    pn, pnk = nps()
                    mm([(pn[:, 0:n], ones[:, :], sqnK[:, 0:n], True, False),
                        (pn[:, 0:n], ones[0:64, :], krsq[:, a:a + n], False, True)], ["sqnK", "ones"] + latk, [pnk])
                    act(rsq2K[:, 0:n], pn[:, 0:n], rfn(), [pnk, "epsb"], ["rsq2K"], bias=epsb[:, 3:4])
                    recip(rs2K[:, 0:n], rsq2K[:, 0:n], ["rsq2K"], ["rs2K"])
                    yield
                    stt(kTn_[:, a:a + n], pk_[:, 0:n], gkc[:, i, 0:1], rs2K[:, 0:n], ALU.mult, ALU.mult,
                        [pkk, "rs2K", "gsm"], [("kTn", slot, a // 512)])
                    release(pkk)
                    tt(kTr_[:, a:a + n], kgr[:, a:a + n], rs2K[0:64, 0:n], ALU.mult, ["rs2K"] + latk, [("kTr", slot, a // 512)])
                    yield
                for c4 in range(0, NK // 128, 4):
                    nn = min(4, NK // 128 - c4)
                    pv_, pvk = nps()
                    mm([(pv_[:, q * 128:(q + 1) * 128], ckvT[:, j, (c4 + q) * 128:(c4 + q + 1) * 128],
                         wK[:, j, h * 256 + 128:h * 256 + 256], j == 0, j == 1) for q in range(nn) for j in range(2)],
                       [wkK] + latk, [pvk])
                    copy("act", Vh_[:, c4:c4 + nn, :], pv_[:, 0:nn * 128].rearrange("p (a b) -> p a b", a=nn), [pvk],
                         [("Vh", slot, c4 // 4)])
                    yield

            for h in range(16):
                if h % 8 == 0:
                    wslQ, wkQ = nw()
                    wcnt[0] += 1
                    wQ = wslQ[:, 0:4 * 1536].rearrange("p (k n) -> p k n", k=4)
                    dma("pool", wQ, w_uq[i].rearrange("(kc p) n -> p kc n", p=128)[:, :, (h // 8) * 1536:(h // 8 + 1) * 1536],
                        [], [wkQ])
                hh = h % 8
                if grp == "P":
                    slot = 0
                    for _ in kside_gen(h, 0):
                        pass
                    gen_next = None
                else:
                    slot = h % 2
                    if h == 0:
                        for _ in kside_gen(0, 0):
                            pass
                    gen_next = kside_gen(h + 1, (h + 1) % 2) if (h + 1 < 16 and OPT_KHIDE) else None
                    if h + 1 < 16 and not OPT_KHIDE:
                        pass
                kTn, kTr, Vh = kT2n[slot], kT2r[slot], Vh2[slot]
                kvk = kvkeys(slot)
                pendm = [None]
                for tb in range(NT):
                    cols = slice(tb * 512, (tb + 1) * 512)
                    qs = tb % 2
                    pq_, pqk = nps()
                    mm([(pq_[:, :], wQ[:, j, hh * 192:hh * 192 + 128], cqT[:, j, cols], j == 0, j == 3) for j in range(4)],
                       [wkQ, ("cqT", tb)], [pqk])
                    pr_, prk = nps()
                    mm([(pr_[0:64, :], wQ[:, j, hh * 192 + 128:hh * 192 + 192], cqT[:, j, cols], j == 0, j == 3)
                        for j in range(4)], [wkQ, ("cqT", tb)], [prk])
                    act(sqn, pq_[:, :], AF.Square, [pqk], ["sqn"])
                    act(sqr, pr_[0:64, :], AF.Square, [prk], ["sqr"])
                    pn, pnk = nps()
                    mm([(pn[:, :], ones[:, :], sqn, True, False), (pn[:, :], ones[0:64, :], sqr, False, True)],
                       ["sqn", "sqr", "ones"], [pnk])
                    act(rsq2, pn[:, :], rfn(), [pnk, "epsb"], ["rsq2"], bias=epsb[:, 3:4])
                    recip(rs2, rsq2, ["rsq2"], ["rs2"])
                    stt(qTn[qs], pq_[:, :], gqc[:, i, 0:1], rs2, ALU.mult, ALU.mult, [pqk, "rs2", "gsm"], [("qTn", qs)])
                    if grp == "P":
                        stt(qTr[qs], pr_[0:64, :], gqc[0:64, i, 1:2], rs2[0:64, :], ALU.mult, ALU.mult,
                            [prk, "rs2", "gsm"], [("qTr", qs)])
                    else:
                        act(qg, pr_[0:64, :], AF.Identity, [prk, "gsm"], ["qg"], scale=gqc[0:64, i, 1:2])
                        if h == 0 or True:
                            dma("sp", cosq, cos_d[:, cols], [], ["cosq"])
                            dma("sp", sinq, sin_d[:, cols], [], ["sinq"])
                        pp, ppk = nps()
                        mm([(pp[0:64, :], ptm[:, :], qg, True, True)], ["qg", "ptm"], [ppk])
                        tt(u1, qg, cosq, ALU.mult, ["qg", "cosq"], ["u1"])
                        tt(u2, pp[0:64, :], sinq, ALU.mult, [ppk, "sinq"], ["u2"])
                        tt(u1, u1, u2, ALU.add, ["u1", "u2"], ["u1"])
                        tt(qTr[qs], u1, rs2[0:64, :], ALU.mult, ["u1", "rs2"], [("qTr", qs)])
                    qk_ = [("qTn", qs), ("qTr", qs)]
                    if grp == "P":
                        for s2 in range(2):
                            s_ = tb * 2 + s2
                            qc = slice(s2 * 256, (s2 + 1) * 256)
                            fin = attn_block([lambda ch: kTn[:, ch * 128:(ch + 1) * 128],
                                              lambda ch: kTr[:, ch * 128:(ch + 1) * 128]],
                                             [qTn[qs][:, qc], qTr[qs][:, qc]], lambda ch: Vh[:, ch, :],
                                             [2 * s_, 2 * s_ + 1], 256, None, yT[:, h, s_ * 256:(s_ + 1) * 256],
                                             ("yt", tb), kvk + qk_, exs[s_ % 2], ("ex", s_ % 2), rec, "rec")
                            if not OPT_PIPE:
                                fin()
                                fin = None
                            if pendm[0]:
                                pendm[0]()
                            pendm[0] = fin
                    else:
                        nch = NK // 128
                        pO, pOk = nps(hold=True)
                        pD, pDk = nps(hold=True)
                        def scores(c):
                            pS, pSk = nps()
                            mm([(pS[:, :], kTn[:, c * 128:(c + 1) * 128], qTn[qs], True, False),
                                (pS[:, :], kTr[:, c * 128:(c + 1) * 128], qTr[qs], False, True)], kvk + qk_, [pSk])
                            act(exs[c % 4], pS[:, :], AF.Exp, [pSk], [("ex", c % 4)])
                        for c0 in range(OPT_LOOK):
                            scores(c0)
                        for c in range(nch):
                            if c + OPT_LOOK < nch:
                                scores(c + OPT_LOOK)
                            mm([(pO[:, :], Vh[:, c, :], exs[c % 4], c == 0, c == nch - 1),
                                (pD[:, :], ones[:, :], exs[c % 4], c == 0, c == nch - 1)],
                               kvk + [("ex", c % 4), "ones"], [pOk, pDk])
                            if gen_next is not None and c % 3 == 1:
                                next(gen_next, None)
                        release(pOk)
                        release(pDk)
                        recip_den(rec, pD[:, :], [pDk], "rec")
                        tt(yT[:, h, cols], pO[:, :], rec, ALU.mult, [pOk, "rec"], [("yt", tb)])
                if pendm[0]:
                    pendm[0]()
                if grp == "S" and h + 1 < 16:
                    if gen_next is None:
                        gen_next = kside_gen(h + 1, (h + 1) % 2)
                    for _ in gen_next:
                        pass
            LNF[0] = OPT_LN
            P.barrier()
            out_proj(w_o[i], t0, T, cd, 32)

        def ffn(l, t0, cd, final):
            T = 1024
            for c4 in range(0, KC, 4):
                dma("sp", xres[:, c4:c4 + 4, :], xdv[:, c4:c4 + 4, t0:t0 + T], [("xd", c) for c in range(c4, c4 + 4)],
                    [("xr", c, t) for c in range(c4, c4 + 4) for t in range(2)])
            sq = carve(AW, 0, [128, KC, 256], BF16)
            tmp = carve(AW, 8192, [128, KC, 256], F32)
            rs = carve(AW, 24576, [128, 256], F32)
            rsq = carve(AW, 25600, [128, 256], F32)
            aT = [carve(AW, 26624 + 4096 * k, [128, 2, 1024], BF16) for k in range(2)]
            sg = [carve(AW, 34816 + 2048 * k, [128, 512], F32) for k in range(2)]
            for b in range(T // 256):
                xk = [("xr", c, b // 2) for c in range(KC)]
                xb = xres[:, :, b * 256:(b + 1) * 256]
                act(sq, xb, AF.Square, xk, ["fsq"])
                pn, pnk = nps()
                mm([(pn[:, 0:256], ones[:, :], sq[:, c, :], c == 0, c == KC - 1) for c in range(KC)], ["fsq", "ones"], [pnk])
                act(rsq, pn[:, 0:256], rfn(), [pnk, "epsb"], ["frsq"], bias=epsb[:, 0:1])
                recip(rs, rsq, ["frsq"], ["frs"])
                tt(tmp, xb, rs.unsqueeze(1).to_broadcast([128, KC, 256]), ALU.mult, xk + ["frs"], ["ftmp"])
                for c in range(KC):
                    act(HT[:, c, b * 256:(b + 1) * 256], tmp[:, c, :], AF.Identity, ["ftmp", "gs2", "modv"],
                        [("ht", c, b // 2)], bias=modv[:, 48 + c, cd:cd + 1], scale=gs2[:, c, cd:cd + 1])
            wi = w_ffn_in[l].rearrange("(kc p) n -> p kc n", p=128)
            wo = w_ffn_out[l].rearrange("(r p) n -> p r n", p=128)
            NG = DFF // 256
            inw = {}
            outw = {}

            def load_in(g):
                wsl, wk = nw()
                wt = wsl.rearrange("p (s k n) -> p s k n", s=2, k=KC)
                dma("pool", wt[:, 0, :, :], wi[:, :, g * 256:(g + 1) * 256], [], [wk])
                dma("pool", wt[:, 1, :, :], wi[:, :, DFF + g * 256:DFF + (g + 1) * 256], [], [wk])
                inw[g] = (wt, wk)

            def load_out(g):
                s_ = g % 2
                k_ = ("wo", s_)
                for j in range(2):
                    for half in range(2):
                        dma("pool", HT[:, 4 * s_ + 2 * j + half, 1024:2048],
                            wo[:, 2 * g + j, half * 1024:(half + 1) * 1024], [], [k_])
                outw[g] = (s_, k_)

            def stage_in(g):
                wt, wk = inw[g]
                s_ = g % 2
                for tb in range(2):
                    hk = [("ht", c, tb) for c in range(KC)]
                    cols = slice(tb * 512, (tb + 1) * 512)
                    for j in range(2):
                        pg, pgk = nps()
                        mm([(pg[:, :], wt[:, 0, kc, j * 128:(j + 1) * 128], HT[:, kc, cols], kc == 0, kc == KC - 1)
                            for kc in range(KC)], [wk] + hk, [pgk])
                        pu, puk = nps()
                        mm([(pu[:, :], wt[:, 1, kc, j * 128:(j + 1) * 128], HT[:, kc, cols], kc == 0, kc == KC - 1)
                            for kc in range(KC)], [wk] + hk, [puk])
                        act(sg[j], pg[:, :], AF.Silu, [pgk], [("sg", j)])
                        tt(aT[s_][:, j, cols], sg[j], pu[:, :], ALU.mult, [("sg", j), puk], [("aT", s_, tb)])

            def stage_out(g):
                s_, k_ = outw[g]
                for tb in range(2):
                    cols = slice(tb * 512, (tb + 1) * 512)
                    for fc in range(KC):
                        py, pyk = nps()
                        mm([(py[:, :], HT[:, 4 * s_ + 2 * j + fc // 8, 1024 + (fc % 8) * 128:1024 + (fc % 8 + 1) * 128],
                             aT[s_][:, j, cols], j == 0, j == 1) for j in range(2)], [k_, ("aT", s_, tb)], [pyk])
                        stt(xres[:, fc, cols], py[:, :], modv[:, 80 + fc, cd:cd + 1], xres[:, fc, cols],
                            ALU.mult, ALU.add, [pyk, ("xr", fc, tb), "modv"], [("xr", fc, tb)])

            load_in(0)
            load_out(0)
            load_in(1)
            stage_in(0)
            for g in range(NG):
                if g + 1 < NG:
                    load_out(g + 1)
                    if g + 2 < NG:
                        load_in(g + 2)
                    stage_in(g + 1)
                stage_out(g)
            dst = fm(yout) if final else xdv
            for c4 in range(0, KC, 4):
                dma("sp", dst[:, c4:c4 + 4, t0:t0 + T], xres[:, c4:c4 + 4, :],
                    [("xr", c, t) for c in range(c4, c4 + 4) for t in range(2)], [("xd", c) for c in range(c4, c4 + 4)])
            P.barrier()

        steps = plan
        if steps is None:
            steps = []
            for l in range(depth):
                steps.append(("ada", l))
                for grp in ("P", "S"):
                    steps.append(("mix", l, grp))
                    if stop_after == ("mix", l, grp):
                        break
                    steps.append(("ffn", l, grp))
                else:
                    continue
                break
        debug = (plan is not None) or (stop_after is not None)
        for s_ in steps:
            l = s_[1]
            last = (l == depth - 1) and not debug
            if s_[0] == "ada":
                adaln(l)
            elif s_[0] == "mix":
                if l % 2 == 0:
                    mixer_even(l, s_[2])
                else:
                    mixer_odd(l, s_[2])
            elif s_[2] == "P":
                ffn(l, 0, 0, last)
            else:
                ffn(l, TP, 1, last)
                ffn(l, TP + 1024, 1, last)
        if debug:
            P.barrier()
            for c in range(KC):
                dma("sp", yout[c * 128:(c + 1) * 128, :], xd[c * 128:(c + 1) * 128, :], [], [])
        P.barrier()

        with nc.Block() as block:
            P.emit(nc, block, esem, dsem)
    return nc, P


def _fmvec(v):
    v = np.asarray(v, np.float32)
    lead = v.shape[:-1]
    n = v.shape[-1] // 128
    v = v.reshape(lead + (n, 128))
    return np.ascontiguousarray(np.moveaxis(v, -1, 0))


def _rope_tables():
    half = 32
    freqs = (10000.0 ** (-np.arange(half // 2, dtype=np.float32) / (half // 2))).astype(np.float32)
    t = np.arange(TS)
    cos = np.zeros((64, TS), np.float32)
    sin = np.zeros((64, TS), np.float32)
    for g, pos in enumerate((t // GRID_W, t % GRID_W)):
        ang = pos.astype(np.float32)[None, :] * freqs[:, None]
        c, s = np.cos(ang).astype(np.float32), np.sin(ang).astype(np.float32)
        cos[g * 32:g * 32 + 16] = c
        cos[g * 32 + 16:g * 32 + 32] = c
        sin[g * 32:g * 32 + 16] = s
        sin[g * 32 + 16:g * 32 + 32] = s
    Pm = np.zeros((64, 64), np.float32)
    for g in range(2):
        for d in range(16):
            Pm[g * 32 + d, g * 32 + d + 16] = -1.0
            Pm[g * 32 + d + 16, g * 32 + d] = 1.0
    return cos, sin, np.ascontiguousarray(Pm.T)


_CACHE = {}


def prepare_inputs(inp, depth=DEPTH):
    f = lambda k: np.asarray(inp[k], np.float32)
    shared = {}
    shared["n1g"] = _fmvec(f("norm1_g"))
    shared["n2g"] = _fmvec(f("norm2_g"))
    shared["w_ada"] = f("w_ada")
    shared["b_ada"] = _fmvec(f("b_ada"))
    shared["w_in_ab"] = f("w_in_ab")
    shared["gqa"] = np.ascontiguousarray(f("g_qn_a").T)
    shared["gka"] = np.ascontiguousarray(f("g_kn_a").T)
    cw = f("conv_b_w")
    shared["convw"] = np.ascontiguousarray(cw.reshape(2, 3, 8, 128).transpose(3, 0, 2, 1))
    ro, co, ok = nat_bias_index()
    rpb = f("rpb_a")
    bt = rpb[:, :, ro, co]
    bt = np.where(ok[None, None], bt, np.float32(-30000.0)).astype(np.float32)
    shared["biasT"] = np.ascontiguousarray(bt)
    shared["w_out_ab"] = f("w_out_ab")
    shared["w_down"] = f("w_down_c")
    shared["gcq"] = _fmvec(f("g_cq"))
    shared["gckv"] = _fmvec(f("g_ckv"))
    for nm, key in (("gqc", "g_qn_c"), ("gkc", "g_kn_c")):
        g = f(key)
        a = np.zeros((128, 2, 2), np.float32)
        a[:, :, 0] = g[:, 0:128].T
        a[0:64, :, 1] = g[:, 128:192].T
        shared[nm] = a
    shared["w_uq"] = f("w_uq_c")
    shared["w_ukv"] = f("w_ukv_c")
    shared["w_o"] = f("w_o_c")
    shared["w_ffn_in"] = f("w_ffn_in")
    shared["w_ffn_out"] = f("w_ffn_out")
    cos, sin, pt = _rope_tables()
    shared["ropecos"], shared["ropesin"], shared["ropept"] = cos, sin, pt
    shared["ident"] = np.eye(128, dtype=np.float32)
    xp, xs = f("x_prompt"), f("x_sample")
    cnk, cnv = f("cache_nat_k"), f("cache_nat_v")
    cckv, ckr = f("cache_mla_ckv"), f("cache_mla_krope")
    c, cctx = f("c"), f("c_ctx")
    maps = []
    for core in range(8):
        b = core // 2
        m = dict(shared)
        xT = np.empty((D, TALL), np.float32)
        xT[:, 0:TP] = xp[4 * core:4 * core + 4].reshape(TP, D).T
        xT[:, TP:] = xs[b].T
        m["xT"] = xT
        cd = np.stack([cctx, c[b]], axis=-1)
        m["cond"] = np.ascontiguousarray(cd.reshape(KC, 128, 2).transpose(1, 0, 2))
        m["cnkT"] = np.ascontiguousarray(cnk[b].transpose(0, 1, 3, 2))
        m["cnv"] = np.ascontiguousarray(cnv[b])
        m["cckvT"] = np.ascontiguousarray(cckv[b].transpose(0, 2, 1))
        m["ckrT"] = np.ascontiguousarray(ckr[b].transpose(0, 2, 1))
        maps.append(m)
    return maps


def assemble(results):
    B, SEQ = 32, 256
    y_p = np.empty((B, SEQ, D), np.float32)
    y_s = np.empty((4, TS, D), np.float32)
    nat_k = np.empty((B, 2, 8, SEQ, 128), np.float32)
    nat_v = np.empty((B, 2, 8, SEQ, 128), np.float32)
    ckv = np.empty((B, 2, SEQ, 256), np.float32)
    kr = np.empty((B, 2, SEQ, 64), np.float32)
    for core in range(8):
        r = results[core]
        yT = r["yT"]
        y_p[4 * core:4 * core + 4] = yT[:, 0:TP].T.reshape(4, SEQ, D)
        if core % 2 == 0:
            y_s[core // 2] = yT[:, TP:].T
        nk = r["natk"].reshape(2, 8, 128, 4, SEQ)
        nat_k[4 * core:4 * core + 4] = nk.transpose(3, 0, 1, 4, 2)
        nv = r["natv"].reshape(2, 4, SEQ, 8, 128)
        nat_v[4 * core:4 * core + 4] = nv.transpose(1, 0, 3, 2, 4)
        co = r["ckvo"].reshape(2, 256, 4, SEQ)
        ckv[4 * core:4 * core + 4] = co.transpose(2, 0, 3, 1)
        ko = r["kro"].reshape(2, 64, 4, SEQ)
        kr[4 * core:4 * core + 4] = ko.transpose(2, 0, 3, 1)
    return y_p, y_s, nat_k, nat_v, ckv, kr


def kernel(**inputs):
    if "nc" not in _CACHE:
        _CACHE["nc"] = build()[0]
    nc = _CACHE["nc"]
    maps = prepare_inputs(inputs)
    res = run_bass_kernel_spmd(nc, maps, core_ids=list(range(8)))
    return assemble(res.results)
```
